# Optimizing a Trainium2 kernel written in Bass

```python
import math
import jax, jax.numpy as jnp
from jax import lax
import numpy as np

D_MODEL = 1024
BATCH = 8
SEQ = 4096
DEPTH = 4

N_META = 16
N_MIXERS = 2
N_ATTN_LAYERS = (DEPTH + 1) // 2
N_GLA_LAYERS = DEPTH // 2
DA_HEADS = 8
DA_HEAD_DIM = D_MODEL // DA_HEADS // 2
DA_V_DIM = 2 * DA_HEAD_DIM
Q_BLOCK = 128
GLA_HEADS = 4
GLA_KEY_DIM = D_MODEL // 2
GLA_VAL_DIM = D_MODEL
GLA_HK = GLA_KEY_DIM // GLA_HEADS
GLA_HV = GLA_VAL_DIM // GLA_HEADS
GLA_GATE_RANK = 16
GLA_GATE_NORM = 16.0
GLA_CHUNK = 64
GLA_IN_DIM = 2 * GLA_KEY_DIM + 2 * GLA_VAL_DIM + GLA_GATE_RANK
D_FF = 4 * D_MODEL
EPS = 1e-6

kernel_name = "hybrid_diffattn_gla_sqrelu"


def rmsnorm(x, w):
    xf = x.astype(jnp.float32)
    y = xf * lax.rsqrt(jnp.mean(xf * xf, axis=-1, keepdims=True) + EPS)
    return (y * w.astype(jnp.float32)).astype(x.dtype)


def lambda_init_for(layer_idx):
    return 0.8 - 0.6 * math.exp(-0.3 * layer_idx)


def alibi_slopes(n_heads):
    return 2.0 ** (-8.0 * jnp.arange(1, n_heads + 1, dtype=jnp.float32) / n_heads)


def diff_attention(x, w_in, lam_params, subln_w, w_out, lambda_init):
    B, L, _ = x.shape
    q, k, v = jnp.split(x @ w_in, [D_MODEL, 2 * D_MODEL], axis=-1)
    q = q.reshape(B, L, DA_HEADS, 2, DA_HEAD_DIM)
    k = k.reshape(B, L, DA_HEADS, 2, DA_HEAD_DIM)
    v = v.reshape(B, L, DA_HEADS, DA_V_DIM)
    lp = lam_params.astype(jnp.float32)
    lam = jnp.exp(jnp.sum(lp[0] * lp[1])) - jnp.exp(jnp.sum(lp[2] * lp[3])) + lambda_init
    slopes = alibi_slopes(DA_HEADS)[None, :, None, None, None]
    scale = DA_HEAD_DIM ** -0.5
    bounds = [(0, N_META)] + [(s, min(s + Q_BLOCK, L)) for s in range(N_META, L, Q_BLOCK)]
    outs = []
    for s, e in bounds:
        qb = q[:, s:e]
        kb = k[:, :e]
        vb = v[:, :e]
        scores = jnp.einsum('bqhcd,bkhcd->bhcqk', qb, kb).astype(jnp.float32) * scale
        dist = (jnp.arange(s, e)[:, None] - jnp.arange(e)[None, :]).astype(jnp.float32)
        scores = jnp.where(dist >= 0.0, scores - slopes * dist, -jnp.inf)
        p = jax.nn.softmax(scores, axis=-1)
        attn = p[:, :, 0] - lam * p[:, :, 1]
        outs.append(jnp.einsum('bhqk,bkhe->bqhe', attn.astype(vb.dtype), vb))
    o = jnp.concatenate(outs, axis=1)
    o = rmsnorm(o, subln_w) * (1.0 - lambda_init)
    return o.reshape(B, L, DA_HEADS * DA_V_DIM) @ w_out


def gla_chunk(S, q, k, v, glog):
    C = q.shape[2]
    b = jnp.cumsum(glog, axis=2)
    causal = jnp.tril(jnp.ones((C, C), dtype=bool))
    diff = b[:, :, :, None, :] - b[:, :, None, :, :]
    decay = jnp.exp(jnp.where(causal[:, :, None], diff, -jnp.inf))
    a = jnp.einsum('bhid,bhjd,bhijd->bhij', q, k, decay)
    o = jnp.einsum('bhij,bhje->bhie', a, v) + jnp.einsum('bhid,bhde->bhie', q * jnp.exp(b), S)
    b_last = b[:, :, -1:, :]
    S = jnp.exp(b_last[:, :, 0, :, None]) * S + jnp.einsum('bhjd,bhje->bhde', k * jnp.exp(b_last - b), v)
    return S, o


def gla(x, w_in, w_gate_up, gate_bias, norm_w, w_out):
    B, L, _ = x.shape
    splits = [GLA_KEY_DIM, 2 * GLA_KEY_DIM, 2 * GLA_KEY_DIM + GLA_VAL_DIM, 2 * GLA_KEY_DIM + 2 * GLA_VAL_DIM]
    q, k, v, g, gz = jnp.split(x @ w_in, splits, axis=-1)
    glog = jax.nn.log_sigmoid((gz @ w_gate_up + gate_bias).astype(jnp.float32)) / GLA_GATE_NORM

    def heads(t, d):
        return t.reshape(B, L, GLA_HEADS, d).transpose(0, 2, 1, 3).astype(jnp.float32)

    q = heads(q, GLA_HK) * (GLA_HK ** -0.5)
    k = heads(k, GLA_HK)
    v = heads(v, GLA_HV)
    glog = heads(glog, GLA_HK)
    S0 = jnp.zeros((B, GLA_HEADS, GLA_HK, GLA_HV), jnp.float32)
    S, o_meta = gla_chunk(S0, q[:, :, :N_META], k[:, :, :N_META], v[:, :, :N_META], glog[:, :, :N_META])
    n_real = L - N_META
    n_chunks = n_real // GLA_CHUNK

    def to_chunks(t):
        t = t[:, :, N_META:]
        return t.reshape(B, GLA_HEADS, n_chunks, GLA_CHUNK, t.shape[-1]).transpose(2, 0, 1, 3, 4)

    def step(state, inp):
        return gla_chunk(state, *inp)

    _, o_real = lax.scan(step, S, (to_chunks(q), to_chunks(k), to_chunks(v), to_chunks(glog)))
    o_real = o_real.transpose(1, 2, 0, 3, 4).reshape(B, GLA_HEADS, n_real, GLA_HV)
    o = jnp.concatenate([o_meta, o_real], axis=2).transpose(0, 2, 1, 3)
    o = rmsnorm(o, norm_w) * jax.nn.silu(g.reshape(B, L, GLA_HEADS, GLA_HV).astype(jnp.float32))
    return o.reshape(B, L, GLA_VAL_DIM).astype(x.dtype) @ w_out


def sq_relu_mlp(x, w_up, w_down):
    return jnp.square(jax.nn.relu(x @ w_up)) @ w_down


def setup_inputs(seed: int = 0) -> dict:
    key = jax.random.key(seed)
    ks = jax.random.split(key, 17)

    def nrm(k, shape, scale):
        return jax.random.normal(k, shape, jnp.float32) * scale

    return {
        "x": nrm(ks[0], (BATCH, SEQ, D_MODEL), 1.0),
        "meta_tokens": nrm(ks[1], (N_META, D_MODEL), 1.0),
        "mix_norm_w": 1.0 + nrm(ks[2], (DEPTH, D_MODEL), 0.02),
        "attn_w_in": nrm(ks[3], (N_ATTN_LAYERS, D_MODEL, 3 * D_MODEL), D_MODEL ** -0.5),
        "attn_lambda": nrm(ks[4], (N_ATTN_LAYERS, 4, DA_HEAD_DIM), 0.1),
        "attn_subln_w": 1.0 + nrm(ks[5], (N_ATTN_LAYERS, DA_V_DIM), 0.02),
        "attn_w_out": nrm(ks[6], (N_ATTN_LAYERS, DA_HEADS * DA_V_DIM, D_MODEL), (DA_HEADS * DA_V_DIM) ** -0.5),
        "gla_w_in": nrm(ks[7], (N_GLA_LAYERS, D_MODEL, GLA_IN_DIM), D_MODEL ** -0.5),
        "gla_w_gate_up": nrm(ks[8], (N_GLA_LAYERS, GLA_GATE_RANK, GLA_KEY_DIM), GLA_GATE_RANK ** -0.5),
        "gla_gate_bias": nrm(ks[9], (N_GLA_LAYERS, GLA_KEY_DIM), 0.02),
        "gla_norm_w": 1.0 + nrm(ks[10], (N_GLA_LAYERS, GLA_HV), 0.02),
        "gla_w_out": nrm(ks[11], (N_GLA_LAYERS, GLA_VAL_DIM, D_MODEL), GLA_VAL_DIM ** -0.5),
        "mlp_norm_w": 1.0 + nrm(ks[12], (DEPTH, D_MODEL), 0.02),
        "mlp_w_up": nrm(ks[13], (DEPTH, D_MODEL, D_FF), D_MODEL ** -0.5),
        "mlp_w_down": nrm(ks[14], (DEPTH, D_FF, D_MODEL), D_FF ** -0.5),
        "final_norm_w": 1.0 + nrm(ks[15], (D_MODEL,), 0.02),
    }


def reference(x, meta_tokens, mix_norm_w, attn_w_in, attn_lambda, attn_subln_w, attn_w_out,
              gla_w_in, gla_w_gate_up, gla_gate_bias, gla_norm_w, gla_w_out,
              mlp_norm_w, mlp_w_up, mlp_w_down, final_norm_w):
    B = x.shape[0]
    meta = jnp.broadcast_to(meta_tokens[None].astype(x.dtype), (B, N_META, D_MODEL))
    h = jnp.concatenate([meta, x], axis=1)
    for i in range(DEPTH):
        hn = rmsnorm(h, mix_norm_w[i])
        j = i // N_MIXERS
        if i % N_MIXERS == 0:
            h = h + diff_attention(hn, attn_w_in[j], attn_lambda[j], attn_subln_w[j], attn_w_out[j],
                                   lambda_init_for(i))
        else:
            h = h + gla(hn, gla_w_in[j], gla_w_gate_up[j], gla_gate_bias[j], gla_norm_w[j], gla_w_out[j])
        h = h + sq_relu_mlp(rmsnorm(h, mlp_norm_w[i]), mlp_w_up[i], mlp_w_down[i])
    return rmsnorm(h[:, N_META:], final_norm_w)
```

```python
import math
import numpy as np
import concourse.bass as bass
import concourse.mybir as mybir
from concourse.bass_utils import run_bass_kernel_spmd

F32 = mybir.dt.float32
BF16 = mybir.dt.bfloat16
I32 = mybir.dt.int32
ALU = mybir.AluOpType
AF = mybir.ActivationFunctionType
AX = mybir.AxisListType

ENGS = ("pe", "act", "dve", "pool", "sp")
EPOCH = 16000
DMA_K = 8


class _Op:
    __slots__ = ("eng", "fn", "deps", "signal", "sig_seq", "dma", "dma_slot", "dma_val", "pre")

    def __init__(self, eng, fn):
        self.eng = eng
        self.fn = fn
        self.deps = []
        self.signal = False
        self.sig_seq = None
        self.dma = False
        self.dma_slot = None
        self.dma_val = None
        self.pre = None


class Prog:
    def __init__(self, nc):
        self.nc = nc
        self.ops = {e: [] for e in ENGS}
        self.last_w = {}
        self.readers = {}
        self.dma_hist = {e: [] for e in ENGS}
        self.bar_deps = []
        self.bar_pending = set()

    def _add(self, eng, fn, reads, writes, dma=False):
        op = _Op(eng, fn)
        op.dma = dma
        deps = []
        if eng in self.bar_pending:
            deps.extend(self.bar_deps)
            self.bar_pending.discard(eng)
        for r in reads:
            w = self.last_w.get(r)
            if w is not None:
                deps.append(w)
        for r in writes:
            w = self.last_w.get(r)
            if w is not None:
                deps.append(w)
            deps.extend(self.readers.get(r, ()))
        for r in reads:
            self.readers.setdefault(r, []).append(op)
        for r in writes:
            self.last_w[r] = op
            self.readers[r] = []
        op.deps = deps
        self.ops[eng].append(op)
        if dma:
            hist = self.dma_hist[eng]
            n = len(hist)
            op.dma_slot = n % DMA_K
            op.dma_val = 16 * (n // DMA_K + 1)
            if n >= DMA_K:
                op.pre = hist[n - DMA_K]
            hist.append(op)
        return op

    def op(self, eng, fn, reads=(), writes=()):
        return self._add(eng, fn, reads, writes)

    def dma(self, eng, out, in_, reads=(), writes=(), **kw):
        return self._add(eng, lambda e: e.dma_start(out=out, in_=in_, **kw), reads, writes, dma=True)

    def barrier(self):
        lasts = []
        for e in ENGS:
            for op in reversed(self.ops[e]):
                if not op.dma:
                    lasts.append(op)
                    break
            lasts.extend(self.dma_hist[e][-DMA_K:])
        self.bar_deps = lasts
        self.bar_pending = set(ENGS)
        self.last_w = {}
        self.readers = {}

    def emit(self):
        nc = self.nc
        for e in ENGS:
            for op in self.ops[e]:
                for d in op.deps:
                    if d.dma:
                        continue
                    if d.eng == "pe" and op.eng == "pe" and not op.dma:
                        continue
                    d.signal = True
        nsig = {}
        for e in ENGS:
            s = 0
            for op in self.ops[e]:
                if op.signal and not op.dma:
                    s += 1
                    op.sig_seq = s
            nsig[e] = s
        import contextlib
        stack = contextlib.ExitStack()
        sems = {}
        dsems = {}
        for e in ENGS:
            n_ep = max(1, (nsig[e] + EPOCH - 1) // EPOCH)
            sems[e] = [stack.enter_context(nc.semaphore(f"s_{e}_{k}")) for k in range(n_ep)]
            if self.dma_hist[e]:
                dsems[e] = [stack.enter_context(nc.semaphore(f"d_{e}_{k}")) for k in range(DMA_K)]

        def target(d):
            if d.dma:
                return ("d", d.eng, d.dma_slot), dsems[d.eng][d.dma_slot], d.dma_val
            ep = (d.sig_seq - 1) // EPOCH
            return ("s", d.eng, ep), sems[d.eng][ep], d.sig_seq - ep * EPOCH

        with stack:
            block = stack.enter_context(nc.Block())

            def run(e, h):
                seen = {}
                for op in self.ops[e]:
                    need = {}
                    dl = op.deps if op.pre is None else op.deps + [op.pre]
                    for d in dl:
                        if (not d.dma) and d.eng == "pe" and e == "pe" and not op.dma:
                            continue
                        key, sem, val = target(d)
                        if seen.get(key, 0) >= val:
                            continue
                        if key not in need or need[key][1] < val:
                            need[key] = (sem, val)
                    for key, (sem, val) in need.items():
                        h.wait_ge(sem, val)
                        seen[key] = val
                    ins = op.fn(h)
                    if op.dma:
                        ins.then_inc(dsems[e][op.dma_slot], 16)
                    elif op.signal:
                        ep = (op.sig_seq - 1) // EPOCH
                        ins.then_inc(sems[e][ep], 1)
                hist = self.dma_hist[e]
                for d in hist[-DMA_K:]:
                    h.wait_ge(dsems[e][d.dma_slot], d.dma_val)

            @block.tensor
            def _(h):
                run("pe", h)

            @block.scalar
            def _(h):
                run("act", h)

            @block.vector
            def _(h):
                run("dve", h)

            @block.gpsimd
            def _(h):
                run("pool", h)

            @block.sync
            def _(h):
                run("sp", h)


D = 1024
NMETA = 16
DFF = 4096
EPS = 1e-6
ARENA0 = 20480
ARENA_END = 229376 - 1024
MASKNEG = -240000.0
ATT_SKIP = 60.0


def lambda_init_for(i):
    return 0.8 - 0.6 * math.exp(-0.3 * i)


def build(SEQ, depth=4, dbg=False):
    nc = bass.Bass("TRN2", target_bir_lowering=False)
    NREAL = SEQ + NMETA
    NT = (NREAL + 127) // 128
    LP = NT * 128
    n_attn = (depth + 1) // 2
    n_gla = depth // 2
    blocks = [(t0, min(4, NT - t0)) for t0 in range(0, NT, 4)]
    skind = "ExternalOutput" if dbg else "Internal"

    def din(name, shape):
        return nc.dram_tensor(name, list(shape), F32, kind="ExternalInput")

    x_d = din("x", [SEQ, D])
    meta_d = din("meta_tokens", [NMETA, D])
    mixw_d = din("mix_norm_w", [depth, D])
    awin_d = din("attn_w_in", [n_attn, D, 3 * D])
    alam_d = din("attn_lambda", [n_attn, 4, 64])
    asub_d = din("attn_subln_w", [n_attn, 128])
    awout_d = din("attn_w_out", [n_attn, D, D])
    gwin_d = din("gla_w_in", [max(n_gla, 1), D, 3088])
    ggu_d = din("gla_w_gate_up", [max(n_gla, 1), 16, 512])
    ggb_d = din("gla_gate_bias", [max(n_gla, 1), 512])
    gnw_d = din("gla_norm_w", [max(n_gla, 1), 256])
    gwout_d = din("gla_w_out", [max(n_gla, 1), D, D])
    mlpw_d = din("mlp_norm_w", [depth, D])
    wup_d = din("mlp_w_up", [depth, D, DFF])
    wdn_d = din("mlp_w_down", [depth, DFF, D])
    fnw_d = din("final_norm_w", [D])
    out_d = nc.dram_tensor("out", [SEQ, D], F32, kind="ExternalOutput")

    def scr(name, shape, dt):
        return nc.dram_tensor(name, list(shape), dt, kind=skind)

    hT_d = scr("s_hT", [D, LP], F32)
    qT_d = scr("s_qT", [D, LP], BF16)
    kT_d = scr("s_kT", [D, LP], BF16)
    v_d = scr("s_v", [LP, D], BF16)
    gq_d = scr("s_gq", [512, LP], F32)
    gk_d = scr("s_gk", [512, LP], F32)
    gsp_d = scr("s_gsp", [512, LP], F32)
    sg_d = scr("s_sg", [D, LP], BF16)
    onT_d = scr("s_onT", [D, LP], BF16)
    oT_d = scr("s_oT", [D, LP], F32)
    Win_b = [nc.dram_tensor(f"w_in{i}", [6, 128, 4096], BF16) for i in range(depth)]
    Wout_b = [nc.dram_tensor(f"w_out{i}", [2, 128, 4096], BF16) for i in range(depth)]
    Wup_b = [nc.dram_tensor(f"w_up{i}", [8, 128, 4096], BF16) for i in range(depth)]
    Wdn_b = [nc.dram_tensor(f"w_dn{i}", [8, 128, 4096], BF16) for i in range(depth)]
    Wgz_b = [nc.dram_tensor(f"w_gz{i}", [128, 128], BF16) for i in range(depth)]
    Wgu_b = [nc.dram_tensor(f"w_gu{i}", [16, 512], BF16) for i in range(depth)]

    P = Prog(nc)
    banks = [nc.alloc_psum_tensor(f"bank{i}", [128, 512], F32) for i in range(8)]

    class Arena:
        def __init__(self, base):
            self.off = base
            self.n = 0

        def __call__(self, shape, dt, name=None):
            esz = 4 if dt in (F32, I32) else 2
            nbytes = int(np.prod(shape[1:])) * esz
            nbytes = (nbytes + 63) // 64 * 64
            uid[0] += 1
            t = nc.alloc_sbuf_tensor_at(f"{name or 't'}_{uid[0]}", list(shape), dt, offset=self.off)
            self.off += nbytes
            assert self.off <= ARENA_END, f"SBUF overflow {self.off}"
            return t

    uid = [0]
    CA = Arena(ARENA0)
    identf = CA([128, 128], F32, "identf")
    identb = CA([128, 128], BF16, "identb")
    onesb = CA([128, 128], BF16, "onesb")
    onesf = CA([128, 128], F32, "onesf")
    trimask = CA([128, 128], F32, "trimask")
    negmask = CA([128, 128], BF16, "negmask")
    epsc = CA([128, 1], F32, "epsc")
    kki = CA([128, 1], I32, "kki")
    kkf = CA([128, 1], F32, "kkf")
    NMB = 40
    btab = CA([128, 8 * NMB], F32, "btab")
    neglam = CA([128, max(n_attn, 1)], F32, "neglam")
    negb = CA([128, 4 * max(n_gla, 1)], F32, "negb")
    fnw = CA([128, 8], F32, "fnw")
    scl = CA([128, 40], F32, "scl")
    lamb = CA([128, 256], F32, "lamb")
    lamt = CA([128, 8], F32, "lamt")
    vstg = CA([8, 128], F32, "vstg")
    CONST_END = CA.off

    def load_vec_T(vec_d, off, nrows, dst_ap, dst_key):
        P.dma("sp", vstg[0:nrows, :], bass.AP(vec_d, off, [[128, nrows], [1, 128]]), writes=["vstg"])
        P.op("pe", lambda e: e.transpose(out=banks[7][:, 0:nrows], in_=vstg[0:nrows, :], identity=identf[0:nrows, 0:nrows]),
             reads=["vstg", "identf"], writes=[("ps", 7)])
        P.op("dve", lambda e: e.tensor_copy(out=dst_ap, in_=banks[7][:, 0:nrows]), reads=[("ps", 7)], writes=[dst_key])

    for idt, nm, val in ((identf, "identf", 1.0), (identb, "identb", 1.0)):
        P.op("pool", lambda e, idt=idt: e.memset(idt[:], 0.0), writes=[nm])
        P.op("pool", lambda e, idt=idt: e.affine_select(out=idt[:], in_=idt[:], compare_op=ALU.not_equal, fill=1.0,
                                                         base=0, pattern=[[-1, 128]], channel_multiplier=1),
             reads=[nm], writes=[nm])
    P.op("pool", lambda e: e.memset(onesb[:], 1.0), writes=["onesb"])
    P.op("pool", lambda e: e.memset(onesf[:], 1.0), writes=["onesf"])
    P.op("pool", lambda e: e.memset(epsc[:], EPS), writes=["epsc"])
    P.op("pool", lambda e: e.memset(trimask[:], 1.0), writes=["trimask"])
    P.op("pool", lambda e: e.affine_select(out=trimask[:], in_=trimask[:], compare_op=ALU.is_ge, fill=0.0, base=0,
                                           pattern=[[1, 128]], channel_multiplier=-1), reads=["trimask"], writes=["trimask"])
    P.op("pool", lambda e: e.memset(negmask[:], 0.0), writes=["negmask"])
    P.op("pool", lambda e: e.affine_select(out=negmask[:], in_=negmask[:], compare_op=ALU.is_ge, fill=MASKNEG, base=0,
                                           pattern=[[1, 128]], channel_multiplier=-1), reads=["negmask"], writes=["negmask"])
    P.op("pool", lambda e: e.iota(out=kki[:], pattern=[[0, 1]], base=0, channel_multiplier=1), writes=["kki"])
    P.op("pool", lambda e: e.tensor_copy(out=kkf[:], in_=kki[:]), reads=["kki"], writes=["kkf"])
    slopes = [2.0 ** (-(h + 1)) for h in range(8)]
    NQ = [128 if h == 0 else (256 if h == 1 else 512) for h in range(8)]
    for h in range(8):
        for mi in range(NMB):
            m = mi - 4
            P.op("dve", lambda e, h=h, mi=mi, m=m: e.tensor_scalar(
                out=btab[:, h * NMB + mi:h * NMB + mi + 1], in0=kkf[:], scalar1=slopes[h],
                scalar2=-slopes[h] * (128.0 * m + NQ[h] / 2.0), op0=ALU.mult, op1=ALU.add),
                reads=["kkf"], writes=[("btab", h, mi)])
    load_vec_T(fnw_d, 0, 8, fnw[:, 0:8], "fnw")
    for j in range(n_attn):
        li = lambda_init_for(2 * j)
        P.dma("sp", lamb[:], bass.AP(alam_d, j * 256, [[0, 128], [1, 256]]), writes=["lamb"])
        P.op("dve", lambda e: e.tensor_tensor(out=lamb[:, 0:64], in0=lamb[:, 0:64], in1=lamb[:, 64:128], op=ALU.mult),
             reads=["lamb"], writes=["lamb"])
        P.op("dve", lambda e: e.tensor_tensor(out=lamb[:, 128:192], in0=lamb[:, 128:192], in1=lamb[:, 192:256], op=ALU.mult),
             reads=["lamb"], writes=["lamb"])
        P.op("dve", lambda e: e.reduce_sum(out=lamt[:, 0:1], in_=lamb[:, 0:64], axis=AX.X), reads=["lamb"], writes=["lamt"])
        P.op("dve", lambda e: e.reduce_sum(out=lamt[:, 1:2], in_=lamb[:, 128:192], axis=AX.X), reads=["lamb"], writes=["lamt"])
        P.op("act", lambda e: e.activation(out=lamt[:, 2:4], in_=lamt[:, 0:2], func=AF.Exp), reads=["lamt"], writes=["lamt"])
        P.op("dve", lambda e: e.tensor_tensor(out=lamt[:, 4:5], in0=lamt[:, 3:4], in1=lamt[:, 2:3], op=ALU.subtract),
             reads=["lamt"], writes=["lamt"])
        P.op("dve", lambda e, j=j, li=li: e.tensor_scalar(out=neglam[:, j:j + 1], in0=lamt[:, 4:5], scalar1=-li, scalar2=None,
                                                          op0=ALU.add), reads=["lamt"], writes=[("neglam", j)])
    for j in range(n_gla):
        load_vec_T(ggb_d, j * 512, 4, negb[:, 4 * j:4 * j + 4], ("negb", j))
        P.op("dve", lambda e, j=j: e.tensor_scalar(out=negb[:, 4 * j:4 * j + 4], in0=negb[:, 4 * j:4 * j + 4], scalar1=-1.0,
                                                   scalar2=None, op0=ALU.mult), reads=[("negb", j)], writes=[("negb", j)])

    rsc = CA([128, 24 * depth], F32, "rsc")
    CONST_END = CA.off
    for i in range(depth):
        j = i // 2
        load_vec_T(mixw_d, i * D, 8, rsc[:, 24 * i:24 * i + 8], ("rsc", i, 0))
        load_vec_T(mlpw_d, i * D, 8, rsc[:, 24 * i + 16:24 * i + 24], ("rsc", i, 2))
        if i % 2 == 0:
            load_vec_T(asub_d, j * 128, 1, scl[:, 16:17], "scl")
            P.op("dve", lambda e, i=i: e.tensor_scalar(out=rsc[:, 24 * i + 8:24 * i + 16], in0=onesf[:, 0:8], scalar1=scl[:, 16:17],
                                                       scalar2=1.0 - lambda_init_for(i), op0=ALU.mult, op1=ALU.mult),
                 reads=["scl", "onesf"], writes=[("rsc", i, 1)])
        else:
            load_vec_T(gnw_d, j * 256, 2, scl[:, 16:18], "scl")
            for c in range(8):
                P.op("dve", lambda e, c=c, i=i: e.tensor_copy(out=rsc[:, 24 * i + 8 + c:24 * i + 9 + c], in_=scl[:, 16 + (c % 2):17 + (c % 2)]),
                     reads=["scl"], writes=[("rsc", i, 1)])

    PREP_BASE = ARENA_END - 32768 - 2048
    pst = [nc.alloc_sbuf_tensor_at(f"pst{k}", [128, 2048], F32, offset=PREP_BASE + k * 8192) for k in range(3)]
    pob = [nc.alloc_sbuf_tensor_at(f"pob{k}", [128, 2048], BF16, offset=PREP_BASE + 24576 + k * 4096) for k in range(2)]

    def matrix_tasks(W_ap, K, Nc, Wb, CW, sc0, tag):
        ts = []
        for kc in range(K // 128):
            for p0 in range(0, Nc, 2048):
                ts.append(("mat", W_ap, kc, p0, min(2048, Nc - p0), Wb, CW, sc0, tag))
        return ts

    def group_tasks(k):
        ts = []
        if k >= 1:
            i = k - 1
            j = i // 2
            wo = awout_d.ap()[j] if i % 2 == 0 else gwout_d.ap()[j]
            ts += matrix_tasks(wo, D, D, Wout_b[i], 512, 24 * i + 8, ("out", i))
            ts += matrix_tasks(wup_d.ap()[i], D, DFF, Wup_b[i], 512, 24 * i + 16, ("up", i))
            ts += matrix_tasks(wdn_d.ap()[i], DFF, D, Wdn_b[i], 128, None, ("dn", i))
        if k < depth:
            i = k
            j = i // 2
            if i % 2 == 0:
                ts += matrix_tasks(awin_d.ap()[j], D, 3 * D, Win_b[i], 512, 24 * i, ("in", i))
            else:
                ts += matrix_tasks(gwin_d.ap()[j][:, 0:3072], D, 3072, Win_b[i], 512, 24 * i, ("in", i))
                ts.append(("gz", i, j))
                ts.append(("gu", i, j))
        return ts

    def t_load(t, n):
        b = n % 3
        if t[0] == "mat":
            _, W_ap, kc, p0, pn, Wb, CW, sc0, tag = t
            P.dma("sp", pst[b][:, :pn], W_ap[kc * 128:(kc + 1) * 128, p0:p0 + pn], writes=[("pst", b)])
        elif t[0] == "gz":
            _, i, j = t
            for kc in range(8):
                P.dma("sp", pst[b][:, kc * 16:(kc + 1) * 16], gwin_d.ap()[j][kc * 128:(kc + 1) * 128, 3072:3088], writes=[("pst", b, kc), ("pst", b)])
        else:
            _, i, j = t
            P.dma("sp", pst[b][0:16, 0:512], ggu_d.ap()[j], writes=[("pst", b)])

    def t_conv(t, n, eng):
        b = n % 3
        o = n % 2
        if t[0] == "mat":
            _, W_ap, kc, p0, pn, Wb, CW, sc0, tag = t
            rs = [("pst", b)]
            if sc0 is None:
                if eng == "act":
                    P.op("act", lambda e: e.activation(out=pob[o][:, :pn], in_=pst[b][:, :pn], func=AF.Copy), reads=rs, writes=[("pob", o)])
                else:
                    P.op(eng, lambda e: e.tensor_copy(out=pob[o][:, :pn], in_=pst[b][:, :pn]), reads=rs, writes=[("pob", o)])
            else:
                sc = rsc[:, sc0 + kc:sc0 + kc + 1]
                if eng == "act":
                    P.op("act", lambda e: e.activation(out=pob[o][:, :pn], in_=pst[b][:, :pn], func=AF.Copy, scale=sc), reads=rs, writes=[("pob", o)])
                else:
                    P.op(eng, lambda e: e.tensor_scalar(out=pob[o][:, :pn], in0=pst[b][:, :pn], scalar1=sc, scalar2=0.0, op0=ALU.mult, op1=ALU.add),
                         reads=rs, writes=[("pob", o)])
        elif t[0] == "gz":
            _, i, j = t
            for kc in range(8):
                P.op("dve", lambda e, kc=kc: e.tensor_scalar(out=pob[o][:, kc * 16:(kc + 1) * 16], in0=pst[b][:, kc * 16:(kc + 1) * 16],
                                                             scalar1=rsc[:, 24 * i + kc:24 * i + kc + 1], scalar2=None, op0=ALU.mult),
                     reads=[("pst", b, kc), ("pst", b)], writes=[("pob", o)])
        else:
            P.op("dve", lambda e: e.tensor_copy(out=pob[o][0:16, 0:512], in_=pst[b][0:16, 0:512]), reads=[("pst", b)], writes=[("pob", o)])

    def t_store(t, n):
        o = n % 2
        if t[0] == "mat":
            _, W_ap, kc, p0, pn, Wb, CW, sc0, tag = t
            for s in range(p0 // CW, (p0 + pn) // CW):
                P.dma("sp", Wb.ap()[s, :, kc * CW:(kc + 1) * CW], pob[o][:, s * CW - p0:(s + 1) * CW - p0], reads=[("pob", o)],
                      writes=[("wb", tag, s, kc)])
        elif t[0] == "gz":
            _, i, j = t
            P.dma("sp", Wgz_b[i].ap()[:, :], pob[o][:, 0:128], reads=[("pob", o)], writes=[("wgz", i)])
        else:
            _, i, j = t
            P.dma("sp", Wgu_b[i].ap()[:, :], pob[o][0:16, 0:512], reads=[("pob", o)], writes=[("wgu", i)])

    def prep_gen(tasks, engs):
        n = len(tasks)
        for s in range(n + 2):
            if s < n:
                t_load(tasks[s], s)
            if 0 <= s - 1 < n:
                t_conv(tasks[s - 1], s - 1, engs[(s - 1) % len(engs)])
            if 0 <= s - 2 < n:
                t_store(tasks[s - 2], s - 2)
            yield

    class BG:
        gen = None
        cnt = 0

        def step(self, every=1):
            if self.gen is None:
                return
            self.cnt += 1
            if self.cnt % every:
                return
            try:
                next(self.gen)
            except StopIteration:
                self.gen = None

        def drain(self):
            while self.gen is not None:
                self.step()

    bg = BG()
    bg.gen = prep_gen(group_tasks(0), ["dve", "act"])
    bg.drain()
    P.barrier()

    def t_phase(li):
        A = Arena(CONST_END)
        NB = 512
        hT = [A([128, 8, NB], F32, "hT") for _ in range(2)]
        xreg_off = A.off
        A([128, 4096], F32, "xreg")
        xst = [nc.alloc_sbuf_tensor_at(f"xst{li}_{k}", [128, 1024], F32, offset=xreg_off + k * 4096) for k in range(4)]
        onTs = [nc.alloc_sbuf_tensor_at(f"onTs{li}_{k}", [128, 8, NB], BF16, offset=xreg_off + k * 8192) for k in range(2)]
        r1_o = A([128, 8, NB], F32, "r1")
        off_r1 = A.off - 8 * NB * 4
        sqb8 = nc.alloc_sbuf_tensor_at(f"sqb8_{li}", [128, 8, NB], BF16, offset=off_r1)
        hnT = nc.alloc_sbuf_tensor_at(f"hnT_{li}", [128, 8, NB], BF16, offset=off_r1 + 8 * NB * 2)
        rt = A([128, NB], F32, "rt")
        rinv = A([128, NB], F32, "rinv")
        tmpf = [A([128, NB], F32, "tmpf") for _ in range(2)]
        hid = A([128, 32, NB], BF16, "hid")
        stg_off = A.off
        stg = [A([128, 4, NB], F32, "stg") for _ in range(2)]
        ost = [nc.alloc_sbuf_tensor_at(f"ost{li}_{k}", [128, 1024], F32, offset=stg_off + k * 8192) for k in range(2)]
        vst = A([128, 4, 1024], BF16, "vst")
        gsps = A([128, 4, NB], F32, "gsps")
        gzb = A([16, NB], BF16, "gzb")
        wgz = A([128, 128], BF16, "wgz")
        wgu = A([16, 512], BF16, "wgu")
        wsl = [A([128, 4096], BF16, "wsl") for _ in range(4)]
        stgB = [A([128, 8, NB], BF16, "stgB") for _ in range(2)]

        fin_layer = li - 1
        do_fin = li > 0
        do_in = li < depth
        is_attn_in = do_in and (li % 2 == 0)
        nb = len(blocks)

        seq_all = []
        if do_fin:
            seq_all += [("out", 0), ("out", 1)]
        for b_ in range(nb):
            if do_fin:
                if b_ + 1 < nb:
                    seq_all.append(("out", 0))
                seq_all += [("up", s) for s in range(8)] + [("dn", s) for s in range(8)]
                if b_ + 1 < nb:
                    seq_all.append(("out", 1))
            if do_in:
                seq_all += [("in", s) for s in range(6)]
        total = len(seq_all)
        issued = [0]
        gctr = [0]

        def w_issue(g):
            kind, s = seq_all[g]
            if kind == "out":
                src_ = Wout_b[fin_layer].ap()[s]
            elif kind == "up":
                src_ = Wup_b[fin_layer].ap()[s]
            elif kind == "dn":
                src_ = Wdn_b[fin_layer].ap()[s]
            else:
                src_ = Win_b[li].ap()[s]
            P.dma("sp", wsl[g % 4][:, :], src_, writes=[("wsl", g % 4)])

        def w_next(kind):
            g = gctr[0]
            gctr[0] += 1
            assert seq_all[g][0] == kind, (g, seq_all[g], kind)
            while issued[0] < min(total, g + 3):
                w_issue(issued[0])
                issued[0] += 1
            return wsl[g % 4], ("wsl", g % 4)

        acc = [0]

        def next_acc():
            b = acc[0] % 4
            acc[0] += 1
            return b

        if do_in and not is_attn_in:
            P.dma("sp", wgz[:], Wgz_b[li].ap()[:, :], writes=["wgz"])
            P.dma("sp", wgu[:], Wgu_b[li].ap()[:, :], writes=["wgu"])

        def norm(hb, hkey, N, want_hn=True):
            for hh in range(2):
                P.op("act", lambda e, hh=hh: e.activation(out=sqb8[:, 4 * hh:4 * hh + 4, :N], in_=hb[:, 4 * hh:4 * hh + 4, :N], func=AF.Square),
                     reads=[hkey], writes=[("sq8", hh)])
            for c in range(8):
                P.op("pe", lambda e, c=c: e.matmul(out=banks[4][:, :N], lhsT=onesb[:], rhs=sqb8[:, c, :N], start=(c == 0), stop=(c == 7)),
                     reads=[("sq8", c // 4), "onesb"], writes=[("ps", 4)])
            P.op("act", lambda e: e.activation(out=rt[:, :N], in_=banks[4][:, :N], func=AF.Sqrt, bias=epsc[:], scale=1.0 / D),
                 reads=[("ps", 4), "epsc"], writes=["rt"])
            P.op("dve", lambda e: e.reciprocal(out=rinv[:, :N], in_=rt[:, :N]), reads=["rt"], writes=["rinv"])
            if want_hn:
                for c in range(8):
                    eng = "pool" if c % 4 == 3 else "dve"
                    P.op(eng, lambda e, c=c: e.tensor_tensor(out=hnT[:, c, :N], in0=hb[:, c, :N], in1=rinv[:, :N], op=ALU.mult),
                         reads=[hkey, "rinv"], writes=[("hnT", c)])

        HN = [("hnT", c) for c in range(8)]

        def lin_fm(kind, oc_list, nkc, cw, rhs_of, rhs_keys, N, epi):
            w, wkey = w_next(kind)
            for j, oc in enumerate(oc_list):
                b = next_acc()
                for kc in range(nkc):
                    P.op("pe", lambda e, b=b, kc=kc, j=j: e.matmul(out=banks[b][:, :N], lhsT=w[:, kc * cw + j * 128:kc * cw + (j + 1) * 128],
                                                                   rhs=rhs_of(kc), start=(kc == 0), stop=(kc == nkc - 1)),
                         reads=[wkey] + rhs_keys, writes=[("ps", b)])
                epi(oc, b)

        def blk(bi):
            t0, nt = blocks[bi]
            return t0, nt, nt * 128, t0 * 128, hT[bi % 2], ("hT", bi % 2)

        def load_block(bi):
            t0, nt, N, c0, hb, hkey = blk(bi)
            if li == 0:
                for tt in range(nt):
                    tile_i = t0 + tt
                    xs = xst[tt % 4]
                    xkey = ("R2", tt % 4)
                    g0 = tile_i * 128
                    if tile_i == 0:
                        P.dma("sp", xs[0:NMETA, :], meta_d.ap()[:, :], writes=[xkey])
                        P.dma("sp", xs[NMETA:128, :], x_d.ap()[0:128 - NMETA, :], writes=[xkey])
                    else:
                        nv = min(128, NREAL - g0)
                        if nv < 128:
                            P.op("dve", lambda e, xs=xs: e.memset(xs[:], 0.0), writes=[xkey])
                        P.dma("sp", xs[0:nv, :], x_d.ap()[g0 - NMETA:g0 - NMETA + nv, :], writes=[xkey])
                for tt in range(nt):
                    xs = xst[tt % 4]
                    xkey = ("R2", tt % 4)
                    for half in range(2):
                        for c4 in range(4):
                            c = half * 4 + c4
                            P.op("pe", lambda e, xs=xs, c=c, c4=c4, half=half: e.transpose(
                                out=banks[6 + half][:, c4 * 128:(c4 + 1) * 128], in_=xs[:, c * 128:(c + 1) * 128], identity=identf[:]),
                                reads=[xkey, "identf"], writes=[("ps", 6 + half)])
                        src_ = banks[6 + half][:, :].rearrange("p (c t) -> p c t", t=128)
                        dst = hb[:, half * 4:half * 4 + 4, tt * 128:(tt + 1) * 128]
                        if half == 0:
                            P.op("act", lambda e, src_=src_, dst=dst: e.activation(out=dst, in_=src_, func=AF.Copy),
                                 reads=[("ps", 6 + half)], writes=[hkey])
                        else:
                            P.op("dve", lambda e, src_=src_, dst=dst: e.tensor_copy(out=dst, in_=src_),
                                 reads=[("ps", 6 + half)], writes=[hkey])
            else:
                P.dma("sp", hb[:, :, :N], hT_d.ap()[:, c0:c0 + N].rearrange("(c p) n -> p c n", p=128), writes=[hkey])
                P.dma("sp", onTs[bi % 2][:, :, :N], onT_d.ap()[:, c0:c0 + N].rearrange("(c p) n -> p c n", p=128), writes=[("onTs", bi % 2)])

        def make_epi_res(hb, hkey, N):
            def epi_res(oc, b):
                P.op("dve", lambda e, oc=oc, b=b: e.tensor_tensor(out=hb[:, oc, :N], in0=hb[:, oc, :N], in1=banks[b][:, :N], op=ALU.add),
                     reads=[hkey, ("ps", b)], writes=[hkey])
            return epi_res

        def stageA_half(bi, s):
            t0, nt, N, c0, hb, hkey = blk(bi)
            on = onTs[bi % 2]
            lin_fm("out", [4 * s + q for q in range(4)], 8, 512, lambda kc: on[:, kc, :N], [("onTs", bi % 2)], N, make_epi_res(hb, hkey, N))

        def do_block(bi):
            t0, nt, N, c0, hb, hkey = blk(bi)
            if bi == 0:
                load_block(0)
                if do_fin:
                    stageA_half(0, 0)
                    stageA_half(0, 1)
            if bi + 1 < nb:
                load_block(bi + 1)
            epi_res = make_epi_res(hb, hkey, N)
            if do_fin:
                norm(hb, hkey, N)
                if bi + 1 < nb:
                    stageA_half(bi + 1, 0)
                rl_i = [0]

                def epi_up(oc, b):
                    k = rl_i[0] % 2
                    rl_i[0] += 1
                    tf = tmpf[k]
                    P.op("act", lambda e, b=b, tf=tf: e.activation(out=tf[:, :N], in_=banks[b][:, :N], func=AF.Relu), reads=[("ps", b)],
                         writes=[("tmpf", k)])
                    P.op("dve", lambda e, oc=oc, tf=tf: e.tensor_tensor(out=hid[:, oc, :N], in0=tf[:, :N], in1=tf[:, :N], op=ALU.mult),
                         reads=[("tmpf", k)], writes=["hid"])

                for s in range(8):
                    lin_fm("up", [4 * s + q for q in range(4)], 8, 512, lambda kc: hnT[:, kc, :N], HN, N, epi_up)
                for s in range(8):
                    lin_fm("dn", [s], 32, 128, lambda kc: hid[:, kc, :N], ["hid"], N, epi_res)

            if do_in:
                norm(hb, hkey, N)
                if do_fin and bi + 1 < nb:
                    stageA_half(bi + 1, 1)
                if is_attn_in:
                    for which, dst_d in ((0, qT_d), (1, kT_d)):
                        sb = stgB[which]
                        skey = ("stgB", which)

                        def epi_cp(oc, b, sb=sb, skey=skey):
                            P.op("act", lambda e, oc=oc, b=b: e.activation(out=sb[:, oc, :N], in_=banks[b][:, :N], func=AF.Copy),
                                 reads=[("ps", b)], writes=[skey])

                        for s in range(2):
                            lin_fm("in", [4 * s + q for q in range(4)], 8, 512, lambda kc: hnT[:, kc, :N], HN, N, epi_cp)
                        P.dma("sp", dst_d.ap()[:, c0:c0 + N].rearrange("(c p) n -> p c n", p=128), sb[:, :, :N], reads=[skey],
                              writes=[("dqk", which, bi)])
                else:
                    sq_ = stg[0]
                    sk_ = stg[1]

                    def epi_q(oc, b):
                        if oc < 4:
                            P.op("act", lambda e, oc=oc, b=b: e.activation(out=sq_[:, oc, :N], in_=banks[b][:, :N], func=AF.Copy, scale=128.0 ** -0.5),
                                 reads=[("ps", b)], writes=[("stg", 0)])
                        else:
                            P.op("dve", lambda e, oc=oc, b=b: e.tensor_copy(out=sk_[:, oc - 4, :N], in_=banks[b][:, :N]),
                                 reads=[("ps", b)], writes=[("stg", 1)])

                    for s in range(2):
                        lin_fm("in", [4 * s + q for q in range(4)], 8, 512, lambda kc: hnT[:, kc, :N], HN, N, epi_q)
                    P.dma("sp", gq_d.ap()[:, c0:c0 + N].rearrange("(c p) n -> p c n", p=128), sq_[:, :, :N], reads=[("stg", 0)],
                          writes=[("dgq", bi)])
                    P.dma("sp", gk_d.ap()[:, c0:c0 + N].rearrange("(c p) n -> p c n", p=128), sk_[:, :, :N], reads=[("stg", 1)],
                          writes=[("dgk", bi)])
                for half in range(2):
                    w, wkey = w_next("in")
                    for tt in range(nt):
                        b = next_acc()
                        for kc in range(8):
                            P.op("pe", lambda e, b=b, kc=kc, tt=tt, w=w: e.matmul(out=banks[b][:, :], lhsT=hnT[:, kc, tt * 128:(tt + 1) * 128],
                                                                                  rhs=w[:, kc * 512:(kc + 1) * 512], start=(kc == 0), stop=(kc == 7)),
                                 reads=[wkey] + HN, writes=[("ps", b)])
                        if tt % 2 == 0:
                            P.op("act", lambda e, b=b, tt=tt, half=half: e.activation(out=vst[:, tt, half * 512:(half + 1) * 512], in_=banks[b][:, :], func=AF.Copy),
                                 reads=[("ps", b)], writes=["vst"])
                        else:
                            P.op("dve", lambda e, b=b, tt=tt, half=half: e.tensor_copy(out=vst[:, tt, half * 512:(half + 1) * 512], in_=banks[b][:, :]),
                                 reads=[("ps", b)], writes=["vst"])
                P.dma("sp", v_d.ap()[c0:c0 + N, :].rearrange("(t p) n -> p t n", p=128), vst[:, :nt, :], reads=["vst"], writes=[("dv", bi)])
                if not is_attn_in:
                    sb = stgB[0]
                    skey = ("stgB", 0)

                    def epi_g(oc, b):
                        P.op("act", lambda e, oc=oc, b=b: e.activation(out=sb[:, oc, :N], in_=banks[b][:, :N], func=AF.Silu), reads=[("ps", b)],
                             writes=[skey])

                    for s in range(2):
                        lin_fm("in", [4 * s + q for q in range(4)], 8, 512, lambda kc: hnT[:, kc, :N], HN, N, epi_g)
                    P.dma("sp", sg_d.ap()[:, c0:c0 + N].rearrange("(c p) n -> p c n", p=128), sb[:, :, :N], reads=[skey], writes=[("dsg", bi)])
                    b = next_acc()
                    for kc in range(8):
                        P.op("pe", lambda e, b=b, kc=kc: e.matmul(out=banks[b][0:16, :N], lhsT=wgz[:, kc * 16:(kc + 1) * 16], rhs=hnT[:, kc, :N],
                                                                  start=(kc == 0), stop=(kc == 7)), reads=["wgz"] + HN, writes=[("ps", b)])
                    P.op("act", lambda e, b=b: e.activation(out=gzb[:, :N], in_=banks[b][0:16, :N], func=AF.Copy), reads=[("ps", b)], writes=["gzb"])
                    jg = li // 2
                    for oc in range(4):
                        b = next_acc()
                        P.op("pe", lambda e, b=b, oc=oc: e.matmul(out=banks[b][:, :N], lhsT=wgu[0:16, oc * 128:(oc + 1) * 128], rhs=gzb[0:16, :N],
                                                                  start=True, stop=True), reads=["wgu", "gzb"], writes=[("ps", b)])
                        tf = tmpf[oc % 2]
                        P.op("act", lambda e, b=b, oc=oc, tf=tf: e.activation(out=tf[:, :N], in_=banks[b][:, :N], func=AF.Exp, scale=-1.0,
                                                                              bias=negb[:, 4 * jg + oc:4 * jg + oc + 1]),
                             reads=[("ps", b), ("negb", jg)], writes=[("tmpf", oc % 2)])
                        P.op("act", lambda e, oc=oc, tf=tf: e.activation(out=gsps[:, oc, :N], in_=tf[:, :N], func=AF.Ln, bias=1.0),
                             reads=[("tmpf", oc % 2)], writes=["gsps"])
                    P.dma("sp", gsp_d.ap()[:, c0:c0 + N].rearrange("(c p) n -> p c n", p=128), gsps[:, :, :N], reads=["gsps"], writes=[("dgsp", bi)])
                P.dma("sp", hT_d.ap()[:, c0:c0 + N].rearrange("(c p) n -> p c n", p=128), hb[:, :, :N], reads=[hkey], writes=[("dhT", bi)])
            else:
                norm(hb, hkey, N, want_hn=False)
                if bi + 1 < nb:
                    stageA_half(bi + 1, 1)
                yT = r1_o
                for c in range(8):
                    P.op("dve", lambda e, c=c: e.scalar_tensor_tensor(out=yT[:, c, :N], in0=hb[:, c, :N], scalar=fnw[:, c:c + 1], in1=rinv[:, :N],
                                                                      op0=ALU.mult, op1=ALU.mult),
                         reads=[hkey, "rinv", "fnw"], writes=[("sq8", 0), ("sq8", 1)] + HN)
                for tt in range(nt):
                    tile_i = t0 + tt
                    g0 = tile_i * 128
                    lo = max(g0, NMETA)
                    hi = min(g0 + 128, NREAL)
                    if hi <= lo:
                        continue
                    os_ = ost[tt % 2]
                    okey = ("stg", tt % 2)
                    for half in range(2):
                        for c4 in range(4):
                            c = half * 4 + c4
                            P.op("pe", lambda e, c=c, c4=c4, half=half, tt=tt: e.transpose(
                                out=banks[6 + half][:, c4 * 128:(c4 + 1) * 128], in_=yT[:, c, tt * 128:(tt + 1) * 128], identity=identf[:]),
                                reads=HN + [("sq8", 0), ("sq8", 1), "identf"], writes=[("ps", 6 + half)])
                        if half == 0:
                            P.op("act", lambda e, os_=os_, half=half: e.activation(out=os_[:, half * 512:(half + 1) * 512], in_=banks[6 + half][:, :], func=AF.Copy),
                                 reads=[("ps", 6 + half)], writes=[okey])
                        else:
                            P.op("dve", lambda e, os_=os_, half=half: e.tensor_copy(out=os_[:, half * 512:(half + 1) * 512], in_=banks[6 + half][:, :]),
                                 reads=[("ps", 6 + half)], writes=[okey])
                    P.dma("sp", out_d.ap()[lo - NMETA:hi - NMETA, :], os_[lo - g0:hi - g0, :], reads=[okey], writes=[("dout", tile_i)])

        for bi in range(nb):
            do_block(bi)
        assert gctr[0] == total, (gctr[0], total)
        P.barrier()

    def attn_core(li):
        ja = li // 2
        ts = []
        for k in (li + 1, li + 2):
            if k <= depth:
                ts += group_tasks(k)
        bg.gen = prep_gen(ts, ["pool"])
        bg.cnt = 0
        A = Arena(CONST_END)
        Vall = A([128, NT, 1024], BF16, "Vall")
        KT = [A([128, LP], BF16, "KT") for _ in range(2)]
        QT = [A([128, LP], BF16, "QT") for _ in range(2)]
        Et = [[A([128, 512], BF16, "Et") for _ in range(3)] for _ in range(2)]
        fbs = [[A([128, 512], F32, "fb") for _ in range(7)] for _ in range(2)]
        sqos = [A([128, 512], BF16, "sqo") for _ in range(2)]
        ons = [A([128, 512], BF16, "ons") for _ in range(2)]
        P.dma("sp", Vall[:, :, :], v_d.ap()[:, :].rearrange("(t p) n -> p t n", p=128), writes=["Vall"])
        sb_i = [0]
        e_i = [0]
        qb_i = [0]
        def do_head(h):
            kt = KT[h % 2]
            qt = QT[h % 2]
            kkey = ("KT", h % 2)
            qkey = ("QT", h % 2)
            P.dma("sp", kt[:, :], kT_d.ap()[h * 128:(h + 1) * 128, :], writes=[kkey])
            P.dma("sp", qt[:, :], qT_d.ap()[h * 128:(h + 1) * 128, :], writes=[qkey])
            nq = NQ[h]
            def do_qblock(q0):
                N = min(nq, LP - q0)
                kbs = []
                for kb in range((q0 + N) // 128):
                    k0 = kb * 128
                    gap = q0 - (k0 + 127)
                    if gap > 0 and slopes[h] * gap > ATT_SKIP:
                        continue
                    kbs.append(kb)
                nk = len(kbs)

                def issue_qk(idx):
                    kb = kbs[idx]
                    k0 = kb * 128
                    m = (q0 - k0) // 128
                    res = []
                    for mp in range(2):
                        b = (sb_i[0] % 2) * 2 + mp
                        lo, hi = mp * 64, (mp + 1) * 64
                        if k0 < q0:
                            cs = 0
                            P.op("pe", lambda e, b=b, lo=lo, hi=hi, k0=k0: e.matmul(out=banks[b][:, 0:N], lhsT=kt[lo:hi, k0:k0 + 128], rhs=qt[lo:hi, q0:q0 + N],
                                                                                   start=True, stop=True), reads=[kkey, qkey], writes=[("ps", b)])
                        else:
                            cs = k0 - q0
                            P.op("pe", lambda e, b=b, lo=lo, hi=hi, k0=k0, cs=cs: e.matmul(out=banks[b][:, cs:cs + 128], lhsT=kt[lo:hi, k0:k0 + 128],
                                                                                          rhs=qt[lo:hi, k0:k0 + 128], start=True, stop=False),
                                 reads=[kkey, qkey], writes=[("ps", b)])
                            P.op("pe", lambda e, b=b, cs=cs: e.matmul(out=banks[b][:, cs:cs + 128], lhsT=identb[:], rhs=negmask[:], start=False, stop=True),
                                 reads=["identb", "negmask"], writes=[("ps", b)])
                            if cs + 128 < N:
                                P.op("pe", lambda e, b=b, lo=lo, hi=hi, k0=k0, cs=cs: e.matmul(out=banks[b][:, cs + 128:N], lhsT=kt[lo:hi, k0:k0 + 128],
                                                                                              rhs=qt[lo:hi, q0 + cs + 128:q0 + N], start=True, stop=True),
                                     reads=[kkey, qkey], writes=[("ps", b)])
                        res.append((b, cs))
                    sb_i[0] += 1
                    return res, m

                def issue_exp_pv(idx, res, m):
                    kb = kbs[idx]
                    first = idx == 0
                    last = idx == nk - 1
                    ei = e_i[0] % 3
                    e_i[0] += 1
                    for mp in range(2):
                        b, cs = res[mp]
                        et = Et[mp][ei]
                        ekey = ("Et", mp, ei)
                        col = h * NMB + (m + 4)
                        P.op("act", lambda e, b=b, cs=cs, et=et, col=col: e.activation(out=et[:, cs:N], in_=banks[b][:, cs:N], func=AF.Exp,
                                                                                      bias=btab[:, col:col + 1], scale=0.125),
                             reads=[("ps", b), ("btab", h, m + 4)], writes=[ekey])
                    for mp in range(2):
                        b, cs = res[mp]
                        et = Et[mp][ei]
                        ekey = ("Et", mp, ei)
                        P.op("pe", lambda e, mp=mp, cs=cs, et=et, kb=kb: e.matmul(out=banks[4 + mp][:, cs:N], lhsT=Vall[:, kb, h * 128:(h + 1) * 128],
                                                                                 rhs=et[:, cs:N], start=first, stop=last),
                             reads=["Vall", ekey], writes=[("ps", 4 + mp)])
                        P.op("pe", lambda e, mp=mp, cs=cs, et=et: e.matmul(out=banks[6 + mp][:, cs:N], lhsT=onesb[:], rhs=et[:, cs:N], start=first, stop=last),
                             reads=["onesb", ekey], writes=[("ps", 6 + mp)])

                pend = issue_qk(0)
                for idx in range(nk):
                    nxt = issue_qk(idx + 1) if idx + 1 < nk else None
                    issue_exp_pv(idx, pend[0], pend[1])
                    pend = nxt
                    bg.step(every=3)
                fsel = qb_i[0] % 2
                fb = fbs[fsel]
                P.op("dve", lambda e: e.reciprocal(out=fb[0][:, :N], in_=banks[6][:, :N]), reads=[("ps", 6)], writes=[("fb", fsel, 0)])
                P.op("dve", lambda e: e.reciprocal(out=fb[1][:, :N], in_=banks[7][:, :N]), reads=[("ps", 7)], writes=[("fb", fsel, 1)])
                P.op("act", lambda e: e.activation(out=fb[2][:, :N], in_=banks[4][:, :N], func=AF.Copy), reads=[("ps", 4)], writes=[("fb", fsel, 2)])
                P.op("act", lambda e: e.activation(out=fb[3][:, :N], in_=banks[5][:, :N], func=AF.Copy), reads=[("ps", 5)], writes=[("fb", fsel, 3)])
                P.op("pool", lambda e: e.tensor_tensor(out=fb[2][:, :N], in0=fb[2][:, :N], in1=fb[0][:, :N], op=ALU.mult),
                     reads=[("fb", fsel, 2), ("fb", fsel, 0)], writes=[("fb", fsel, 2)])
                P.op("pool", lambda e: e.tensor_tensor(out=fb[3][:, :N], in0=fb[3][:, :N], in1=fb[1][:, :N], op=ALU.mult),
                     reads=[("fb", fsel, 3), ("fb", fsel, 1)], writes=[("fb", fsel, 3)])
                P.op("dve", lambda e: e.scalar_tensor_tensor(out=fb[4][:, :N], in0=fb[3][:, :N], scalar=neglam[:, ja:ja + 1], in1=fb[2][:, :N],
                                                             op0=ALU.mult, op1=ALU.add), reads=[("fb", fsel, 2), ("fb", fsel, 3), ("neglam", ja)], writes=[("fb", fsel, 4)])
                P.op("act", lambda e: e.activation(out=sqos[fsel][:, :N], in_=fb[4][:, :N], func=AF.Square), reads=[("fb", fsel, 4)], writes=[("sqo", fsel)])
                bss = (sb_i[0] % 2) * 2
                sb_i[0] += 1
                P.op("pe", lambda e, bss=bss: e.matmul(out=banks[bss][:, :N], lhsT=onesb[:], rhs=sqos[fsel][:, :N], start=True, stop=True),
                     reads=["onesb", ("sqo", fsel)], writes=[("ps", bss)])
                P.op("act", lambda e, bss=bss: e.activation(out=fb[5][:, :N], in_=banks[bss][:, :N], func=AF.Sqrt, bias=epsc[:], scale=1.0 / 128),
                     reads=[("ps", bss), "epsc"], writes=[("fb", fsel, 5)])
                P.op("dve", lambda e: e.reciprocal(out=fb[6][:, :N], in_=fb[5][:, :N]), reads=[("fb", fsel, 5)], writes=[("fb", fsel, 6)])
                oi = qb_i[0] % 2
                qb_i[0] += 1
                on = ons[oi]
                P.op("pool", lambda e, on=on: e.tensor_tensor(out=on[:, :N], in0=fb[4][:, :N], in1=fb[6][:, :N], op=ALU.mult),
                     reads=[("fb", fsel, 4), ("fb", fsel, 6)], writes=[("ons", oi)])
                P.dma("sp", onT_d.ap()[h * 128:(h + 1) * 128, q0:q0 + N], on[:, :N], reads=[("ons", oi)], writes=[("donT", h, q0)])

            for q0 in range(0, LP, nq):
                do_qblock(q0)

        for h in range(8):
            do_head(h)
        bg.drain()
        P.barrier()

    def gla_core(li):
        A = Arena(CONST_END)
        NB = 512
        gq = [A([128, 4, NB], F32, "gq") for _ in range(2)]
        gk = [A([128, 4, NB], F32, "gk") for _ in range(2)]
        gs = [A([128, 4, NB], F32, "gs") for _ in range(2)]
        vv = [A([128, 4, 1024], BF16, "vv") for _ in range(2)]
        ost = [A([128, 8, NB], F32, "ost") for _ in range(2)]
        S = A([128, 4, 256], F32, "S")
        Sb = A([128, 4, 256], BF16, "Sb")
        NS = 8
        bpos = [A([128, 128], F32, "bpos") for _ in range(NS)]
        Ep = [A([128, 128], F32, "Ep") for _ in range(NS)]
        Em = [A([128, 128], F32, "Em") for _ in range(NS)]
        qs = [A([128, 128], BF16, "qs") for _ in range(NS)]
        ks = [A([128, 128], BF16, "ks") for _ in range(NS)]
        ktl = [A([128, 128], BF16, "ktl") for _ in range(NS)]
        ktk = [A([128, 128], BF16, "ktk") for _ in range(NS)]
        Am = [A([128, 128], BF16, "Am") for _ in range(NS)]
        sgs = [A([128, 8, NB], BF16, "sgs") for _ in range(2)]
        onst = [A([128, 8, NB], BF16, "onst") for _ in range(2)]
        sqg = [A([128, 2, NB], BF16, "sqg") for _ in range(2)]
        grt = [A([128, NB], F32, "grt") for _ in range(2)]
        gri = [A([128, NB], F32, "gri") for _ in range(2)]
        gtf = [A([128, NB], F32, "gtf") for _ in range(2)]
        P.op("dve", lambda e: e.memset(S[:], 0.0), writes=[("S", 0), ("S", 1), ("S", 2), ("S", 3)])
        P.op("dve", lambda e: e.memset(Sb[:], 0.0), writes=[("Sb", 0), ("Sb", 1), ("Sb", 2), ("Sb", 3)])
        it = [0]

        def do_gblock(bi, t0, nt):
            N = nt * 128
            c0 = t0 * 128
            k2 = bi % 2
            P.dma("sp", gq[k2][:, :, :N], gq_d.ap()[:, c0:c0 + N].rearrange("(c p) n -> p c n", p=128), writes=[("gq", k2)])
            P.dma("sp", gk[k2][:, :, :N], gk_d.ap()[:, c0:c0 + N].rearrange("(c p) n -> p c n", p=128), writes=[("gk", k2)])
            P.dma("sp", gs[k2][:, :, :N], gsp_d.ap()[:, c0:c0 + N].rearrange("(c p) n -> p c n", p=128), writes=[("gs", k2)])
            P.dma("sp", vv[k2][:, :nt, :], v_d.ap()[c0:c0 + N, :].rearrange("(t p) n -> p t n", p=128), writes=[("vv", k2)])
            P.dma("sp", sgs[k2][:, :, :N], sg_d.ap()[:, c0:c0 + N].rearrange("(c p) n -> p c n", p=128), writes=[("sgs", k2)])

            def do_chunk(tt):
                cs = tt * 128
                par = it[0] % 2
                it[0] += 1
                ix = [par * 4 + h for h in range(4)]
                pA = [banks[h][:, 0:128] for h in range(4)]
                pO = [banks[h][:, 128:384] for h in range(4)]
                pD = [banks[4 + h][:, 0:256] for h in range(4)]
                pT = [banks[4 + h][:, 256:320].bitcast(BF16) for h in range(4)]
                for h in range(4):
                    i3 = ix[h]
                    P.op("dve", lambda e, i3=i3, h=h: e.tensor_tensor_scan(out=bpos[i3][:], data0=onesf[:], data1=gs[k2][:, h, cs:cs + 128], initial=0.0,
                                                                          op0=ALU.mult, op1=ALU.add), reads=[("gs", k2), "onesf"], writes=[("bpos", i3)])
                for h in range(4):
                    i3 = ix[h]
                    P.op("act", lambda e, i3=i3: e.activation(out=Ep[i3][:], in_=bpos[i3][:], func=AF.Exp, scale=-1.0 / 16), reads=[("bpos", i3)],
                         writes=[("Ep", i3)])
                    P.op("act", lambda e, i3=i3: e.activation(out=Em[i3][:], in_=bpos[i3][:], func=AF.Exp, scale=1.0 / 16), reads=[("bpos", i3)],
                         writes=[("Em", i3)])
                for h in range(4):
                    i3 = ix[h]
                    P.op("dve", lambda e, i3=i3, h=h: e.tensor_tensor(out=qs[i3][:], in0=gq[k2][:, h, cs:cs + 128], in1=Ep[i3][:], op=ALU.mult),
                         reads=[("gq", k2), ("Ep", i3)], writes=[("qs", i3)])
                    P.op("pool", lambda e, i3=i3, h=h: e.tensor_tensor(out=ks[i3][:], in0=gk[k2][:, h, cs:cs + 128], in1=Em[i3][:], op=ALU.mult),
                         reads=[("gk", k2), ("Em", i3)], writes=[("ks", i3)])
                    P.op("dve", lambda e, i3=i3, h=h: e.scalar_tensor_tensor(out=ktl[i3][:], in0=gk[k2][:, h, cs:cs + 128], scalar=Ep[i3][:, 127:128],
                                                                            in1=Em[i3][:], op0=ALU.mult, op1=ALU.mult),
                         reads=[("gk", k2), ("Ep", i3), ("Em", i3)], writes=[("ktl", i3)])
                for h in range(4):
                    i3 = ix[h]
                    P.op("pe", lambda e, i3=i3, h=h: e.matmul(out=pA[h], lhsT=ks[i3][:], rhs=qs[i3][:], start=True, stop=True),
                         reads=[("ks", i3), ("qs", i3)], writes=[("ps", h)])
                    P.op("pe", lambda e, i3=i3, h=h: e.transpose(out=pT[h], in_=ktl[i3][:], identity=identb[:]), reads=[("ktl", i3), "identb"],
                         writes=[("ps", 4 + h)])
                for h in range(4):
                    i3 = ix[h]
                    P.op("act", lambda e, i3=i3, h=h: e.activation(out=ktk[i3][:], in_=pT[h], func=AF.Copy), reads=[("ps", 4 + h)], writes=[("ktk", i3)])
                    P.op("dve", lambda e, i3=i3, h=h: e.tensor_tensor(out=Am[i3][:], in0=pA[h], in1=trimask[:], op=ALU.mult),
                         reads=[("ps", h), "trimask"], writes=[("Am", i3)])
                for h in range(4):
                    i3 = ix[h]
                    for ec in range(2):
                        P.op("pe", lambda e, i3=i3, ec=ec, h=h: e.matmul(out=banks[h][:, 128 + ec * 128:256 + ec * 128],
                                                                        lhsT=vv[k2][:, tt, h * 256 + ec * 128:h * 256 + (ec + 1) * 128], rhs=Am[i3][:],
                                                                        start=True, stop=False), reads=[("vv", k2), ("Am", i3)], writes=[("ps", h)])
                        P.op("pe", lambda e, i3=i3, ec=ec, h=h: e.matmul(out=banks[h][:, 128 + ec * 128:256 + ec * 128],
                                                                        lhsT=Sb[:, h, ec * 128:(ec + 1) * 128], rhs=qs[i3][:], start=False, stop=True),
                             reads=[("Sb", h), ("qs", i3)], writes=[("ps", h)])
                    P.op("pe", lambda e, i3=i3, h=h: e.matmul(out=pD[h], lhsT=ktk[i3][:], rhs=vv[k2][:, tt, h * 256:(h + 1) * 256],
                                                              start=True, stop=True), reads=[("ktk", i3), ("vv", k2)], writes=[("ps", 4 + h)])
                for h in range(4):
                    i3 = ix[h]
                    P.op("dve", lambda e, i3=i3, h=h: e.scalar_tensor_tensor(out=S[:, h, :], in0=S[:, h, :], scalar=Ep[i3][:, 127:128],
                                                                            in1=pD[h], op0=ALU.mult, op1=ALU.add),
                         reads=[("S", h), ("Ep", i3), ("ps", 4 + h)], writes=[("S", h)])
                    P.op("pool", lambda e, h=h: e.tensor_copy(out=Sb[:, h, :], in_=S[:, h, :]), reads=[("S", h)], writes=[("Sb", h)])
                    src_ = pO[h].rearrange("p (c t) -> p c t", t=128)
                    P.op("act", lambda e, src_=src_, h=h: e.activation(out=ost[k2][:, 2 * h:2 * h + 2, cs:cs + 128], in_=src_, func=AF.Copy),
                         reads=[("ps", h)], writes=[("ost", k2, h)])

            for tt in range(nt):
                do_chunk(tt)
            for h in range(4):
                p2 = h % 2
                P.op("act", lambda e, h=h, p2=p2: e.activation(out=sqg[p2][:, :, :N], in_=ost[k2][:, 2 * h:2 * h + 2, :N], func=AF.Square),
                     reads=[("ost", k2, h)], writes=[("sqg", p2)])
                for jj in range(2):
                    P.op("pe", lambda e, h=h, jj=jj, p2=p2: e.matmul(out=banks[4 + h][:, :N], lhsT=onesb[:], rhs=sqg[p2][:, jj, :N],
                                                                    start=(jj == 0), stop=(jj == 1)),
                         reads=[("sqg", p2), "onesb"], writes=[("ps", 4 + h)])
                P.op("act", lambda e, h=h, p2=p2: e.activation(out=grt[p2][:, :N], in_=banks[4 + h][:, :N], func=AF.Sqrt, bias=epsc[:], scale=1.0 / 256),
                     reads=[("ps", 4 + h), "epsc"], writes=[("grt", p2)])
                P.op("dve", lambda e, p2=p2: e.reciprocal(out=gri[p2][:, :N], in_=grt[p2][:, :N]), reads=[("grt", p2)], writes=[("gri", p2)])
                for jj in range(2):
                    P.op("dve", lambda e, h=h, jj=jj, p2=p2: e.tensor_tensor(out=gtf[jj][:, :N], in0=ost[k2][:, 2 * h + jj, :N], in1=gri[p2][:, :N], op=ALU.mult),
                         reads=[("ost", k2, h), ("gri", p2)], writes=[("gtf", jj)])
                    P.op("pool", lambda e, h=h, jj=jj: e.tensor_tensor(out=onst[k2][:, 2 * h + jj, :N], in0=gtf[jj][:, :N], in1=sgs[k2][:, 2 * h + jj, :N], op=ALU.mult),
                         reads=[("gtf", jj), ("sgs", k2)], writes=[("onst", k2, h)])
            P.dma("sp", onT_d.ap()[:, c0:c0 + N].rearrange("(c p) n -> p c n", p=128), onst[k2][:, :, :N],
                  reads=[("onst", k2, h) for h in range(4)], writes=[("donT", bi)])

        for bi, (t0, nt) in enumerate(blocks):
            do_gblock(bi, t0, nt)
        P.barrier()

    for li in range(depth + 1):
        t_phase(li)
        if li < depth:
            if li % 2 == 0:
                attn_core(li)
            else:
                gla_core(li)
    P.emit()
    return nc


_CACHE = {}
_NAMES = ["meta_tokens", "mix_norm_w", "attn_w_in", "attn_lambda", "attn_subln_w", "attn_w_out", "gla_w_in", "gla_w_gate_up",
          "gla_gate_bias", "gla_norm_w", "gla_w_out", "mlp_norm_w", "mlp_w_up", "mlp_w_down", "final_norm_w"]


def run(inputs, depth=4, dbg=False):
    x = np.ascontiguousarray(np.asarray(inputs["x"], dtype=np.float32))
    B, SEQ, _ = x.shape
    key = (SEQ, depth, dbg)
    if key not in _CACHE:
        _CACHE[key] = build(SEQ, depth, dbg)
    nc = _CACHE[key]
    shared = {n: np.ascontiguousarray(np.asarray(inputs[n], dtype=np.float32)) for n in _NAMES}
    in_maps = []
    for b in range(B):
        m = dict(shared)
        m["x"] = x[b]
        in_maps.append(m)
    res = run_bass_kernel_spmd(nc, in_maps, core_ids=list(range(B)))
    return res


def kernel(**inputs):
    res = run(inputs, depth=4, dbg=False)
    B = np.asarray(inputs["x"]).shape[0]
    return np.stack([np.asarray(res.results[b]["out"], dtype=np.float32) for b in range(B)], axis=0)
```

```python
import math
import numpy as np
import concourse.bass as bass
import concourse.mybir as mybir
from concourse.bass_utils import run_bass_kernel_spmd

F32 = mybir.dt.float32
BF16 = mybir.dt.bfloat16
I32 = mybir.dt.int32
ALU = mybir.AluOpType
AF = mybir.ActivationFunctionType
AX = mybir.AxisListType

ENGS = ("pe", "act", "dve", "pool", "sp")
EPOCH = 16000
DMA_K = 8


class _Op:
    __slots__ = ("eng", "fn", "deps", "signal", "sig_seq", "dma", "dma_slot", "dma_val", "pre")

    def __init__(self, eng, fn):
        self.eng = eng
        self.fn = fn
        self.deps = []
        self.signal = False
        self.sig_seq = None
        self.dma = False
        self.dma_slot = None
        self.dma_val = None
        self.pre = None


class Prog:
    def __init__(self, nc):
        self.nc = nc
        self.ops = {e: [] for e in ENGS}
        self.last_w = {}
        self.readers = {}
        self.dma_hist = {e: [] for e in ENGS}
        self.bar_deps = []
        self.bar_pending = set()

    def _add(self, eng, fn, reads, writes, dma=False):
        op = _Op(eng, fn)
        op.dma = dma
        deps = []
        if eng in self.bar_pending:
            deps.extend(self.bar_deps)
            self.bar_pending.discard(eng)
        for r in reads:
            w = self.last_w.get(r)
            if w is not None:
                deps.append(w)
        for r in writes:
            w = self.last_w.get(r)
            if w is not None:
                deps.append(w)
            deps.extend(self.readers.get(r, ()))
        for r in reads:
            self.readers.setdefault(r, []).append(op)
        for r in writes:
            self.last_w[r] = op
            self.readers[r] = []
        op.deps = deps
        self.ops[eng].append(op)
        if dma:
            hist = self.dma_hist[eng]
            n = len(hist)
            op.dma_slot = n % DMA_K
            op.dma_val = 16 * (n // DMA_K + 1)
            if n >= DMA_K:
                op.pre = hist[n - DMA_K]
            hist.append(op)
        return op

    def op(self, eng, fn, reads=(), writes=()):
        return self._add(eng, fn, reads, writes)

    def dma(self, eng, out, in_, reads=(), writes=(), **kw):
        return self._add(eng, lambda e: e.dma_start(out=out, in_=in_, **kw), reads, writes, dma=True)

    def barrier(self):
        lasts = []
        for e in ENGS:
            for op in reversed(self.ops[e]):
                if not op.dma:
                    lasts.append(op)
                    break
            lasts.extend(self.dma_hist[e][-DMA_K:])
        self.bar_deps = lasts
        self.bar_pending = set(ENGS)
        self.last_w = {}
        self.readers = {}

    def emit(self):
        nc = self.nc
        for e in ENGS:
            for op in self.ops[e]:
                for d in op.deps:
                    if d.dma:
                        continue
                    if d.eng == "pe" and op.eng == "pe" and not op.dma:
                        continue
                    d.signal = True
        nsig = {}
        for e in ENGS:
            s = 0
            for op in self.ops[e]:
                if op.signal and not op.dma:
                    s += 1
                    op.sig_seq = s
            nsig[e] = s
        import contextlib
        stack = contextlib.ExitStack()
        sems = {}
        dsems = {}
        for e in ENGS:
            n_ep = max(1, (nsig[e] + EPOCH - 1) // EPOCH)
            sems[e] = [stack.enter_context(nc.semaphore(f"s_{e}_{k}")) for k in range(n_ep)]
            if self.dma_hist[e]:
                dsems[e] = [stack.enter_context(nc.semaphore(f"d_{e}_{k}")) for k in range(DMA_K)]

        def target(d):
            if d.dma:
                return ("d", d.eng, d.dma_slot), dsems[d.eng][d.dma_slot], d.dma_val
            ep = (d.sig_seq - 1) // EPOCH
            return ("s", d.eng, ep), sems[d.eng][ep], d.sig_seq - ep * EPOCH

        with stack:
            block = stack.enter_context(nc.Block())

            def run(e, h):
                seen = {}
                for op in self.ops[e]:
                    need = {}
                    dl = op.deps if op.pre is None else op.deps + [op.pre]
                    for d in dl:
                        if (not d.dma) and d.eng == "pe" and e == "pe" and not op.dma:
                            continue
                        key, sem, val = target(d)
                        if seen.get(key, 0) >= val:
                            continue
                        if key not in need or need[key][1] < val:
                            need[key] = (sem, val)
                    for key, (sem, val) in need.items():
                        h.wait_ge(sem, val)
                        seen[key] = val
                    ins = op.fn(h)
                    if op.dma:
                        ins.then_inc(dsems[e][op.dma_slot], 16)
                    elif op.signal:
                        ep = (op.sig_seq - 1) // EPOCH
                        ins.then_inc(sems[e][ep], 1)
                hist = self.dma_hist[e]
                for d in hist[-DMA_K:]:
                    h.wait_ge(dsems[e][d.dma_slot], d.dma_val)

            @block.tensor
            def _(h):
                run("pe", h)

            @block.scalar
            def _(h):
                run("act", h)

            @block.vector
            def _(h):
                run("dve", h)

            @block.gpsimd
            def _(h):
                run("pool", h)

            @block.sync
            def _(h):
                run("sp", h)


D = 1024
NMETA = 16
DFF = 4096
EPS = 1e-6
ARENA0 = 20480
ARENA_END = 229376 - 1024
MASKNEG = -240000.0
ATT_SKIP = 60.0


def lambda_init_for(i):
    return 0.8 - 0.6 * math.exp(-0.3 * i)


def build(SEQ, depth=4, dbg=False):
    nc = bass.Bass("TRN2", target_bir_lowering=False)
    NREAL = SEQ + NMETA
    NT = (NREAL + 127) // 128
    LP = NT * 128
    n_attn = (depth + 1) // 2
    n_gla = depth // 2
    blocks = [(t0, min(4, NT - t0)) for t0 in range(0, NT, 4)]
    skind = "ExternalOutput" if dbg else "Internal"

    def din(name, shape):
        return nc.dram_tensor(name, list(shape), F32, kind="ExternalInput")

    x_d = din("x", [SEQ, D])
    meta_d = din("meta_tokens", [NMETA, D])
    mixw_d = din("mix_norm_w", [depth, D])
    awin_d = din("attn_w_in", [n_attn, D, 3 * D])
    alam_d = din("attn_lambda", [n_attn, 4, 64])
    asub_d = din("attn_subln_w", [n_attn, 128])
    awout_d = din("attn_w_out", [n_attn, D, D])
    gwin_d = din("gla_w_in", [max(n_gla, 1), D, 3088])
    ggu_d = din("gla_w_gate_up", [max(n_gla, 1), 16, 512])
    ggb_d = din("gla_gate_bias", [max(n_gla, 1), 512])
    gnw_d = din("gla_norm_w", [max(n_gla, 1), 256])
    gwout_d = din("gla_w_out", [max(n_gla, 1), D, D])
    mlpw_d = din("mlp_norm_w", [depth, D])
    wup_d = din("mlp_w_up", [depth, D, DFF])
    wdn_d = din("mlp_w_down", [depth, DFF, D])
    fnw_d = din("final_norm_w", [D])
    out_d = nc.dram_tensor("out", [SEQ, D], F32, kind="ExternalOutput")

    def scr(name, shape, dt):
        return nc.dram_tensor(name, list(shape), dt, kind=skind)

    hT_d = scr("s_hT", [D, LP], F32)
    qT_d = scr("s_qT", [D, LP], BF16)
    kT_d = scr("s_kT", [D, LP], BF16)
    v_d = scr("s_v", [LP, D], BF16)
    gq_d = scr("s_gq", [512, LP], F32)
    gk_d = scr("s_gk", [512, LP], F32)
    gsp_d = scr("s_gsp", [512, LP], F32)
    sg_d = scr("s_sg", [D, LP], BF16)
    onT_d = scr("s_onT", [D, LP], BF16)
    oT_d = scr("s_oT", [D, LP], F32)
    Win_b = [nc.dram_tensor(f"w_in{i}", [6, 128, 4096], BF16) for i in range(depth)]
    Wout_b = [nc.dram_tensor(f"w_out{i}", [2, 128, 4096], BF16) for i in range(depth)]
    Wup_b = [nc.dram_tensor(f"w_up{i}", [8, 128, 4096], BF16) for i in range(depth)]
    Wdn_b = [nc.dram_tensor(f"w_dn{i}", [8, 128, 4096], BF16) for i in range(depth)]
    Wgz_b = [nc.dram_tensor(f"w_gz{i}", [128, 128], BF16) for i in range(depth)]
    Wgu_b = [nc.dram_tensor(f"w_gu{i}", [16, 512], BF16) for i in range(depth)]

    P = Prog(nc)
    banks = [nc.alloc_psum_tensor(f"bank{i}", [128, 512], F32) for i in range(8)]

    class Arena:
        def __init__(self, base):
            self.off = base
            self.n = 0

        def __call__(self, shape, dt, name=None):
            esz = 4 if dt in (F32, I32) else 2
            nbytes = int(np.prod(shape[1:])) * esz
            nbytes = (nbytes + 63) // 64 * 64
            uid[0] += 1
            t = nc.alloc_sbuf_tensor_at(f"{name or 't'}_{uid[0]}", list(shape), dt, offset=self.off)
            self.off += nbytes
            assert self.off <= ARENA_END, f"SBUF overflow {self.off}"
            return t

    uid = [0]
    CA = Arena(ARENA0)
    identf = CA([128, 128], F32, "identf")
    identb = CA([128, 128], BF16, "identb")
    onesb = CA([128, 128], BF16, "onesb")
    onesf = CA([128, 128], F32, "onesf")
    trimask = CA([128, 128], F32, "trimask")
    negmask = CA([128, 128], BF16, "negmask")
    epsc = CA([128, 1], F32, "epsc")
    kki = CA([128, 1], I32, "kki")
    kkf = CA([128, 1], F32, "kkf")
    NMB = 40
    btab = CA([128, 8 * NMB], F32, "btab")
    neglam = CA([128, max(n_attn, 1)], F32, "neglam")
    negb = CA([128, 4 * max(n_gla, 1)], F32, "negb")
    fnw = CA([128, 8], F32, "fnw")
    scl = CA([128, 40], F32, "scl")
    lamb = CA([128, 256], F32, "lamb")
    lamt = CA([128, 8], F32, "lamt")
    vstg = CA([8, 128], F32, "vstg")
    CONST_END = CA.off

    def load_vec_T(vec_d, off, nrows, dst_ap, dst_key):
        P.dma("sp", vstg[0:nrows, :], bass.AP(vec_d, off, [[128, nrows], [1, 128]]), writes=["vstg"])
        P.op("pe", lambda e: e.transpose(out=banks[7][:, 0:nrows], in_=vstg[0:nrows, :], identity=identf[0:nrows, 0:nrows]),
             reads=["vstg", "identf"], writes=[("ps", 7)])
        P.op("dve", lambda e: e.tensor_copy(out=dst_ap, in_=banks[7][:, 0:nrows]), reads=[("ps", 7)], writes=[dst_key])

    for idt, nm, val in ((identf, "identf", 1.0), (identb, "identb", 1.0)):
        P.op("pool", lambda e, idt=idt: e.memset(idt[:], 0.0), writes=[nm])
        P.op("pool", lambda e, idt=idt: e.affine_select(out=idt[:], in_=idt[:], compare_op=ALU.not_equal, fill=1.0,
                                                         base=0, pattern=[[-1, 128]], channel_multiplier=1),
             reads=[nm], writes=[nm])
    P.op("pool", lambda e: e.memset(onesb[:], 1.0), writes=["onesb"])
    P.op("pool", lambda e: e.memset(onesf[:], 1.0), writes=["onesf"])
    P.op("pool", lambda e: e.memset(epsc[:], EPS), writes=["epsc"])
    P.op("pool", lambda e: e.memset(trimask[:], 1.0), writes=["trimask"])
    P.op("pool", lambda e: e.affine_select(out=trimask[:], in_=trimask[:], compare_op=ALU.is_ge, fill=0.0, base=0,
                                           pattern=[[1, 128]], channel_multiplier=-1), reads=["trimask"], writes=["trimask"])
    P.op("pool", lambda e: e.memset(negmask[:], 0.0), writes=["negmask"])
    P.op("pool", lambda e: e.affine_select(out=negmask[:], in_=negmask[:], compare_op=ALU.is_ge, fill=MASKNEG, base=0,
                                           pattern=[[1, 128]], channel_multiplier=-1), reads=["negmask"], writes=["negmask"])
    P.op("pool", lambda e: e.iota(out=kki[:], pattern=[[0, 1]], base=0, channel_multiplier=1), writes=["kki"])
    P.op("pool", lambda e: e.tensor_copy(out=kkf[:], in_=kki[:]), reads=["kki"], writes=["kkf"])
    slopes = [2.0 ** (-(h + 1)) for h in range(8)]
    NQ = [128 if h == 0 else (256 if h == 1 else 512) for h in range(8)]
    for h in range(8):
        for mi in range(NMB):
            m = mi - 4
            P.op("dve", lambda e, h=h, mi=mi, m=m: e.tensor_scalar(
                out=btab[:, h * NMB + mi:h * NMB + mi + 1], in0=kkf[:], scalar1=slopes[h],
                scalar2=-slopes[h] * (128.0 * m + NQ[h] / 2.0), op0=ALU.mult, op1=ALU.add),
                reads=["kkf"], writes=[("btab", h, mi)])
    load_vec_T(fnw_d, 0, 8, fnw[:, 0:8], "fnw")
    for j in range(n_attn):
        li = lambda_init_for(2 * j)
        P.dma("sp", lamb[:], bass.AP(alam_d, j * 256, [[0, 128], [1, 256]]), writes=["lamb"])
        P.op("dve", lambda e: e.tensor_tensor(out=lamb[:, 0:64], in0=lamb[:, 0:64], in1=lamb[:, 64:128], op=ALU.mult),
             reads=["lamb"], writes=["lamb"])
        P.op("dve", lambda e: e.tensor_tensor(out=lamb[:, 128:192], in0=lamb[:, 128:192], in1=lamb[:, 192:256], op=ALU.mult),
             reads=["lamb"], writes=["lamb"])
        P.op("dve", lambda e: e.reduce_sum(out=lamt[:, 0:1], in_=lamb[:, 0:64], axis=AX.X), reads=["lamb"], writes=["lamt"])
        P.op("dve", lambda e: e.reduce_sum(out=lamt[:, 1:2], in_=lamb[:, 128:192], axis=AX.X), reads=["lamb"], writes=["lamt"])
        P.op("act", lambda e: e.activation(out=lamt[:, 2:4], in_=lamt[:, 0:2], func=AF.Exp), reads=["lamt"], writes=["lamt"])
        P.op("dve", lambda e: e.tensor_tensor(out=lamt[:, 4:5], in0=lamt[:, 3:4], in1=lamt[:, 2:3], op=ALU.subtract),
             reads=["lamt"], writes=["lamt"])
        P.op("dve", lambda e, j=j, li=li: e.tensor_scalar(out=neglam[:, j:j + 1], in0=lamt[:, 4:5], scalar1=-li, scalar2=None,
                                                          op0=ALU.add), reads=["lamt"], writes=[("neglam", j)])
    for j in range(n_gla):
        load_vec_T(ggb_d, j * 512, 4, negb[:, 4 * j:4 * j + 4], ("negb", j))
        P.op("dve", lambda e, j=j: e.tensor_scalar(out=negb[:, 4 * j:4 * j + 4], in0=negb[:, 4 * j:4 * j + 4], scalar1=-1.0,
                                                   scalar2=None, op0=ALU.mult), reads=[("negb", j)], writes=[("negb", j)])

    rsc = CA([128, 24 * depth], F32, "rsc")
    CONST_END = CA.off
    for i in range(depth):
        j = i // 2
        load_vec_T(mixw_d, i * D, 8, rsc[:, 24 * i:24 * i + 8], ("rsc", i, 0))
        load_vec_T(mlpw_d, i * D, 8, rsc[:, 24 * i + 16:24 * i + 24], ("rsc", i, 2))
        if i % 2 == 0:
            load_vec_T(asub_d, j * 128, 1, scl[:, 16:17], "scl")
            P.op("dve", lambda e, i=i: e.tensor_scalar(out=rsc[:, 24 * i + 8:24 * i + 16], in0=onesf[:, 0:8], scalar1=scl[:, 16:17],
                                                       scalar2=1.0 - lambda_init_for(i), op0=ALU.mult, op1=ALU.mult),
                 reads=["scl", "onesf"], writes=[("rsc", i, 1)])
        else:
            load_vec_T(gnw_d, j * 256, 2, scl[:, 16:18], "scl")
            for c in range(8):
                P.op("dve", lambda e, c=c, i=i: e.tensor_copy(out=rsc[:, 24 * i + 8 + c:24 * i + 9 + c], in_=scl[:, 16 + (c % 2):17 + (c % 2)]),
                     reads=["scl"], writes=[("rsc", i, 1)])

    PREP_BASE = ARENA_END - 32768 - 2048
    pst = [nc.alloc_sbuf_tensor_at(f"pst{k}", [128, 2048], F32, offset=PREP_BASE + k * 8192) for k in range(3)]
    pob = [nc.alloc_sbuf_tensor_at(f"pob{k}", [128, 2048], BF16, offset=PREP_BASE + 24576 + k * 4096) for k in range(2)]

    def matrix_tasks(W_ap, K, Nc, Wb, CW, sc0, tag):
        ts = []
        for kc in range(K // 128):
            for p0 in range(0, Nc, 2048):
                ts.append(("mat", W_ap, kc, p0, min(2048, Nc - p0), Wb, CW, sc0, tag))
        return ts

    def group_tasks(k):
        ts = []
        if k >= 1:
            i = k - 1
            j = i // 2
            wo = awout_d.ap()[j] if i % 2 == 0 else gwout_d.ap()[j]
            ts += matrix_tasks(wo, D, D, Wout_b[i], 512, 24 * i + 8, ("out", i))
            ts += matrix_tasks(wup_d.ap()[i], D, DFF, Wup_b[i], 512, 24 * i + 16, ("up", i))
            ts += matrix_tasks(wdn_d.ap()[i], DFF, D, Wdn_b[i], 128, None, ("dn", i))
        if k < depth:
            i = k
            j = i // 2
            if i % 2 == 0:
                ts += matrix_tasks(awin_d.ap()[j], D, 3 * D, Win_b[i], 512, 24 * i, ("in", i))
            else:
                ts += matrix_tasks(gwin_d.ap()[j][:, 0:3072], D, 3072, Win_b[i], 512, 24 * i, ("in", i))
                ts.append(("gz", i, j))
                ts.append(("gu", i, j))
        return ts

    def t_load(t, n):
        b = n % 3
        if t[0] == "mat":
            _, W_ap, kc, p0, pn, Wb, CW, sc0, tag = t
            P.dma("sp", pst[b][:, :pn], W_ap[kc * 128:(kc + 1) * 128, p0:p0 + pn], writes=[("pst", b)])
        elif t[0] == "gz":
            _, i, j = t
            for kc in range(8):
                P.dma("sp", pst[b][:, kc * 16:(kc + 1) * 16], gwin_d.ap()[j][kc * 128:(kc + 1) * 128, 3072:3088], writes=[("pst", b, kc), ("pst", b)])
        else:
            _, i, j = t
            P.dma("sp", pst[b][0:16, 0:512], ggu_d.ap()[j], writes=[("pst", b)])

    def t_conv(t, n, eng):
        b = n % 3
        o = n % 2
        if t[0] == "mat":
            _, W_ap, kc, p0, pn, Wb, CW, sc0, tag = t
            rs = [("pst", b)]
            if sc0 is None:
                if eng == "act":
                    P.op("act", lambda e: e.activation(out=pob[o][:, :pn], in_=pst[b][:, :pn], func=AF.Copy), reads=rs, writes=[("pob", o)])
                else:
                    P.op(eng, lambda e: e.tensor_copy(out=pob[o][:, :pn], in_=pst[b][:, :pn]), reads=rs, writes=[("pob", o)])
            else:
                sc = rsc[:, sc0 + kc:sc0 + kc + 1]
                if eng == "act":
                    P.op("act", lambda e: e.activation(out=pob[o][:, :pn], in_=pst[b][:, :pn], func=AF.Copy, scale=sc), reads=rs, writes=[("pob", o)])
                else:
                    P.op(eng, lambda e: e.tensor_scalar(out=pob[o][:, :pn], in0=pst[b][:, :pn], scalar1=sc, scalar2=0.0, op0=ALU.mult, op1=ALU.add),
                         reads=rs, writes=[("pob", o)])
        elif t[0] == "gz":
            _, i, j = t
            for kc in range(8):
                P.op("dve", lambda e, kc=kc: e.tensor_scalar(out=pob[o][:, kc * 16:(kc + 1) * 16], in0=pst[b][:, kc * 16:(kc + 1) * 16],
                                                             scalar1=rsc[:, 24 * i + kc:24 * i + kc + 1], scalar2=None, op0=ALU.mult),
                     reads=[("pst", b, kc), ("pst", b)], writes=[("pob", o)])
        else:
            P.op("dve", lambda e: e.tensor_copy(out=pob[o][0:16, 0:512], in_=pst[b][0:16, 0:512]), reads=[("pst", b)], writes=[("pob", o)])

    def t_store(t, n):
        o = n % 2
        if t[0] == "mat":
            _, W_ap, kc, p0, pn, Wb, CW, sc0, tag = t
            for s in range(p0 // CW, (p0 + pn) // CW):
                P.dma("sp", Wb.ap()[s, :, kc * CW:(kc + 1) * CW], pob[o][:, s * CW - p0:(s + 1) * CW - p0], reads=[("pob", o)],
                      writes=[("wb", tag, s, kc)])
        elif t[0] == "gz":
            _, i, j = t
            P.dma("sp", Wgz_b[i].ap()[:, :], pob[o][:, 0:128], reads=[("pob", o)], writes=[("wgz", i)])
        else:
            _, i, j = t
            P.dma("sp", Wgu_b[i].ap()[:, :], pob[o][0:16, 0:512], reads=[("pob", o)], writes=[("wgu", i)])

    def prep_gen(tasks, engs):
        n = len(tasks)
        for s in range(n + 2):
            if s < n:
                t_load(tasks[s], s)
            if 0 <= s - 1 < n:
                t_conv(tasks[s - 1], s - 1, engs[(s - 1) % len(engs)])
            if 0 <= s - 2 < n:
                t_store(tasks[s - 2], s - 2)
            yield

    class BG:
        gen = None
        cnt = 0

        def step(self, every=1):
            if self.gen is None:
                return
            self.cnt += 1
            if self.cnt % every:
                return
            try:
                next(self.gen)
            except StopIteration:
                self.gen = None

        def drain(self):
            while self.gen is not None:
                self.step()

    bg = BG()
    bg.gen = prep_gen(group_tasks(0), ["dve", "act"])
    bg.drain()
    P.barrier()

    def t_phase(li):
        A = Arena(CONST_END)
        NB = 512
        hT = [A([128, 8, NB], F32, "hT") for _ in range(2)]
        xreg_off = A.off
        A([128, 4096], F32, "xreg")
        xst = [nc.alloc_sbuf_tensor_at(f"xst{li}_{k}", [128, 1024], F32, offset=xreg_off + k * 4096) for k in range(4)]
        onTs = [nc.alloc_sbuf_tensor_at(f"onTs{li}_{k}", [128, 8, NB], BF16, offset=xreg_off + k * 8192) for k in range(2)]
        r1_o = A([128, 8, NB], F32, "r1")
        off_r1 = A.off - 8 * NB * 4
        sqb8 = nc.alloc_sbuf_tensor_at(f"sqb8_{li}", [128, 8, NB], BF16, offset=off_r1)
        hnT = nc.alloc_sbuf_tensor_at(f"hnT_{li}", [128, 8, NB], BF16, offset=off_r1 + 8 * NB * 2)
        rt = A([128, NB], F32, "rt")
        rinv = A([128, NB], F32, "rinv")
        tmpf = [A([128, NB], F32, "tmpf") for _ in range(2)]
        hid = A([128, 32, NB], BF16, "hid")
        stg_off = A.off
        stg = [A([128, 4, NB], F32, "stg") for _ in range(2)]
        ost = [nc.alloc_sbuf_tensor_at(f"ost{li}_{k}", [128, 1024], F32, offset=stg_off + k * 8192) for k in range(2)]
        vst = A([128, 4, 1024], BF16, "vst")
        gsps = A([128, 4, NB], F32, "gsps")
        gzb = A([16, NB], BF16, "gzb")
        wgz = A([128, 128], BF16, "wgz")
        wgu = A([16, 512], BF16, "wgu")
        wsl = [A([128, 4096], BF16, "wsl") for _ in range(4)]
        stgB = [A([128, 8, NB], BF16, "stgB") for _ in range(2)]

        fin_layer = li - 1
        do_fin = li > 0
        do_in = li < depth
        is_attn_in = do_in and (li % 2 == 0)
        nb = len(blocks)

        seq_all = []
        if do_fin:
            seq_all += [("out", 0), ("out", 1)]
        for b_ in range(nb):
            if do_fin:
                if b_ + 1 < nb:
                    seq_all.append(("out", 0))
                seq_all += [("up", s) for s in range(8)] + [("dn", s) for s in range(8)]
                if b_ + 1 < nb:
                    seq_all.append(("out", 1))
            if do_in:
                seq_all += [("in", s) for s in range(6)]
        total = len(seq_all)
        issued = [0]
        gctr = [0]

        def w_issue(g):
            kind, s = seq_all[g]
            if kind == "out":
                src_ = Wout_b[fin_layer].ap()[s]
            elif kind == "up":
                src_ = Wup_b[fin_layer].ap()[s]
            elif kind == "dn":
                src_ = Wdn_b[fin_layer].ap()[s]
            else:
                src_ = Win_b[li].ap()[s]
            P.dma("sp", wsl[g % 4][:, :], src_, writes=[("wsl", g % 4)])

        def w_next(kind):
            g = gctr[0]
            gctr[0] += 1
            assert seq_all[g][0] == kind, (g, seq_all[g], kind)
            while issued[0] < min(total, g + 3):
                w_issue(issued[0])
                issued[0] += 1
            return wsl[g % 4], ("wsl", g % 4)

        acc = [0]

        def next_acc():
            b = acc[0] % 4
            acc[0] += 1
            return b

        if do_in and not is_attn_in:
            P.dma("sp", wgz[:], Wgz_b[li].ap()[:, :], writes=["wgz"])
            P.dma("sp", wgu[:], Wgu_b[li].ap()[:, :], writes=["wgu"])

        def norm(hb, hkey, N, want_hn=True):
            for hh in range(2):
                P.op("act", lambda e, hh=hh: e.activation(out=sqb8[:, 4 * hh:4 * hh + 4, :N], in_=hb[:, 4 * hh:4 * hh + 4, :N], func=AF.Square),
                     reads=[hkey], writes=[("sq8", hh)])
            for c in range(8):
                P.op("pe", lambda e, c=c: e.matmul(out=banks[4][:, :N], lhsT=onesb[:], rhs=sqb8[:, c, :N], start=(c == 0), stop=(c == 7)),
                     reads=[("sq8", c // 4), "onesb"], writes=[("ps", 4)])
            P.op("act", lambda e: e.activation(out=rt[:, :N], in_=banks[4][:, :N], func=AF.Ln, bias=epsc[:], scale=1.0 / D),
                 reads=[("ps", 4), "epsc"], writes=["rt"])
            P.op("act", lambda e: e.activation(out=rinv[:, :N], in_=rt[:, :N], func=AF.Exp, scale=-0.5), reads=["rt"], writes=["rinv"])
            if want_hn:
                for c in range(8):
                    eng = "dve"
                    P.op(eng, lambda e, c=c: e.tensor_tensor(out=hnT[:, c, :N], in0=hb[:, c, :N], in1=rinv[:, :N], op=ALU.mult),
                         reads=[hkey, "rinv"], writes=[("hnT", c)])

        HN = [("hnT", c) for c in range(8)]

        def lin_fm(kind, oc_list, nkc, cw, rhs_of, rhs_keys, N, epi):
            w, wkey = w_next(kind)
            for j, oc in enumerate(oc_list):
                b = next_acc()
                for kc in range(nkc):
                    P.op("pe", lambda e, b=b, kc=kc, j=j: e.matmul(out=banks[b][:, :N], lhsT=w[:, kc * cw + j * 128:kc * cw + (j + 1) * 128],
                                                                   rhs=rhs_of(kc), start=(kc == 0), stop=(kc == nkc - 1)),
                         reads=[wkey] + rhs_keys, writes=[("ps", b)])
                epi(oc, b)

        def blk(bi):
            t0, nt = blocks[bi]
            return t0, nt, nt * 128, t0 * 128, hT[bi % 2], ("hT", bi % 2)

        def load_block(bi):
            t0, nt, N, c0, hb, hkey = blk(bi)
            if li == 0:
                for tt in range(nt):
                    tile_i = t0 + tt
                    xs = xst[tt % 4]
                    xkey = ("R2", tt % 4)
                    g0 = tile_i * 128
                    if tile_i == 0:
                        P.dma("sp", xs[0:NMETA, :], meta_d.ap()[:, :], writes=[xkey])
                        P.dma("sp", xs[NMETA:128, :], x_d.ap()[0:128 - NMETA, :], writes=[xkey])
                    else:
                        nv = min(128, NREAL - g0)
                        if nv < 128:
                            P.op("dve", lambda e, xs=xs: e.memset(xs[:], 0.0), writes=[xkey])
                        P.dma("sp", xs[0:nv, :], x_d.ap()[g0 - NMETA:g0 - NMETA + nv, :], writes=[xkey])
                for tt in range(nt):
                    xs = xst[tt % 4]
                    xkey = ("R2", tt % 4)
                    for half in range(2):
                        for c4 in range(4):
                            c = half * 4 + c4
                            P.op("pe", lambda e, xs=xs, c=c, c4=c4, half=half: e.transpose(
                                out=banks[6 + half][:, c4 * 128:(c4 + 1) * 128], in_=xs[:, c * 128:(c + 1) * 128], identity=identf[:]),
                                reads=[xkey, "identf"], writes=[("ps", 6 + half)])
                        src_ = banks[6 + half][:, :].rearrange("p (c t) -> p c t", t=128)
                        dst = hb[:, half * 4:half * 4 + 4, tt * 128:(tt + 1) * 128]
                        if half == 0:
                            P.op("act", lambda e, src_=src_, dst=dst: e.activation(out=dst, in_=src_, func=AF.Copy),
                                 reads=[("ps", 6 + half)], writes=[hkey])
                        else:
                            P.op("dve", lambda e, src_=src_, dst=dst: e.tensor_copy(out=dst, in_=src_),
                                 reads=[("ps", 6 + half)], writes=[hkey])
            else:
                P.dma("sp", hb[:, :, :N], hT_d.ap()[:, c0:c0 + N].rearrange("(c p) n -> p c n", p=128), writes=[hkey])
                P.dma("sp", onTs[bi % 2][:, :, :N], onT_d.ap()[:, c0:c0 + N].rearrange("(c p) n -> p c n", p=128), writes=[("onTs", bi % 2)])

        def make_epi_res(hb, hkey, N):
            def epi_res(oc, b):
                P.op("dve", lambda e, oc=oc, b=b: e.tensor_tensor(out=hb[:, oc, :N], in0=hb[:, oc, :N], in1=banks[b][:, :N], op=ALU.add),
                     reads=[hkey, ("ps", b)], writes=[hkey])
            return epi_res

        def stageA_half(bi, s):
            t0, nt, N, c0, hb, hkey = blk(bi)
            on = onTs[bi % 2]
            lin_fm("out", [4 * s + q for q in range(4)], 8, 512, lambda kc: on[:, kc, :N], [("onTs", bi % 2)], N, make_epi_res(hb, hkey, N))

        def do_block(bi):
            t0, nt, N, c0, hb, hkey = blk(bi)
            if bi == 0:
                load_block(0)
                if do_fin:
                    stageA_half(0, 0)
                    stageA_half(0, 1)
            if bi + 1 < nb:
                load_block(bi + 1)
            epi_res = make_epi_res(hb, hkey, N)
            if do_fin:
                norm(hb, hkey, N)
                if bi + 1 < nb:
                    stageA_half(bi + 1, 0)
                rl_i = [0]

                def epi_up(oc, b):
                    k = rl_i[0] % 2
                    rl_i[0] += 1
                    tf = tmpf[k]
                    P.op("act", lambda e, b=b, tf=tf: e.activation(out=tf[:, :N], in_=banks[b][:, :N], func=AF.Relu), reads=[("ps", b)],
                         writes=[("tmpf", k)])
                    P.op("dve", lambda e, oc=oc, tf=tf: e.tensor_tensor(out=hid[:, oc, :N], in0=tf[:, :N], in1=tf[:, :N], op=ALU.mult),
                         reads=[("tmpf", k)], writes=["hid"])

                for s in range(8):
                    lin_fm("up", [4 * s + q for q in range(4)], 8, 512, lambda kc: hnT[:, kc, :N], HN, N, epi_up)
                for s in range(8):
                    lin_fm("dn", [s], 32, 128, lambda kc: hid[:, kc, :N], ["hid"], N, epi_res)

            if do_in:
                norm(hb, hkey, N)
                if do_fin and bi + 1 < nb:
                    stageA_half(bi + 1, 1)
                if is_attn_in:
                    for which, dst_d in ((0, qT_d), (1, kT_d)):
                        sb = stgB[which]
                        skey = ("stgB", which)

                        def epi_cp(oc, b, sb=sb, skey=skey):
                            P.op("act", lambda e, oc=oc, b=b: e.activation(out=sb[:, oc, :N], in_=banks[b][:, :N], func=AF.Copy),
                                 reads=[("ps", b)], writes=[skey])

                        for s in range(2):
                            lin_fm("in", [4 * s + q for q in range(4)], 8, 512, lambda kc: hnT[:, kc, :N], HN, N, epi_cp)
                        P.dma("sp", dst_d.ap()[:, c0:c0 + N].rearrange("(c p) n -> p c n", p=128), sb[:, :, :N], reads=[skey],
                              writes=[("dqk", which, bi)])
                else:
                    sq_ = stg[0]
                    sk_ = stg[1]

                    def epi_q(oc, b):
                        if oc < 4:
                            P.op("act", lambda e, oc=oc, b=b: e.activation(out=sq_[:, oc, :N], in_=banks[b][:, :N], func=AF.Copy, scale=128.0 ** -0.5),
                                 reads=[("ps", b)], writes=[("stg", 0)])
                        else:
                            P.op("dve", lambda e, oc=oc, b=b: e.tensor_copy(out=sk_[:, oc - 4, :N], in_=banks[b][:, :N]),
                                 reads=[("ps", b)], writes=[("stg", 1)])

                    for s in range(2):
                        lin_fm("in", [4 * s + q for q in range(4)], 8, 512, lambda kc: hnT[:, kc, :N], HN, N, epi_q)
                    P.dma("sp", gq_d.ap()[:, c0:c0 + N].rearrange("(c p) n -> p c n", p=128), sq_[:, :, :N], reads=[("stg", 0)],
                          writes=[("dgq", bi)])
                    P.dma("sp", gk_d.ap()[:, c0:c0 + N].rearrange("(c p) n -> p c n", p=128), sk_[:, :, :N], reads=[("stg", 1)],
                          writes=[("dgk", bi)])
                for half in range(2):
                    w, wkey = w_next("in")
                    for tt in range(nt):
                        b = next_acc()
                        for kc in range(8):
                            P.op("pe", lambda e, b=b, kc=kc, tt=tt, w=w: e.matmul(out=banks[b][:, :], lhsT=hnT[:, kc, tt * 128:(tt + 1) * 128],
                                                                                  rhs=w[:, kc * 512:(kc + 1) * 512], start=(kc == 0), stop=(kc == 7)),
                                 reads=[wkey] + HN, writes=[("ps", b)])
                        if tt % 2 == 0:
                            P.op("act", lambda e, b=b, tt=tt, half=half: e.activation(out=vst[:, tt, half * 512:(half + 1) * 512], in_=banks[b][:, :], func=AF.Copy),
                                 reads=[("ps", b)], writes=["vst"])
                        else:
                            P.op("dve", lambda e, b=b, tt=tt, half=half: e.tensor_copy(out=vst[:, tt, half * 512:(half + 1) * 512], in_=banks[b][:, :]),
                                 reads=[("ps", b)], writes=["vst"])
                P.dma("sp", v_d.ap()[c0:c0 + N, :].rearrange("(t p) n -> p t n", p=128), vst[:, :nt, :], reads=["vst"], writes=[("dv", bi)])
                if not is_attn_in:
                    sb = stgB[0]
                    skey = ("stgB", 0)

                    def epi_g(oc, b):
                        P.op("act", lambda e, oc=oc, b=b: e.activation(out=sb[:, oc, :N], in_=banks[b][:, :N], func=AF.Silu), reads=[("ps", b)],
                             writes=[skey])

                    for s in range(2):
                        lin_fm("in", [4 * s + q for q in range(4)], 8, 512, lambda kc: hnT[:, kc, :N], HN, N, epi_g)
                    P.dma("sp", sg_d.ap()[:, c0:c0 + N].rearrange("(c p) n -> p c n", p=128), sb[:, :, :N], reads=[skey], writes=[("dsg", bi)])
                    b = next_acc()
                    for kc in range(8):
                        P.op("pe", lambda e, b=b, kc=kc: e.matmul(out=banks[b][0:16, :N], lhsT=wgz[:, kc * 16:(kc + 1) * 16], rhs=hnT[:, kc, :N],
                                                                  start=(kc == 0), stop=(kc == 7)), reads=["wgz"] + HN, writes=[("ps", b)])
                    P.op("act", lambda e, b=b: e.activation(out=gzb[:, :N], in_=banks[b][0:16, :N], func=AF.Copy), reads=[("ps", b)], writes=["gzb"])
                    jg = li // 2
                    for oc in range(4):
                        b = next_acc()
                        P.op("pe", lambda e, b=b, oc=oc: e.matmul(out=banks[b][:, :N], lhsT=wgu[0:16, oc * 128:(oc + 1) * 128], rhs=gzb[0:16, :N],
                                                                  start=True, stop=True), reads=["wgu", "gzb"], writes=[("ps", b)])
                        tf = tmpf[oc % 2]
                        P.op("act", lambda e, b=b, oc=oc, tf=tf: e.activation(out=tf[:, :N], in_=banks[b][:, :N], func=AF.Exp, scale=-1.0,
                                                                              bias=negb[:, 4 * jg + oc:4 * jg + oc + 1]),
                             reads=[("ps", b), ("negb", jg)], writes=[("tmpf", oc % 2)])
                        P.op("act", lambda e, oc=oc, tf=tf: e.activation(out=gsps[:, oc, :N], in_=tf[:, :N], func=AF.Ln, bias=1.0),
                             reads=[("tmpf", oc % 2)], writes=["gsps"])
                    P.dma("sp", gsp_d.ap()[:, c0:c0 + N].rearrange("(c p) n -> p c n", p=128), gsps[:, :, :N], reads=["gsps"], writes=[("dgsp", bi)])
                P.dma("sp", hT_d.ap()[:, c0:c0 + N].rearrange("(c p) n -> p c n", p=128), hb[:, :, :N], reads=[hkey], writes=[("dhT", bi)])
            else:
                norm(hb, hkey, N, want_hn=False)
                if bi + 1 < nb:
                    stageA_half(bi + 1, 1)
                yT = r1_o
                for c in range(8):
                    P.op("dve", lambda e, c=c: e.scalar_tensor_tensor(out=yT[:, c, :N], in0=hb[:, c, :N], scalar=fnw[:, c:c + 1], in1=rinv[:, :N],
                                                                      op0=ALU.mult, op1=ALU.mult),
                         reads=[hkey, "rinv", "fnw"], writes=[("sq8", 0), ("sq8", 1)] + HN)
                for tt in range(nt):
                    tile_i = t0 + tt
                    g0 = tile_i * 128
                    lo = max(g0, NMETA)
                    hi = min(g0 + 128, NREAL)
                    if hi <= lo:
                        continue
                    os_ = ost[tt % 2]
                    okey = ("stg", tt % 2)
                    for half in range(2):
                        for c4 in range(4):
                            c = half * 4 + c4
                            P.op("pe", lambda e, c=c, c4=c4, half=half, tt=tt: e.transpose(
                                out=banks[6 + half][:, c4 * 128:(c4 + 1) * 128], in_=yT[:, c, tt * 128:(tt + 1) * 128], identity=identf[:]),
                                reads=HN + [("sq8", 0), ("sq8", 1), "identf"], writes=[("ps", 6 + half)])
                        if half == 0:
                            P.op("act", lambda e, os_=os_, half=half: e.activation(out=os_[:, half * 512:(half + 1) * 512], in_=banks[6 + half][:, :], func=AF.Copy),
                                 reads=[("ps", 6 + half)], writes=[okey])
                        else:
                            P.op("dve", lambda e, os_=os_, half=half: e.tensor_copy(out=os_[:, half * 512:(half + 1) * 512], in_=banks[6 + half][:, :]),
                                 reads=[("ps", 6 + half)], writes=[okey])
                    P.dma("sp", out_d.ap()[lo - NMETA:hi - NMETA, :], os_[lo - g0:hi - g0, :], reads=[okey], writes=[("dout", tile_i)])

        for bi in range(nb):
            do_block(bi)
        assert gctr[0] == total, (gctr[0], total)
        P.barrier()

    def attn_core(li):
        ja = li // 2
        ts = []
        for k in (li + 1, li + 2):
            if k <= depth:
                ts += group_tasks(k)
        bg.gen = prep_gen(ts, ["pool"])
        bg.cnt = 0
        A = Arena(CONST_END)
        Vall = A([128, NT, 1024], BF16, "Vall")
        KT = [A([128, LP], BF16, "KT") for _ in range(2)]
        QT = [A([128, LP], BF16, "QT") for _ in range(2)]
        Et = [[A([128, 512], BF16, "Et") for _ in range(3)] for _ in range(2)]
        fbs = [[A([128, 512], F32, "fb") for _ in range(7)] for _ in range(3)]
        pend2 = []
        sqos = [A([128, 512], BF16, "sqo") for _ in range(3)]
        ons = [A([128, 512], BF16, "ons") for _ in range(3)]
        P.dma("sp", Vall[:, :, :], v_d.ap()[:, :].rearrange("(t p) n -> p t n", p=128), writes=["Vall"])
        sb_i = [0]
        e_i = [0]
        qb_i = [0]
        def do_head(h):
            kt = KT[h % 2]
            qt = QT[h % 2]
            kkey = ("KT", h % 2)
            qkey = ("QT", h % 2)
            P.dma("sp", kt[:, :], kT_d.ap()[h * 128:(h + 1) * 128, :], writes=[kkey])
            P.dma("sp", qt[:, :], qT_d.ap()[h * 128:(h + 1) * 128, :], writes=[qkey])
            nq = NQ[h]
            def do_qblock(q0):
                N = min(nq, LP - q0)
                kbs = []
                for kb in range((q0 + N) // 128):
                    k0 = kb * 128
                    gap = q0 - (k0 + 127)
                    if gap > 0 and slopes[h] * gap > ATT_SKIP:
                        continue
                    kbs.append(kb)
                nk = len(kbs)

                def issue_qk(idx):
                    kb = kbs[idx]
                    k0 = kb * 128
                    m = (q0 - k0) // 128
                    res = []
                    for mp in range(2):
                        b = (sb_i[0] % 2) * 2 + mp
                        lo, hi = mp * 64, (mp + 1) * 64
                        if k0 < q0:
                            cs = 0
                            P.op("pe", lambda e, b=b, lo=lo, hi=hi, k0=k0: e.matmul(out=banks[b][:, 0:N], lhsT=kt[lo:hi, k0:k0 + 128], rhs=qt[lo:hi, q0:q0 + N],
                                                                                   start=True, stop=True), reads=[kkey, qkey], writes=[("ps", b)])
                        else:
                            cs = k0 - q0
                            P.op("pe", lambda e, b=b, lo=lo, hi=hi, k0=k0, cs=cs: e.matmul(out=banks[b][:, cs:cs + 128], lhsT=kt[lo:hi, k0:k0 + 128],
                                                                                          rhs=qt[lo:hi, k0:k0 + 128], start=True, stop=False),
                                 reads=[kkey, qkey], writes=[("ps", b)])
                            P.op("pe", lambda e, b=b, cs=cs: e.matmul(out=banks[b][:, cs:cs + 128], lhsT=identb[:], rhs=negmask[:], start=False, stop=True),
                                 reads=["identb", "negmask"], writes=[("ps", b)])
                            if cs + 128 < N:
                                P.op("pe", lambda e, b=b, lo=lo, hi=hi, k0=k0, cs=cs: e.matmul(out=banks[b][:, cs + 128:N], lhsT=kt[lo:hi, k0:k0 + 128],
                                                                                              rhs=qt[lo:hi, q0 + cs + 128:q0 + N], start=True, stop=True),
                                     reads=[kkey, qkey], writes=[("ps", b)])
                        res.append((b, cs))
                    sb_i[0] += 1
                    return res, m

                def issue_exp_pv(idx, res, m):
                    kb = kbs[idx]
                    first = idx == 0
                    last = idx == nk - 1
                    ei = e_i[0] % 3
                    e_i[0] += 1
                    for mp in range(2):
                        b, cs = res[mp]
                        et = Et[mp][ei]
                        ekey = ("Et", mp, ei)
                        col = h * NMB + (m + 4)
                        P.op("act", lambda e, b=b, cs=cs, et=et, col=col: e.activation(out=et[:, cs:N], in_=banks[b][:, cs:N], func=AF.Exp,
                                                                                      bias=btab[:, col:col + 1], scale=0.125),
                             reads=[("ps", b), ("btab", h, m + 4)], writes=[ekey])
                    for mp in range(2):
                        b, cs = res[mp]
                        et = Et[mp][ei]
                        ekey = ("Et", mp, ei)
                        P.op("pe", lambda e, mp=mp, cs=cs, et=et, kb=kb: e.matmul(out=banks[4 + mp][:, cs:N], lhsT=Vall[:, kb, h * 128:(h + 1) * 128],
                                                                                 rhs=et[:, cs:N], start=first, stop=last),
                             reads=["Vall", ekey], writes=[("ps", 4 + mp)])
                        P.op("pe", lambda e, mp=mp, cs=cs, et=et: e.matmul(out=banks[6 + mp][:, cs:N], lhsT=onesb[:], rhs=et[:, cs:N], start=first, stop=last),
                             reads=["onesb", ekey], writes=[("ps", 6 + mp)])

                pend = issue_qk(0)
                for idx in range(nk):
                    nxt = issue_qk(idx + 1) if idx + 1 < nk else None
                    issue_exp_pv(idx, pend[0], pend[1])
                    pend = nxt
                    bg.step(every=3)
                fsel = qb_i[0] % 3
                qb_i[0] += 1
                fb = fbs[fsel]
                P.op("dve", lambda e: e.tensor_copy(out=fb[0][:, :N], in_=banks[6][:, :N]), reads=[("ps", 6)], writes=[("fb", fsel, 0)])
                P.op("dve", lambda e: e.tensor_copy(out=fb[1][:, :N], in_=banks[7][:, :N]), reads=[("ps", 7)], writes=[("fb", fsel, 1)])
                P.op("dve", lambda e: e.tensor_copy(out=fb[2][:, :N], in_=banks[4][:, :N]), reads=[("ps", 4)], writes=[("fb", fsel, 2)])
                P.op("dve", lambda e: e.tensor_copy(out=fb[3][:, :N], in_=banks[5][:, :N]), reads=[("ps", 5)], writes=[("fb", fsel, 3)])
                P.op("dve", lambda e: e.reciprocal(out=fb[0][:, :N], in_=fb[0][:, :N]), reads=[("fb", fsel, 0)], writes=[("fb", fsel, 0)])
                P.op("dve", lambda e: e.reciprocal(out=fb[1][:, :N], in_=fb[1][:, :N]), reads=[("fb", fsel, 1)], writes=[("fb", fsel, 1)])
                P.op("pool", lambda e: e.tensor_tensor(out=fb[2][:, :N], in0=fb[2][:, :N], in1=fb[0][:, :N], op=ALU.mult),
                     reads=[("fb", fsel, 2), ("fb", fsel, 0)], writes=[("fb", fsel, 2)])
                P.op("pool", lambda e: e.tensor_tensor(out=fb[3][:, :N], in0=fb[3][:, :N], in1=fb[1][:, :N], op=ALU.mult),
                     reads=[("fb", fsel, 3), ("fb", fsel, 1)], writes=[("fb", fsel, 3)])
                P.op("dve", lambda e: e.scalar_tensor_tensor(out=fb[4][:, :N], in0=fb[3][:, :N], scalar=neglam[:, ja:ja + 1], in1=fb[2][:, :N],
                                                             op0=ALU.mult, op1=ALU.add), reads=[("fb", fsel, 2), ("fb", fsel, 3), ("neglam", ja)], writes=[("fb", fsel, 4)])
                P.op("act", lambda e: e.activation(out=sqos[fsel][:, :N], in_=fb[4][:, :N], func=AF.Square), reads=[("fb", fsel, 4)], writes=[("sqo", fsel)])

                def part2():
                    bss = (sb_i[0] % 2) * 2
                    sb_i[0] += 1
                    P.op("pe", lambda e: e.matmul(out=banks[bss][:, :N], lhsT=onesb[:], rhs=sqos[fsel][:, :N], start=True, stop=True),
                         reads=["onesb", ("sqo", fsel)], writes=[("ps", bss)])
                    P.op("act", lambda e: e.activation(out=fb[5][:, :N], in_=banks[bss][:, :N], func=AF.Ln, bias=epsc[:], scale=1.0 / 128),
                         reads=[("ps", bss), "epsc"], writes=[("fb", fsel, 5)])
                    P.op("act", lambda e: e.activation(out=fb[6][:, :N], in_=fb[5][:, :N], func=AF.Exp, scale=-0.5), reads=[("fb", fsel, 5)], writes=[("fb", fsel, 6)])
                    on = ons[fsel]
                    P.op("pool", lambda e: e.tensor_tensor(out=on[:, :N], in0=fb[4][:, :N], in1=fb[6][:, :N], op=ALU.mult),
                         reads=[("fb", fsel, 4), ("fb", fsel, 6)], writes=[("ons", fsel)])
                    P.dma("sp", onT_d.ap()[h * 128:(h + 1) * 128, q0:q0 + N], on[:, :N], reads=[("ons", fsel)], writes=[("donT", h, q0)])

                pend2.append(part2)
                if len(pend2) > 2:
                    pend2.pop(0)()

            for q0 in range(0, LP, nq):
                do_qblock(q0)

        for h in range(8):
            do_head(h)
        while pend2:
            pend2.pop(0)()
        bg.drain()
        P.barrier()

    def gla_core(li):
        A = Arena(CONST_END)
        NB = 512
        gq = [A([128, 4, NB], F32, "gq") for _ in range(2)]
        gk = [A([128, 4, NB], F32, "gk") for _ in range(2)]
        gs = [A([128, 4, NB], F32, "gs") for _ in range(2)]
        vv = [A([128, 4, 1024], BF16, "vv") for _ in range(2)]
        ost = [A([128, 8, NB], F32, "ost") for _ in range(2)]
        S = A([128, 4, 256], F32, "S")
        Sb = A([128, 4, 256], BF16, "Sb")
        NS = 8
        bpos = [A([128, 128], F32, "bpos") for _ in range(NS)]
        Ep = [A([128, 128], F32, "Ep") for _ in range(NS)]
        Em = [A([128, 128], F32, "Em") for _ in range(NS)]
        qs = [A([128, 128], BF16, "qs") for _ in range(NS)]
        ks = [A([128, 128], BF16, "ks") for _ in range(NS)]
        ktl = [A([128, 128], BF16, "ktl") for _ in range(NS)]
        ktk = [A([128, 128], BF16, "ktk") for _ in range(NS)]
        Am = [A([128, 128], BF16, "Am") for _ in range(NS)]
        sgs = [A([128, 8, NB], BF16, "sgs") for _ in range(2)]
        onst = [A([128, 8, NB], BF16, "onst") for _ in range(2)]
        sqg = [A([128, 2, NB], BF16, "sqg") for _ in range(2)]
        grt = [A([128, NB], F32, "grt") for _ in range(2)]
        gri = [A([128, NB], F32, "gri") for _ in range(2)]
        gtf = [A([128, NB], F32, "gtf") for _ in range(2)]
        P.op("dve", lambda e: e.memset(S[:], 0.0), writes=[("S", 0), ("S", 1), ("S", 2), ("S", 3)])
        P.op("dve", lambda e: e.memset(Sb[:], 0.0), writes=[("Sb", 0), ("Sb", 1), ("Sb", 2), ("Sb", 3)])
        it = [0]

        def do_gblock(bi, t0, nt):
            N = nt * 128
            c0 = t0 * 128
            k2 = bi % 2
            P.dma("sp", gq[k2][:, :, :N], gq_d.ap()[:, c0:c0 + N].rearrange("(c p) n -> p c n", p=128), writes=[("gq", k2)])
            P.dma("sp", gk[k2][:, :, :N], gk_d.ap()[:, c0:c0 + N].rearrange("(c p) n -> p c n", p=128), writes=[("gk", k2)])
            P.dma("sp", gs[k2][:, :, :N], gsp_d.ap()[:, c0:c0 + N].rearrange("(c p) n -> p c n", p=128), writes=[("gs", k2)])
            P.dma("sp", vv[k2][:, :nt, :], v_d.ap()[c0:c0 + N, :].rearrange("(t p) n -> p t n", p=128), writes=[("vv", k2)])
            P.dma("sp", sgs[k2][:, :, :N], sg_d.ap()[:, c0:c0 + N].rearrange("(c p) n -> p c n", p=128), writes=[("sgs", k2)])

            def do_chunk(tt):
                cs = tt * 128
                par = it[0] % 2
                it[0] += 1
                ix = [par * 4 + h for h in range(4)]
                pA = [banks[h][:, 0:128] for h in range(4)]
                pO = [banks[h][:, 128:384] for h in range(4)]
                pD = [banks[4 + h][:, 0:256] for h in range(4)]
                pT = [banks[4 + h][:, 256:320].bitcast(BF16) for h in range(4)]
                for h in range(4):
                    i3 = ix[h]
                    P.op("dve", lambda e, i3=i3, h=h: e.tensor_tensor_scan(out=bpos[i3][:], data0=onesf[:], data1=gs[k2][:, h, cs:cs + 128], initial=0.0,
                                                                          op0=ALU.mult, op1=ALU.add), reads=[("gs", k2), "onesf"], writes=[("bpos", i3)])
                for h in range(4):
                    i3 = ix[h]
                    P.op("act", lambda e, i3=i3: e.activation(out=Ep[i3][:], in_=bpos[i3][:], func=AF.Exp, scale=-1.0 / 16), reads=[("bpos", i3)],
                         writes=[("Ep", i3)])
                    P.op("act", lambda e, i3=i3: e.activation(out=Em[i3][:], in_=bpos[i3][:], func=AF.Exp, scale=1.0 / 16), reads=[("bpos", i3)],
                         writes=[("Em", i3)])
                for h in range(4):
                    i3 = ix[h]
                    P.op("dve", lambda e, i3=i3, h=h: e.tensor_tensor(out=qs[i3][:], in0=gq[k2][:, h, cs:cs + 128], in1=Ep[i3][:], op=ALU.mult),
                         reads=[("gq", k2), ("Ep", i3)], writes=[("qs", i3)])
                    P.op("pool", lambda e, i3=i3, h=h: e.tensor_tensor(out=ks[i3][:], in0=gk[k2][:, h, cs:cs + 128], in1=Em[i3][:], op=ALU.mult),
                         reads=[("gk", k2), ("Em", i3)], writes=[("ks", i3)])
                    P.op("dve", lambda e, i3=i3, h=h: e.scalar_tensor_tensor(out=ktl[i3][:], in0=gk[k2][:, h, cs:cs + 128], scalar=Ep[i3][:, 127:128],
                                                                            in1=Em[i3][:], op0=ALU.mult, op1=ALU.mult),
                         reads=[("gk", k2), ("Ep", i3), ("Em", i3)], writes=[("ktl", i3)])
                for h in range(4):
                    i3 = ix[h]
                    P.op("pe", lambda e, i3=i3, h=h: e.matmul(out=pA[h], lhsT=ks[i3][:], rhs=qs[i3][:], start=True, stop=True),
                         reads=[("ks", i3), ("qs", i3)], writes=[("ps", h)])
                    P.op("pe", lambda e, i3=i3, h=h: e.transpose(out=pT[h], in_=ktl[i3][:], identity=identb[:]), reads=[("ktl", i3), "identb"],
                         writes=[("ps", 4 + h)])
                for h in range(4):
                    i3 = ix[h]
                    P.op("act", lambda e, i3=i3, h=h: e.activation(out=ktk[i3][:], in_=pT[h], func=AF.Copy), reads=[("ps", 4 + h)], writes=[("ktk", i3)])
                    P.op("dve", lambda e, i3=i3, h=h: e.tensor_tensor(out=Am[i3][:], in0=pA[h], in1=trimask[:], op=ALU.mult),
                         reads=[("ps", h), "trimask"], writes=[("Am", i3)])
                for h in range(4):
                    i3 = ix[h]
                    for ec in range(2):
                        P.op("pe", lambda e, i3=i3, ec=ec, h=h: e.matmul(out=banks[h][:, 128 + ec * 128:256 + ec * 128],
                                                                        lhsT=vv[k2][:, tt, h * 256 + ec * 128:h * 256 + (ec + 1) * 128], rhs=Am[i3][:],
                                                                        start=True, stop=False), reads=[("vv", k2), ("Am", i3)], writes=[("ps", h)])
                        P.op("pe", lambda e, i3=i3, ec=ec, h=h: e.matmul(out=banks[h][:, 128 + ec * 128:256 + ec * 128],
                                                                        lhsT=Sb[:, h, ec * 128:(ec + 1) * 128], rhs=qs[i3][:], start=False, stop=True),
                             reads=[("Sb", h), ("qs", i3)], writes=[("ps", h)])
                    P.op("pe", lambda e, i3=i3, h=h: e.matmul(out=pD[h], lhsT=ktk[i3][:], rhs=vv[k2][:, tt, h * 256:(h + 1) * 256],
                                                              start=True, stop=True), reads=[("ktk", i3), ("vv", k2)], writes=[("ps", 4 + h)])
                for h in range(4):
                    i3 = ix[h]
                    P.op("dve", lambda e, i3=i3, h=h: e.scalar_tensor_tensor(out=S[:, h, :], in0=S[:, h, :], scalar=Ep[i3][:, 127:128],
                                                                            in1=pD[h], op0=ALU.mult, op1=ALU.add),
                         reads=[("S", h), ("Ep", i3), ("ps", 4 + h)], writes=[("S", h)])
                    P.op("pool", lambda e, h=h: e.tensor_copy(out=Sb[:, h, :], in_=S[:, h, :]), reads=[("S", h)], writes=[("Sb", h)])
                    src_ = pO[h].rearrange("p (c t) -> p c t", t=128)
                    P.op("act", lambda e, src_=src_, h=h: e.activation(out=ost[k2][:, 2 * h:2 * h + 2, cs:cs + 128], in_=src_, func=AF.Copy),
                         reads=[("ps", h)], writes=[("ost", k2, h)])

            for tt in range(nt):
                do_chunk(tt)
            for h in range(4):
                p2 = h % 2
                P.op("act", lambda e, h=h, p2=p2: e.activation(out=sqg[p2][:, :, :N], in_=ost[k2][:, 2 * h:2 * h + 2, :N], func=AF.Square),
                     reads=[("ost", k2, h)], writes=[("sqg", p2)])
                for jj in range(2):
                    P.op("pe", lambda e, h=h, jj=jj, p2=p2: e.matmul(out=banks[4 + h][:, :N], lhsT=onesb[:], rhs=sqg[p2][:, jj, :N],
                                                                    start=(jj == 0), stop=(jj == 1)),
                         reads=[("sqg", p2), "onesb"], writes=[("ps", 4 + h)])
                P.op("act", lambda e, h=h, p2=p2: e.activation(out=grt[p2][:, :N], in_=banks[4 + h][:, :N], func=AF.Ln, bias=epsc[:], scale=1.0 / 256),
                     reads=[("ps", 4 + h), "epsc"], writes=[("grt", p2)])
                P.op("act", lambda e, p2=p2: e.activation(out=gri[p2][:, :N], in_=grt[p2][:, :N], func=AF.Exp, scale=-0.5), reads=[("grt", p2)], writes=[("gri", p2)])
                for jj in range(2):
                    P.op("dve", lambda e, h=h, jj=jj, p2=p2: e.tensor_tensor(out=gtf[jj][:, :N], in0=ost[k2][:, 2 * h + jj, :N], in1=gri[p2][:, :N], op=ALU.mult),
                         reads=[("ost", k2, h), ("gri", p2)], writes=[("gtf", jj)])
                    P.op("pool", lambda e, h=h, jj=jj: e.tensor_tensor(out=onst[k2][:, 2 * h + jj, :N], in0=gtf[jj][:, :N], in1=sgs[k2][:, 2 * h + jj, :N], op=ALU.mult),
                         reads=[("gtf", jj), ("sgs", k2)], writes=[("onst", k2, h)])
            P.dma("sp", onT_d.ap()[:, c0:c0 + N].rearrange("(c p) n -> p c n", p=128), onst[k2][:, :, :N],
                  reads=[("onst", k2, h) for h in range(4)], writes=[("donT", bi)])

        for bi, (t0, nt) in enumerate(blocks):
            do_gblock(bi, t0, nt)
        P.barrier()

    for li in range(depth + 1):
        t_phase(li)
        if li < depth:
            if li % 2 == 0:
                attn_core(li)
            else:
                gla_core(li)
    P.emit()
    return nc


_CACHE = {}
_NAMES = ["meta_tokens", "mix_norm_w", "attn_w_in", "attn_lambda", "attn_subln_w", "attn_w_out", "gla_w_in", "gla_w_gate_up",
          "gla_gate_bias", "gla_norm_w", "gla_w_out", "mlp_norm_w", "mlp_w_up", "mlp_w_down", "final_norm_w"]


def run(inputs, depth=4, dbg=False):
    x = np.ascontiguousarray(np.asarray(inputs["x"], dtype=np.float32))
    B, SEQ, _ = x.shape
    key = (SEQ, depth, dbg)
    if key not in _CACHE:
        _CACHE[key] = build(SEQ, depth, dbg)
    nc = _CACHE[key]
    shared = {n: np.ascontiguousarray(np.asarray(inputs[n], dtype=np.float32)) for n in _NAMES}
    in_maps = []
    for b in range(B):
        m = dict(shared)
        m["x"] = x[b]
        in_maps.append(m)
    res = run_bass_kernel_spmd(nc, in_maps, core_ids=list(range(B)))
    return res


def kernel(**inputs):
    res = run(inputs, depth=4, dbg=False)
    B = np.asarray(inputs["x"]).shape[0]
    return np.stack([np.asarray(res.results[b]["out"], dtype=np.float32) for b in range(B)], axis=0)
```

```python
import math
import numpy as np
import concourse.bass as bass
import concourse.mybir as mybir
from concourse.bass_utils import run_bass_kernel_spmd

F32 = mybir.dt.float32
BF16 = mybir.dt.bfloat16
I32 = mybir.dt.int32
ALU = mybir.AluOpType
AF = mybir.ActivationFunctionType
AX = mybir.AxisListType

ENGS = ("pe", "act", "dve", "pool", "sp")
EPOCH = 16000
DMA_K = 8


class _Op:
    __slots__ = ("eng", "fn", "deps", "signal", "sig_seq", "dma", "dma_slot", "dma_val", "pre")

    def __init__(self, eng, fn):
        self.eng = eng
        self.fn = fn
        self.deps = []
        self.signal = False
        self.sig_seq = None
        self.dma = False
        self.dma_slot = None
        self.dma_val = None
        self.pre = None


class Prog:
    def __init__(self, nc):
        self.nc = nc
        self.ops = {e: [] for e in ENGS}
        self.last_w = {}
        self.readers = {}
        self.dma_hist = {e: [] for e in ENGS}
        self.bar_deps = []
        self.bar_pending = set()

    def _add(self, eng, fn, reads, writes, dma=False):
        op = _Op(eng, fn)
        op.dma = dma
        deps = []
        if eng in self.bar_pending:
            deps.extend(self.bar_deps)
            self.bar_pending.discard(eng)
        for r in reads:
            w = self.last_w.get(r)
            if w is not None:
                deps.append(w)
        for r in writes:
            w = self.last_w.get(r)
            if w is not None:
                deps.append(w)
            deps.extend(self.readers.get(r, ()))
        for r in reads:
            self.readers.setdefault(r, []).append(op)
        for r in writes:
            self.last_w[r] = op
            self.readers[r] = []
        op.deps = deps
        self.ops[eng].append(op)
        if dma:
            hist = self.dma_hist[eng]
            n = len(hist)
            op.dma_slot = n % DMA_K
            op.dma_val = 16 * (n // DMA_K + 1)
            if n >= DMA_K:
                op.pre = hist[n - DMA_K]
            hist.append(op)
        return op

    def op(self, eng, fn, reads=(), writes=()):
        return self._add(eng, fn, reads, writes)

    def dma(self, eng, out, in_, reads=(), writes=(), **kw):
        return self._add(eng, lambda e: e.dma_start(out=out, in_=in_, **kw), reads, writes, dma=True)

    def barrier(self):
        lasts = []
        for e in ENGS:
            for op in reversed(self.ops[e]):
                if not op.dma:
                    lasts.append(op)
                    break
            lasts.extend(self.dma_hist[e][-DMA_K:])
        self.bar_deps = lasts
        self.bar_pending = set(ENGS)
        self.last_w = {}
        self.readers = {}

    def emit(self):
        nc = self.nc
        for e in ENGS:
            for op in self.ops[e]:
                for d in op.deps:
                    if d.dma:
                        continue
                    if d.eng == "pe" and op.eng == "pe" and not op.dma:
                        continue
                    d.signal = True
        nsig = {}
        for e in ENGS:
            s = 0
            for op in self.ops[e]:
                if op.signal and not op.dma:
                    s += 1
                    op.sig_seq = s
            nsig[e] = s
        import contextlib
        stack = contextlib.ExitStack()
        sems = {}
        dsems = {}
        for e in ENGS:
            n_ep = max(1, (nsig[e] + EPOCH - 1) // EPOCH)
            sems[e] = [stack.enter_context(nc.semaphore(f"s_{e}_{k}")) for k in range(n_ep)]
            if self.dma_hist[e]:
                dsems[e] = [stack.enter_context(nc.semaphore(f"d_{e}_{k}")) for k in range(DMA_K)]

        def target(d):
            if d.dma:
                return ("d", d.eng, d.dma_slot), dsems[d.eng][d.dma_slot], d.dma_val
            ep = (d.sig_seq - 1) // EPOCH
            return ("s", d.eng, ep), sems[d.eng][ep], d.sig_seq - ep * EPOCH

        with stack:
            block = stack.enter_context(nc.Block())

            def run(e, h):
                seen = {}
                for op in self.ops[e]:
                    need = {}
                    dl = op.deps if op.pre is None else op.deps + [op.pre]
                    for d in dl:
                        if (not d.dma) and d.eng == "pe" and e == "pe" and not op.dma:
                            continue
                        key, sem, val = target(d)
                        if seen.get(key, 0) >= val:
                            continue
                        if key not in need or need[key][1] < val:
                            need[key] = (sem, val)
                    for key, (sem, val) in need.items():
                        h.wait_ge(sem, val)
                        seen[key] = val
                    ins = op.fn(h)
                    if op.dma:
                        ins.then_inc(dsems[e][op.dma_slot], 16)
                    elif op.signal:
                        ep = (op.sig_seq - 1) // EPOCH
                        ins.then_inc(sems[e][ep], 1)
                hist = self.dma_hist[e]
                for d in hist[-DMA_K:]:
                    h.wait_ge(dsems[e][d.dma_slot], d.dma_val)

            @block.tensor
            def _(h):
                run("pe", h)

            @block.scalar
            def _(h):
                run("act", h)

            @block.vector
            def _(h):
                run("dve", h)

            @block.gpsimd
            def _(h):
                run("pool", h)

            @block.sync
            def _(h):
                run("sp", h)


D = 1024
NMETA = 16
DFF = 4096
EPS = 1e-6
ARENA0 = 20480
ARENA_END = 229376 - 1024
MASKNEG = -240000.0
ATT_SKIP = 60.0


def lambda_init_for(i):
    return 0.8 - 0.6 * math.exp(-0.3 * i)


def build(SEQ, depth=4, dbg=False):
    nc = bass.Bass("TRN2", target_bir_lowering=False)
    NREAL = SEQ + NMETA
    NT = (NREAL + 127) // 128
    LP = NT * 128
    n_attn = (depth + 1) // 2
    n_gla = depth // 2
    blocks = [(t0, min(4, NT - t0)) for t0 in range(0, NT, 4)]
    skind = "ExternalOutput" if dbg else "Internal"

    def din(name, shape):
        return nc.dram_tensor(name, list(shape), F32, kind="ExternalInput")

    x_d = din("x", [SEQ, D])
    meta_d = din("meta_tokens", [NMETA, D])
    mixw_d = din("mix_norm_w", [depth, D])
    awin_d = din("attn_w_in", [n_attn, D, 3 * D])
    alam_d = din("attn_lambda", [n_attn, 4, 64])
    asub_d = din("attn_subln_w", [n_attn, 128])
    awout_d = din("attn_w_out", [n_attn, D, D])
    gwin_d = din("gla_w_in", [max(n_gla, 1), D, 3088])
    ggu_d = din("gla_w_gate_up", [max(n_gla, 1), 16, 512])
    ggb_d = din("gla_gate_bias", [max(n_gla, 1), 512])
    gnw_d = din("gla_norm_w", [max(n_gla, 1), 256])
    gwout_d = din("gla_w_out", [max(n_gla, 1), D, D])
    mlpw_d = din("mlp_norm_w", [depth, D])
    wup_d = din("mlp_w_up", [depth, D, DFF])
    wdn_d = din("mlp_w_down", [depth, DFF, D])
    fnw_d = din("final_norm_w", [D])
    out_d = nc.dram_tensor("out", [SEQ, D], F32, kind="ExternalOutput")

    def scr(name, shape, dt):
        return nc.dram_tensor(name, list(shape), dt, kind=skind)

    hT_d = scr("s_hT", [D, LP], F32)
    qT_d = scr("s_qT", [D, LP], BF16)
    kT_d = scr("s_kT", [D, LP], BF16)
    v_d = scr("s_v", [LP, D], BF16)
    gq_d = scr("s_gq", [512, LP], F32)
    gk_d = scr("s_gk", [512, LP], F32)
    gsp_d = scr("s_gsp", [512, LP], F32)
    sg_d = scr("s_sg", [D, LP], BF16)
    onT_d = scr("s_onT", [D, LP], BF16)
    oT_d = scr("s_oT", [D, LP], F32)
    Win_b = [nc.dram_tensor(f"w_in{i}", [6, 128, 4096], BF16) for i in range(depth)]
    Wout_b = [nc.dram_tensor(f"w_out{i}", [2, 128, 4096], BF16) for i in range(depth)]
    Wup_b = [nc.dram_tensor(f"w_up{i}", [8, 128, 4096], BF16) for i in range(depth)]
    Wdn_b = [nc.dram_tensor(f"w_dn{i}", [8, 128, 4096], BF16) for i in range(depth)]
    Wgz_b = [nc.dram_tensor(f"w_gz{i}", [128, 128], BF16) for i in range(depth)]
    Wgu_b = [nc.dram_tensor(f"w_gu{i}", [16, 512], BF16) for i in range(depth)]

    P = Prog(nc)
    banks = [nc.alloc_psum_tensor(f"bank{i}", [128, 512], F32) for i in range(8)]

    class Arena:
        def __init__(self, base):
            self.off = base
            self.n = 0

        def __call__(self, shape, dt, name=None):
            esz = 4 if dt in (F32, I32) else 2
            nbytes = int(np.prod(shape[1:])) * esz
            nbytes = (nbytes + 63) // 64 * 64
            uid[0] += 1
            t = nc.alloc_sbuf_tensor_at(f"{name or 't'}_{uid[0]}", list(shape), dt, offset=self.off)
            self.off += nbytes
            assert self.off <= ARENA_END, f"SBUF overflow {self.off}"
            return t

    uid = [0]
    CA = Arena(ARENA0)
    identf = CA([128, 128], F32, "identf")
    identb = CA([128, 128], BF16, "identb")
    onesb = CA([128, 128], BF16, "onesb")
    onesf = CA([128, 128], F32, "onesf")
    trimask = CA([128, 128], F32, "trimask")
    negmask = CA([128, 128], BF16, "negmask")
    epsc = CA([128, 1], F32, "epsc")
    kki = CA([128, 1], I32, "kki")
    kkf = CA([128, 1], F32, "kkf")
    NMB = 40
    btab = CA([128, 8 * NMB], F32, "btab")
    neglam = CA([128, max(n_attn, 1)], F32, "neglam")
    negb = CA([128, 4 * max(n_gla, 1)], F32, "negb")
    fnw = CA([128, 8], F32, "fnw")
    scl = CA([128, 40], F32, "scl")
    lamb = CA([128, 256], F32, "lamb")
    lamt = CA([128, 8], F32, "lamt")
    vstg = CA([8, 128], F32, "vstg")
    CONST_END = CA.off

    def load_vec_T(vec_d, off, nrows, dst_ap, dst_key):
        P.dma("sp", vstg[0:nrows, :], bass.AP(vec_d, off, [[128, nrows], [1, 128]]), writes=["vstg"])
        P.op("pe", lambda e: e.transpose(out=banks[7][:, 0:nrows], in_=vstg[0:nrows, :], identity=identf[0:nrows, 0:nrows]),
             reads=["vstg", "identf"], writes=[("ps", 7)])
        P.op("dve", lambda e: e.tensor_copy(out=dst_ap, in_=banks[7][:, 0:nrows]), reads=[("ps", 7)], writes=[dst_key])

    for idt, nm, val in ((identf, "identf", 1.0), (identb, "identb", 1.0)):
        P.op("pool", lambda e, idt=idt: e.memset(idt[:], 0.0), writes=[nm])
        P.op("pool", lambda e, idt=idt: e.affine_select(out=idt[:], in_=idt[:], compare_op=ALU.not_equal, fill=1.0,
                                                         base=0, pattern=[[-1, 128]], channel_multiplier=1),
             reads=[nm], writes=[nm])
    P.op("pool", lambda e: e.memset(onesb[:], 1.0), writes=["onesb"])
    P.op("pool", lambda e: e.memset(onesf[:], 1.0), writes=["onesf"])
    P.op("pool", lambda e: e.memset(epsc[:], EPS), writes=["epsc"])
    P.op("pool", lambda e: e.memset(trimask[:], 1.0), writes=["trimask"])
    P.op("pool", lambda e: e.affine_select(out=trimask[:], in_=trimask[:], compare_op=ALU.is_ge, fill=0.0, base=0,
                                           pattern=[[1, 128]], channel_multiplier=-1), reads=["trimask"], writes=["trimask"])
    P.op("pool", lambda e: e.memset(negmask[:], 0.0), writes=["negmask"])
    P.op("pool", lambda e: e.affine_select(out=negmask[:], in_=negmask[:], compare_op=ALU.is_ge, fill=MASKNEG, base=0,
                                           pattern=[[1, 128]], channel_multiplier=-1), reads=["negmask"], writes=["negmask"])
    P.op("pool", lambda e: e.iota(out=kki[:], pattern=[[0, 1]], base=0, channel_multiplier=1), writes=["kki"])
    P.op("pool", lambda e: e.tensor_copy(out=kkf[:], in_=kki[:]), reads=["kki"], writes=["kkf"])
    slopes = [2.0 ** (-(h + 1)) for h in range(8)]
    NQ = [128 if h == 0 else (256 if h == 1 else 512) for h in range(8)]
    for h in range(8):
        for mi in range(NMB):
            m = mi - 4
            P.op("dve", lambda e, h=h, mi=mi, m=m: e.tensor_scalar(
                out=btab[:, h * NMB + mi:h * NMB + mi + 1], in0=kkf[:], scalar1=slopes[h],
                scalar2=-slopes[h] * (128.0 * m + NQ[h] / 2.0), op0=ALU.mult, op1=ALU.add),
                reads=["kkf"], writes=[("btab", h, mi)])
    load_vec_T(fnw_d, 0, 8, fnw[:, 0:8], "fnw")
    for j in range(n_attn):
        li = lambda_init_for(2 * j)
        P.dma("sp", lamb[:], bass.AP(alam_d, j * 256, [[0, 128], [1, 256]]), writes=["lamb"])
        P.op("dve", lambda e: e.tensor_tensor(out=lamb[:, 0:64], in0=lamb[:, 0:64], in1=lamb[:, 64:128], op=ALU.mult),
             reads=["lamb"], writes=["lamb"])
        P.op("dve", lambda e: e.tensor_tensor(out=lamb[:, 128:192], in0=lamb[:, 128:192], in1=lamb[:, 192:256], op=ALU.mult),
             reads=["lamb"], writes=["lamb"])
        P.op("dve", lambda e: e.reduce_sum(out=lamt[:, 0:1], in_=lamb[:, 0:64], axis=AX.X), reads=["lamb"], writes=["lamt"])
        P.op("dve", lambda e: e.reduce_sum(out=lamt[:, 1:2], in_=lamb[:, 128:192], axis=AX.X), reads=["lamb"], writes=["lamt"])
        P.op("act", lambda e: e.activation(out=lamt[:, 2:4], in_=lamt[:, 0:2], func=AF.Exp), reads=["lamt"], writes=["lamt"])
        P.op("dve", lambda e: e.tensor_tensor(out=lamt[:, 4:5], in0=lamt[:, 3:4], in1=lamt[:, 2:3], op=ALU.subtract),
             reads=["lamt"], writes=["lamt"])
        P.op("dve", lambda e, j=j, li=li: e.tensor_scalar(out=neglam[:, j:j + 1], in0=lamt[:, 4:5], scalar1=-li, scalar2=None,
                                                          op0=ALU.add), reads=["lamt"], writes=[("neglam", j)])
    for j in range(n_gla):
        load_vec_T(ggb_d, j * 512, 4, negb[:, 4 * j:4 * j + 4], ("negb", j))
        P.op("dve", lambda e, j=j: e.tensor_scalar(out=negb[:, 4 * j:4 * j + 4], in0=negb[:, 4 * j:4 * j + 4], scalar1=-1.0,
                                                   scalar2=None, op0=ALU.mult), reads=[("negb", j)], writes=[("negb", j)])

    rsc = CA([128, 24 * depth], F32, "rsc")
    CONST_END = CA.off
    for i in range(depth):
        j = i // 2
        load_vec_T(mixw_d, i * D, 8, rsc[:, 24 * i:24 * i + 8], ("rsc", i, 0))
        load_vec_T(mlpw_d, i * D, 8, rsc[:, 24 * i + 16:24 * i + 24], ("rsc", i, 2))
        if i % 2 == 0:
            load_vec_T(asub_d, j * 128, 1, scl[:, 16:17], "scl")
            P.op("dve", lambda e, i=i: e.tensor_scalar(out=rsc[:, 24 * i + 8:24 * i + 16], in0=onesf[:, 0:8], scalar1=scl[:, 16:17],
                                                       scalar2=1.0 - lambda_init_for(i), op0=ALU.mult, op1=ALU.mult),
                 reads=["scl", "onesf"], writes=[("rsc", i, 1)])
        else:
            load_vec_T(gnw_d, j * 256, 2, scl[:, 16:18], "scl")
            for c in range(8):
                P.op("dve", lambda e, c=c, i=i: e.tensor_copy(out=rsc[:, 24 * i + 8 + c:24 * i + 9 + c], in_=scl[:, 16 + (c % 2):17 + (c % 2)]),
                     reads=["scl"], writes=[("rsc", i, 1)])

    PREP_BASE = ARENA_END - 32768 - 2048
    pst = [nc.alloc_sbuf_tensor_at(f"pst{k}", [128, 2048], F32, offset=PREP_BASE + k * 8192) for k in range(3)]
    pob = [nc.alloc_sbuf_tensor_at(f"pob{k}", [128, 2048], BF16, offset=PREP_BASE + 24576 + k * 4096) for k in range(2)]

    def matrix_tasks(W_ap, K, Nc, Wb, CW, sc0, tag):
        ts = []
        for kc in range(K // 128):
            for p0 in range(0, Nc, 2048):
                ts.append(("mat", W_ap, kc, p0, min(2048, Nc - p0), Wb, CW, sc0, tag))
        return ts

    def group_tasks(k):
        ts = []
        if k >= 1:
            i = k - 1
            j = i // 2
            wo = awout_d.ap()[j] if i % 2 == 0 else gwout_d.ap()[j]
            ts += matrix_tasks(wo, D, D, Wout_b[i], 512, 24 * i + 8, ("out", i))
            ts += matrix_tasks(wup_d.ap()[i], D, DFF, Wup_b[i], 512, 24 * i + 16, ("up", i))
            ts += matrix_tasks(wdn_d.ap()[i], DFF, D, Wdn_b[i], 128, None, ("dn", i))
        if k < depth:
            i = k
            j = i // 2
            if i % 2 == 0:
                ts += matrix_tasks(awin_d.ap()[j], D, 3 * D, Win_b[i], 512, 24 * i, ("in", i))
            else:
                ts += matrix_tasks(gwin_d.ap()[j][:, 0:3072], D, 3072, Win_b[i], 512, 24 * i, ("in", i))
                ts.append(("gz", i, j))
                ts.append(("gu", i, j))
        return ts

    def t_load(t, n):
        b = n % 3
        if t[0] == "mat":
            _, W_ap, kc, p0, pn, Wb, CW, sc0, tag = t
            P.dma("sp", pst[b][:, :pn], W_ap[kc * 128:(kc + 1) * 128, p0:p0 + pn], writes=[("pst", b)])
        elif t[0] == "gz":
            _, i, j = t
            for kc in range(8):
                P.dma("sp", pst[b][:, kc * 16:(kc + 1) * 16], gwin_d.ap()[j][kc * 128:(kc + 1) * 128, 3072:3088], writes=[("pst", b, kc), ("pst", b)])
        else:
            _, i, j = t
            P.dma("sp", pst[b][0:16, 0:512], ggu_d.ap()[j], writes=[("pst", b)])

    def t_conv(t, n, eng):
        b = n % 3
        o = n % 2
        if t[0] == "mat":
            _, W_ap, kc, p0, pn, Wb, CW, sc0, tag = t
            rs = [("pst", b)]
            if sc0 is None:
                if eng == "act":
                    P.op("act", lambda e: e.activation(out=pob[o][:, :pn], in_=pst[b][:, :pn], func=AF.Copy), reads=rs, writes=[("pob", o)])
                else:
                    P.op(eng, lambda e: e.tensor_copy(out=pob[o][:, :pn], in_=pst[b][:, :pn]), reads=rs, writes=[("pob", o)])
            else:
                sc = rsc[:, sc0 + kc:sc0 + kc + 1]
                if eng == "act":
                    P.op("act", lambda e: e.activation(out=pob[o][:, :pn], in_=pst[b][:, :pn], func=AF.Copy, scale=sc), reads=rs, writes=[("pob", o)])
                else:
                    P.op(eng, lambda e: e.tensor_scalar(out=pob[o][:, :pn], in0=pst[b][:, :pn], scalar1=sc, scalar2=0.0, op0=ALU.mult, op1=ALU.add),
                         reads=rs, writes=[("pob", o)])
        elif t[0] == "gz":
            _, i, j = t
            for kc in range(8):
                P.op("dve", lambda e, kc=kc: e.tensor_scalar(out=pob[o][:, kc * 16:(kc + 1) * 16], in0=pst[b][:, kc * 16:(kc + 1) * 16],
                                                             scalar1=rsc[:, 24 * i + kc:24 * i + kc + 1], scalar2=None, op0=ALU.mult),
                     reads=[("pst", b, kc), ("pst", b)], writes=[("pob", o)])
        else:
            P.op("dve", lambda e: e.tensor_copy(out=pob[o][0:16, 0:512], in_=pst[b][0:16, 0:512]), reads=[("pst", b)], writes=[("pob", o)])

    def t_store(t, n):
        o = n % 2
        if t[0] == "mat":
            _, W_ap, kc, p0, pn, Wb, CW, sc0, tag = t
            for s in range(p0 // CW, (p0 + pn) // CW):
                P.dma("sp", Wb.ap()[s, :, kc * CW:(kc + 1) * CW], pob[o][:, s * CW - p0:(s + 1) * CW - p0], reads=[("pob", o)],
                      writes=[("wb", tag, s, kc)])
        elif t[0] == "gz":
            _, i, j = t
            P.dma("sp", Wgz_b[i].ap()[:, :], pob[o][:, 0:128], reads=[("pob", o)], writes=[("wgz", i)])
        else:
            _, i, j = t
            P.dma("sp", Wgu_b[i].ap()[:, :], pob[o][0:16, 0:512], reads=[("pob", o)], writes=[("wgu", i)])

    def prep_gen(tasks, engs):
        n = len(tasks)
        for s in range(n + 2):
            if s < n:
                t_load(tasks[s], s)
            if 0 <= s - 1 < n:
                t_conv(tasks[s - 1], s - 1, engs[(s - 1) % len(engs)])
            if 0 <= s - 2 < n:
                t_store(tasks[s - 2], s - 2)
            yield

    class BG:
        gen = None
        cnt = 0

        def step(self, every=1):
            if self.gen is None:
                return
            self.cnt += 1
            if self.cnt % every:
                return
            try:
                next(self.gen)
            except StopIteration:
                self.gen = None

        def drain(self):
            while self.gen is not None:
                self.step()

    bg = BG()
    bg.gen = prep_gen(group_tasks(0), ["dve", "act"])
    bg.drain()
    P.barrier()

    def t_phase(li):
        A = Arena(CONST_END)
        NB = 512
        hT = [A([128, 8, NB], F32, "hT") for _ in range(2)]
        xreg_off = A.off
        A([128, 4096], F32, "xreg")
        xst = [nc.alloc_sbuf_tensor_at(f"xst{li}_{k}", [128, 1024], F32, offset=xreg_off + k * 4096) for k in range(4)]
        onTs = [nc.alloc_sbuf_tensor_at(f"onTs{li}_{k}", [128, 8, NB], BF16, offset=xreg_off + k * 8192) for k in range(2)]
        r1_o = A([128, 8, NB], F32, "r1")
        off_r1 = A.off - 8 * NB * 4
        sqb8 = nc.alloc_sbuf_tensor_at(f"sqb8_{li}", [128, 8, NB], BF16, offset=off_r1)
        hnT = nc.alloc_sbuf_tensor_at(f"hnT_{li}", [128, 8, NB], BF16, offset=off_r1 + 8 * NB * 2)
        rt = A([128, NB], F32, "rt")
        rinv = A([128, NB], F32, "rinv")
        tmpf = [A([128, NB], F32, "tmpf") for _ in range(2)]
        hid = A([128, 32, NB], BF16, "hid")
        stg_off = A.off
        stg = [A([128, 4, NB], F32, "stg") for _ in range(2)]
        ost = [nc.alloc_sbuf_tensor_at(f"ost{li}_{k}", [128, 1024], F32, offset=stg_off + k * 8192) for k in range(2)]
        vst = A([128, 4, 1024], BF16, "vst")
        gsps = A([128, 4, NB], F32, "gsps")
        gzb = A([16, NB], BF16, "gzb")
        wgz = A([128, 128], BF16, "wgz")
        wgu = A([16, 512], BF16, "wgu")
        wsl = [A([128, 4096], BF16, "wsl") for _ in range(4)]
        stgB = [A([128, 8, NB], BF16, "stgB") for _ in range(2)]

        fin_layer = li - 1
        do_fin = li > 0
        do_in = li < depth
        is_attn_in = do_in and (li % 2 == 0)
        nb = len(blocks)

        seq_all = []
        if do_fin:
            seq_all += [("out", 0), ("out", 1)]
        for b_ in range(nb):
            if do_fin:
                if b_ + 1 < nb:
                    seq_all.append(("out", 0))
                seq_all += [("up", s) for s in range(8)] + [("dn", s) for s in range(8)]
                if b_ + 1 < nb:
                    seq_all.append(("out", 1))
            if do_in:
                seq_all += [("in", s) for s in range(6)]
        total = len(seq_all)
        issued = [0]
        gctr = [0]

        def w_issue(g):
            kind, s = seq_all[g]
            if kind == "out":
                src_ = Wout_b[fin_layer].ap()[s]
            elif kind == "up":
                src_ = Wup_b[fin_layer].ap()[s]
            elif kind == "dn":
                src_ = Wdn_b[fin_layer].ap()[s]
            else:
                src_ = Win_b[li].ap()[s]
            P.dma("sp", wsl[g % 4][:, :], src_, writes=[("wsl", g % 4)])

        def w_next(kind):
            g = gctr[0]
            gctr[0] += 1
            assert seq_all[g][0] == kind, (g, seq_all[g], kind)
            while issued[0] < min(total, g + 3):
                w_issue(issued[0])
                issued[0] += 1
            return wsl[g % 4], ("wsl", g % 4)

        acc = [0]

        def next_acc():
            b = acc[0] % 4
            acc[0] += 1
            return b

        if do_in and not is_attn_in:
            P.dma("sp", wgz[:], Wgz_b[li].ap()[:, :], writes=["wgz"])
            P.dma("sp", wgu[:], Wgu_b[li].ap()[:, :], writes=["wgu"])

        def norm(hb, hkey, N, want_hn=True):
            for hh in range(2):
                P.op("act", lambda e, hh=hh: e.activation(out=sqb8[:, 4 * hh:4 * hh + 4, :N], in_=hb[:, 4 * hh:4 * hh + 4, :N], func=AF.Square),
                     reads=[hkey], writes=[("sq8", hh)])
            for c in range(8):
                P.op("pe", lambda e, c=c: e.matmul(out=banks[4][:, :N], lhsT=onesb[:], rhs=sqb8[:, c, :N], start=(c == 0), stop=(c == 7)),
                     reads=[("sq8", c // 4), "onesb"], writes=[("ps", 4)])
            P.op("act", lambda e: e.activation(out=rt[:, :N], in_=banks[4][:, :N], func=AF.Ln, bias=epsc[:], scale=1.0 / D),
                 reads=[("ps", 4), "epsc"], writes=["rt"])
            P.op("act", lambda e: e.activation(out=rinv[:, :N], in_=rt[:, :N], func=AF.Exp, scale=-0.5), reads=["rt"], writes=["rinv"])
            if want_hn:
                for c in range(8):
                    eng = "dve"
                    P.op(eng, lambda e, c=c: e.tensor_tensor(out=hnT[:, c, :N], in0=hb[:, c, :N], in1=rinv[:, :N], op=ALU.mult),
                         reads=[hkey, "rinv"], writes=[("hnT", c)])

        HN = [("hnT", c) for c in range(8)]

        def lin_fm(kind, oc_list, nkc, cw, rhs_of, rhs_keys, N, epi):
            w, wkey = w_next(kind)
            for j, oc in enumerate(oc_list):
                b = next_acc()
                for kc in range(nkc):
                    P.op("pe", lambda e, b=b, kc=kc, j=j: e.matmul(out=banks[b][:, :N], lhsT=w[:, kc * cw + j * 128:kc * cw + (j + 1) * 128],
                                                                   rhs=rhs_of(kc), start=(kc == 0), stop=(kc == nkc - 1)),
                         reads=[wkey] + rhs_keys, writes=[("ps", b)])
                epi(oc, b)

        def blk(bi):
            t0, nt = blocks[bi]
            return t0, nt, nt * 128, t0 * 128, hT[bi % 2], ("hT", bi % 2)

        def load_block(bi):
            t0, nt, N, c0, hb, hkey = blk(bi)
            if li == 0:
                for tt in range(nt):
                    tile_i = t0 + tt
                    xs = xst[tt % 4]
                    xkey = ("R2", tt % 4)
                    g0 = tile_i * 128
                    if tile_i == 0:
                        P.dma("sp", xs[0:NMETA, :], meta_d.ap()[:, :], writes=[xkey])
                        P.dma("sp", xs[NMETA:128, :], x_d.ap()[0:128 - NMETA, :], writes=[xkey])
                    else:
                        nv = min(128, NREAL - g0)
                        if nv < 128:
                            P.op("dve", lambda e, xs=xs: e.memset(xs[:], 0.0), writes=[xkey])
                        P.dma("sp", xs[0:nv, :], x_d.ap()[g0 - NMETA:g0 - NMETA + nv, :], writes=[xkey])
                for tt in range(nt):
                    xs = xst[tt % 4]
                    xkey = ("R2", tt % 4)
                    for half in range(2):
                        for c4 in range(4):
                            c = half * 4 + c4
                            P.op("pe", lambda e, xs=xs, c=c, c4=c4, half=half: e.transpose(
                                out=banks[6 + half][:, c4 * 128:(c4 + 1) * 128], in_=xs[:, c * 128:(c + 1) * 128], identity=identf[:]),
                                reads=[xkey, "identf"], writes=[("ps", 6 + half)])
                        src_ = banks[6 + half][:, :].rearrange("p (c t) -> p c t", t=128)
                        dst = hb[:, half * 4:half * 4 + 4, tt * 128:(tt + 1) * 128]
                        if half == 0:
                            P.op("act", lambda e, src_=src_, dst=dst: e.activation(out=dst, in_=src_, func=AF.Copy),
                                 reads=[("ps", 6 + half)], writes=[hkey])
                        else:
                            P.op("dve", lambda e, src_=src_, dst=dst: e.tensor_copy(out=dst, in_=src_),
                                 reads=[("ps", 6 + half)], writes=[hkey])
            else:
                P.dma("sp", hb[:, :, :N], hT_d.ap()[:, c0:c0 + N].rearrange("(c p) n -> p c n", p=128), writes=[hkey])
                P.dma("sp", onTs[bi % 2][:, :, :N], onT_d.ap()[:, c0:c0 + N].rearrange("(c p) n -> p c n", p=128), writes=[("onTs", bi % 2)])

        def make_epi_res(hb, hkey, N):
            def epi_res(oc, b):
                P.op("dve", lambda e, oc=oc, b=b: e.tensor_tensor(out=hb[:, oc, :N], in0=hb[:, oc, :N], in1=banks[b][:, :N], op=ALU.add),
                     reads=[hkey, ("ps", b)], writes=[hkey])
            return epi_res

        def stageA_half(bi, s):
            t0, nt, N, c0, hb, hkey = blk(bi)
            on = onTs[bi % 2]
            lin_fm("out", [4 * s + q for q in range(4)], 8, 512, lambda kc: on[:, kc, :N], [("onTs", bi % 2)], N, make_epi_res(hb, hkey, N))

        def do_block(bi):
            t0, nt, N, c0, hb, hkey = blk(bi)
            if bi == 0:
                load_block(0)
                if do_fin:
                    stageA_half(0, 0)
                    stageA_half(0, 1)
            if bi + 1 < nb:
                load_block(bi + 1)
            epi_res = make_epi_res(hb, hkey, N)
            if do_fin:
                norm(hb, hkey, N)
                if bi + 1 < nb:
                    stageA_half(bi + 1, 0)
                rl_i = [0]

                def epi_up(oc, b):
                    k = rl_i[0] % 2
                    rl_i[0] += 1
                    tf = tmpf[k]
                    P.op("act", lambda e, b=b, tf=tf: e.activation(out=tf[:, :N], in_=banks[b][:, :N], func=AF.Relu), reads=[("ps", b)],
                         writes=[("tmpf", k)])
                    P.op("dve", lambda e, oc=oc, tf=tf: e.tensor_tensor(out=hid[:, oc, :N], in0=tf[:, :N], in1=tf[:, :N], op=ALU.mult),
                         reads=[("tmpf", k)], writes=["hid"])

                for s in range(8):
                    lin_fm("up", [4 * s + q for q in range(4)], 8, 512, lambda kc: hnT[:, kc, :N], HN, N, epi_up)
                for s in range(8):
                    lin_fm("dn", [s], 32, 128, lambda kc: hid[:, kc, :N], ["hid"], N, epi_res)

            if do_in:
                norm(hb, hkey, N)
                if do_fin and bi + 1 < nb:
                    stageA_half(bi + 1, 1)
                if is_attn_in:
                    for which, dst_d in ((0, qT_d), (1, kT_d)):
                        sb = stgB[which]
                        skey = ("stgB", which)

                        def epi_cp(oc, b, sb=sb, skey=skey):
                            P.op("act", lambda e, oc=oc, b=b: e.activation(out=sb[:, oc, :N], in_=banks[b][:, :N], func=AF.Copy),
                                 reads=[("ps", b)], writes=[skey])

                        for s in range(2):
                            lin_fm("in", [4 * s + q for q in range(4)], 8, 512, lambda kc: hnT[:, kc, :N], HN, N, epi_cp)
                        P.dma("sp", dst_d.ap()[:, c0:c0 + N].rearrange("(c p) n -> p c n", p=128), sb[:, :, :N], reads=[skey],
                              writes=[("dqk", which, bi)])
                else:
                    sq_ = stg[0]
                    sk_ = stg[1]

                    def epi_q(oc, b):
                        if oc < 4:
                            P.op("act", lambda e, oc=oc, b=b: e.activation(out=sq_[:, oc, :N], in_=banks[b][:, :N], func=AF.Copy, scale=128.0 ** -0.5),
                                 reads=[("ps", b)], writes=[("stg", 0)])
                        else:
                            P.op("dve", lambda e, oc=oc, b=b: e.tensor_copy(out=sk_[:, oc - 4, :N], in_=banks[b][:, :N]),
                                 reads=[("ps", b)], writes=[("stg", 1)])

                    for s in range(2):
                        lin_fm("in", [4 * s + q for q in range(4)], 8, 512, lambda kc: hnT[:, kc, :N], HN, N, epi_q)
                    P.dma("sp", gq_d.ap()[:, c0:c0 + N].rearrange("(c p) n -> p c n", p=128), sq_[:, :, :N], reads=[("stg", 0)],
                          writes=[("dgq", bi)])
                    P.dma("sp", gk_d.ap()[:, c0:c0 + N].rearrange("(c p) n -> p c n", p=128), sk_[:, :, :N], reads=[("stg", 1)],
                          writes=[("dgk", bi)])
                for half in range(2):
                    w, wkey = w_next("in")
                    for tt in range(nt):
                        b = next_acc()
                        for kc in range(8):
                            P.op("pe", lambda e, b=b, kc=kc, tt=tt, w=w: e.matmul(out=banks[b][:, :], lhsT=hnT[:, kc, tt * 128:(tt + 1) * 128],
                                                                                  rhs=w[:, kc * 512:(kc + 1) * 512], start=(kc == 0), stop=(kc == 7)),
                                 reads=[wkey] + HN, writes=[("ps", b)])
                        if tt % 2 == 0:
                            P.op("act", lambda e, b=b, tt=tt, half=half: e.activation(out=vst[:, tt, half * 512:(half + 1) * 512], in_=banks[b][:, :], func=AF.Copy),
                                 reads=[("ps", b)], writes=["vst"])
                        else:
                            P.op("dve", lambda e, b=b, tt=tt, half=half: e.tensor_copy(out=vst[:, tt, half * 512:(half + 1) * 512], in_=banks[b][:, :]),
                                 reads=[("ps", b)], writes=["vst"])
                P.dma("sp", v_d.ap()[c0:c0 + N, :].rearrange("(t p) n -> p t n", p=128), vst[:, :nt, :], reads=["vst"], writes=[("dv", bi)])
                if not is_attn_in:
                    sb = stgB[0]
                    skey = ("stgB", 0)

                    def epi_g(oc, b):
                        P.op("act", lambda e, oc=oc, b=b: e.activation(out=sb[:, oc, :N], in_=banks[b][:, :N], func=AF.Silu), reads=[("ps", b)],
                             writes=[skey])

                    for s in range(2):
                        lin_fm("in", [4 * s + q for q in range(4)], 8, 512, lambda kc: hnT[:, kc, :N], HN, N, epi_g)
                    P.dma("sp", sg_d.ap()[:, c0:c0 + N].rearrange("(c p) n -> p c n", p=128), sb[:, :, :N], reads=[skey], writes=[("dsg", bi)])
                    b = next_acc()
                    for kc in range(8):
                        P.op("pe", lambda e, b=b, kc=kc: e.matmul(out=banks[b][0:16, :N], lhsT=wgz[:, kc * 16:(kc + 1) * 16], rhs=hnT[:, kc, :N],
                                                                  start=(kc == 0), stop=(kc == 7)), reads=["wgz"] + HN, writes=[("ps", b)])
                    P.op("act", lambda e, b=b: e.activation(out=gzb[:, :N], in_=banks[b][0:16, :N], func=AF.Copy), reads=[("ps", b)], writes=["gzb"])
                    jg = li // 2
                    for oc in range(4):
                        b = next_acc()
                        P.op("pe", lambda e, b=b, oc=oc: e.matmul(out=banks[b][:, :N], lhsT=wgu[0:16, oc * 128:(oc + 1) * 128], rhs=gzb[0:16, :N],
                                                                  start=True, stop=True), reads=["wgu", "gzb"], writes=[("ps", b)])
                        tf = tmpf[oc % 2]
                        P.op("act", lambda e, b=b, oc=oc, tf=tf: e.activation(out=tf[:, :N], in_=banks[b][:, :N], func=AF.Exp, scale=-1.0,
                                                                              bias=negb[:, 4 * jg + oc:4 * jg + oc + 1]),
                             reads=[("ps", b), ("negb", jg)], writes=[("tmpf", oc % 2)])
                        P.op("act", lambda e, oc=oc, tf=tf: e.activation(out=gsps[:, oc, :N], in_=tf[:, :N], func=AF.Ln, bias=1.0),
                             reads=[("tmpf", oc % 2)], writes=["gsps"])
                    P.dma("sp", gsp_d.ap()[:, c0:c0 + N].rearrange("(c p) n -> p c n", p=128), gsps[:, :, :N], reads=["gsps"], writes=[("dgsp", bi)])
                P.dma("sp", hT_d.ap()[:, c0:c0 + N].rearrange("(c p) n -> p c n", p=128), hb[:, :, :N], reads=[hkey], writes=[("dhT", bi)])
            else:
                norm(hb, hkey, N, want_hn=False)
                if bi + 1 < nb:
                    stageA_half(bi + 1, 1)
                yT = r1_o
                for c in range(8):
                    P.op("dve", lambda e, c=c: e.scalar_tensor_tensor(out=yT[:, c, :N], in0=hb[:, c, :N], scalar=fnw[:, c:c + 1], in1=rinv[:, :N],
                                                                      op0=ALU.mult, op1=ALU.mult),
                         reads=[hkey, "rinv", "fnw"], writes=[("sq8", 0), ("sq8", 1)] + HN)
                for tt in range(nt):
                    tile_i = t0 + tt
                    g0 = tile_i * 128
                    lo = max(g0, NMETA)
                    hi = min(g0 + 128, NREAL)
                    if hi <= lo:
                        continue
                    os_ = ost[tt % 2]
                    okey = ("stg", tt % 2)
                    for half in range(2):
                        for c4 in range(4):
                            c = half * 4 + c4
                            P.op("pe", lambda e, c=c, c4=c4, half=half, tt=tt: e.transpose(
                                out=banks[6 + half][:, c4 * 128:(c4 + 1) * 128], in_=yT[:, c, tt * 128:(tt + 1) * 128], identity=identf[:]),
                                reads=HN + [("sq8", 0), ("sq8", 1), "identf"], writes=[("ps", 6 + half)])
                        if half == 0:
                            P.op("act", lambda e, os_=os_, half=half: e.activation(out=os_[:, half * 512:(half + 1) * 512], in_=banks[6 + half][:, :], func=AF.Copy),
                                 reads=[("ps", 6 + half)], writes=[okey])
                        else:
                            P.op("dve", lambda e, os_=os_, half=half: e.tensor_copy(out=os_[:, half * 512:(half + 1) * 512], in_=banks[6 + half][:, :]),
                                 reads=[("ps", 6 + half)], writes=[okey])
                    P.dma("sp", out_d.ap()[lo - NMETA:hi - NMETA, :], os_[lo - g0:hi - g0, :], reads=[okey], writes=[("dout", tile_i)])

        for bi in range(nb):
            do_block(bi)
        assert gctr[0] == total, (gctr[0], total)
        P.barrier()

    def attn_core(li):
        ja = li // 2
        ts = []
        for k in (li + 1, li + 2):
            if k <= depth:
                ts += group_tasks(k)
        bg.gen = prep_gen(ts, ["pool"])
        bg.cnt = 0
        A = Arena(CONST_END)
        Vall = A([128, NT, 1024], BF16, "Vall")
        KT = [A([128, LP], BF16, "KT") for _ in range(2)]
        QT = [A([128, LP], BF16, "QT") for _ in range(2)]
        Et = [[A([128, 512], BF16, "Et") for _ in range(3)] for _ in range(2)]
        fbs = [[A([128, 512], F32, "fb") for _ in range(7)] for _ in range(3)]
        pend2 = []
        sqos = [A([128, 512], BF16, "sqo") for _ in range(3)]
        ons = [A([128, 512], BF16, "ons") for _ in range(3)]
        P.dma("sp", Vall[:, :, :], v_d.ap()[:, :].rearrange("(t p) n -> p t n", p=128), writes=["Vall"])
        sb_i = [0]
        e_i = [0]
        qb_i = [0]
        def do_head(h):
            kt = KT[h % 2]
            qt = QT[h % 2]
            kkey = ("KT", h % 2)
            qkey = ("QT", h % 2)
            P.dma("sp", kt[:, :], kT_d.ap()[h * 128:(h + 1) * 128, :], writes=[kkey])
            P.dma("sp", qt[:, :], qT_d.ap()[h * 128:(h + 1) * 128, :], writes=[qkey])
            nq = NQ[h]
            def do_qblock(q0):
                N = min(nq, LP - q0)
                kbs = []
                for kb in range((q0 + N) // 128):
                    k0 = kb * 128
                    gap = q0 - (k0 + 127)
                    if gap > 0 and slopes[h] * gap > ATT_SKIP:
                        continue
                    kbs.append(kb)
                nk = len(kbs)

                def issue_qk(idx):
                    kb = kbs[idx]
                    k0 = kb * 128
                    m = (q0 - k0) // 128
                    res = []
                    for mp in range(2):
                        b = (sb_i[0] % 2) * 2 + mp
                        lo, hi = mp * 64, (mp + 1) * 64
                        if k0 < q0:
                            cs = 0
                            P.op("pe", lambda e, b=b, lo=lo, hi=hi, k0=k0: e.matmul(out=banks[b][:, 0:N], lhsT=kt[lo:hi, k0:k0 + 128], rhs=qt[lo:hi, q0:q0 + N],
                                                                                   start=True, stop=True), reads=[kkey, qkey], writes=[("ps", b)])
                        else:
                            cs = k0 - q0
                            P.op("pe", lambda e, b=b, lo=lo, hi=hi, k0=k0, cs=cs: e.matmul(out=banks[b][:, cs:cs + 128], lhsT=kt[lo:hi, k0:k0 + 128],
                                                                                          rhs=qt[lo:hi, k0:k0 + 128], start=True, stop=False),
                                 reads=[kkey, qkey], writes=[("ps", b)])
                            P.op("pe", lambda e, b=b, cs=cs: e.matmul(out=banks[b][:, cs:cs + 128], lhsT=identb[:], rhs=negmask[:], start=False, stop=True),
                                 reads=["identb", "negmask"], writes=[("ps", b)])
                            if cs + 128 < N:
                                P.op("pe", lambda e, b=b, lo=lo, hi=hi, k0=k0, cs=cs: e.matmul(out=banks[b][:, cs + 128:N], lhsT=kt[lo:hi, k0:k0 + 128],
                                                                                              rhs=qt[lo:hi, q0 + cs + 128:q0 + N], start=True, stop=True),
                                     reads=[kkey, qkey], writes=[("ps", b)])
                        res.append((b, cs))
                    sb_i[0] += 1
                    return res, m

                def issue_exp_pv(idx, res, m):
                    kb = kbs[idx]
                    first = idx == 0
                    last = idx == nk - 1
                    ei = e_i[0] % 3
                    e_i[0] += 1
                    for mp in range(2):
                        b, cs = res[mp]
                        et = Et[mp][ei]
                        ekey = ("Et", mp, ei)
                        col = h * NMB + (m + 4)
                        P.op("act", lambda e, b=b, cs=cs, et=et, col=col: e.activation(out=et[:, cs:N], in_=banks[b][:, cs:N], func=AF.Exp,
                                                                                      bias=btab[:, col:col + 1], scale=0.125),
                             reads=[("ps", b), ("btab", h, m + 4)], writes=[ekey])
                    for mp in range(2):
                        b, cs = res[mp]
                        et = Et[mp][ei]
                        ekey = ("Et", mp, ei)
                        P.op("pe", lambda e, mp=mp, cs=cs, et=et, kb=kb: e.matmul(out=banks[4 + mp][:, cs:N], lhsT=Vall[:, kb, h * 128:(h + 1) * 128],
                                                                                 rhs=et[:, cs:N], start=first, stop=last),
                             reads=["Vall", ekey], writes=[("ps", 4 + mp)])
                        P.op("pe", lambda e, mp=mp, cs=cs, et=et: e.matmul(out=banks[6 + mp][:, cs:N], lhsT=onesb[:], rhs=et[:, cs:N], start=first, stop=last),
                             reads=["onesb", ekey], writes=[("ps", 6 + mp)])

                pend = issue_qk(0)
                for idx in range(nk):
                    nxt = issue_qk(idx + 1) if idx + 1 < nk else None
                    issue_exp_pv(idx, pend[0], pend[1])
                    pend = nxt
                    bg.step(every=3)
                fsel = qb_i[0] % 3
                qb_i[0] += 1
                fb = fbs[fsel]
                P.op("dve", lambda e: e.tensor_copy(out=fb[2][:, :N], in_=banks[4][:, :N]), reads=[("ps", 4)], writes=[("fb", fsel, 2)])
                P.op("dve", lambda e: e.tensor_copy(out=fb[0][:, :N], in_=banks[6][:, :N]), reads=[("ps", 6)], writes=[("fb", fsel, 0)])
                P.op("dve", lambda e: e.tensor_copy(out=fb[3][:, :N], in_=banks[5][:, :N]), reads=[("ps", 5)], writes=[("fb", fsel, 3)])
                P.op("dve", lambda e: e.tensor_copy(out=fb[1][:, :N], in_=banks[7][:, :N]), reads=[("ps", 7)], writes=[("fb", fsel, 1)])
                P.op("dve", lambda e: e.reciprocal(out=fb[0][:, :N], in_=fb[0][:, :N]), reads=[("fb", fsel, 0)], writes=[("fb", fsel, 0)])
                P.op("dve", lambda e: e.reciprocal(out=fb[1][:, :N], in_=fb[1][:, :N]), reads=[("fb", fsel, 1)], writes=[("fb", fsel, 1)])
                P.op("pool", lambda e: e.tensor_tensor(out=fb[2][:, :N], in0=fb[2][:, :N], in1=fb[0][:, :N], op=ALU.mult),
                     reads=[("fb", fsel, 2), ("fb", fsel, 0)], writes=[("fb", fsel, 2)])
                P.op("pool", lambda e: e.tensor_tensor(out=fb[3][:, :N], in0=fb[3][:, :N], in1=fb[1][:, :N], op=ALU.mult),
                     reads=[("fb", fsel, 3), ("fb", fsel, 1)], writes=[("fb", fsel, 3)])
                P.op("dve", lambda e: e.scalar_tensor_tensor(out=fb[4][:, :N], in0=fb[3][:, :N], scalar=neglam[:, ja:ja + 1], in1=fb[2][:, :N],
                                                             op0=ALU.mult, op1=ALU.add), reads=[("fb", fsel, 2), ("fb", fsel, 3), ("neglam", ja)], writes=[("fb", fsel, 4)])

                def part2():
                    P.op("act", lambda e: e.activation(out=sqos[fsel][:, :N], in_=fb[4][:, :N], func=AF.Square), reads=[("fb", fsel, 4)], writes=[("sqo", fsel)])
                    bss = (sb_i[0] % 2) * 2
                    sb_i[0] += 1
                    P.op("pe", lambda e: e.matmul(out=banks[bss][:, :N], lhsT=onesb[:], rhs=sqos[fsel][:, :N], start=True, stop=True),
                         reads=["onesb", ("sqo", fsel)], writes=[("ps", bss)])
                    P.op("act", lambda e: e.activation(out=fb[5][:, :N], in_=banks[bss][:, :N], func=AF.Ln, bias=epsc[:], scale=1.0 / 128),
                         reads=[("ps", bss), "epsc"], writes=[("fb", fsel, 5)])
                    P.op("act", lambda e: e.activation(out=fb[6][:, :N], in_=fb[5][:, :N], func=AF.Exp, scale=-0.5), reads=[("fb", fsel, 5)], writes=[("fb", fsel, 6)])
                    on = ons[fsel]
                    P.op("pool", lambda e: e.tensor_tensor(out=on[:, :N], in0=fb[4][:, :N], in1=fb[6][:, :N], op=ALU.mult),
                         reads=[("fb", fsel, 4), ("fb", fsel, 6)], writes=[("ons", fsel)])
                    P.dma("sp", onT_d.ap()[h * 128:(h + 1) * 128, q0:q0 + N], on[:, :N], reads=[("ons", fsel)], writes=[("donT", h, q0)])

                pend2.append(part2)
                if len(pend2) > 2:
                    pend2.pop(0)()

            for q0 in range(0, LP, nq):
                do_qblock(q0)

        for h in range(8):
            do_head(h)
        while pend2:
            pend2.pop(0)()
        bg.drain()
        P.barrier()

    def gla_core(li):
        A = Arena(CONST_END)
        NB = 512
        gq = [A([128, 4, NB], F32, "gq") for _ in range(2)]
        gk = [A([128, 4, NB], F32, "gk") for _ in range(2)]
        gs = [A([128, 4, NB], F32, "gs") for _ in range(2)]
        vv = [A([128, 4, 1024], BF16, "vv") for _ in range(2)]
        ost = [A([128, 8, NB], F32, "ost") for _ in range(2)]
        S = A([128, 4, 256], F32, "S")
        Sb = A([128, 4, 256], BF16, "Sb")
        NS = 8
        bpos = [A([128, 128], F32, "bpos") for _ in range(NS)]
        Ep = [A([128, 128], F32, "Ep") for _ in range(NS)]
        Em = [A([128, 128], F32, "Em") for _ in range(NS)]
        qs = [A([128, 128], BF16, "qs") for _ in range(NS)]
        ks = [A([128, 128], BF16, "ks") for _ in range(NS)]
        ktl = [A([128, 128], BF16, "ktl") for _ in range(NS)]
        ktk = [A([128, 128], BF16, "ktk") for _ in range(NS)]
        Am = [A([128, 128], BF16, "Am") for _ in range(NS)]
        sgs = [A([128, 8, NB], BF16, "sgs") for _ in range(2)]
        onst = [A([128, 8, NB], BF16, "onst") for _ in range(2)]
        sqg = [A([128, 2, NB], BF16, "sqg") for _ in range(2)]
        grt = [A([128, NB], F32, "grt") for _ in range(2)]
        gri = [A([128, NB], F32, "gri") for _ in range(2)]
        gtf = [A([128, NB], F32, "gtf") for _ in range(2)]
        P.op("dve", lambda e: e.memset(S[:], 0.0), writes=[("S", 0), ("S", 1), ("S", 2), ("S", 3)])
        P.op("dve", lambda e: e.memset(Sb[:], 0.0), writes=[("Sb", 0), ("Sb", 1), ("Sb", 2), ("Sb", 3)])
        it = [0]

        def do_gblock(bi, t0, nt):
            N = nt * 128
            c0 = t0 * 128
            k2 = bi % 2
            P.dma("sp", gq[k2][:, :, :N], gq_d.ap()[:, c0:c0 + N].rearrange("(c p) n -> p c n", p=128), writes=[("gq", k2)])
            P.dma("sp", gk[k2][:, :, :N], gk_d.ap()[:, c0:c0 + N].rearrange("(c p) n -> p c n", p=128), writes=[("gk", k2)])
            P.dma("sp", gs[k2][:, :, :N], gsp_d.ap()[:, c0:c0 + N].rearrange("(c p) n -> p c n", p=128), writes=[("gs", k2)])
            P.dma("sp", vv[k2][:, :nt, :], v_d.ap()[c0:c0 + N, :].rearrange("(t p) n -> p t n", p=128), writes=[("vv", k2)])
            P.dma("sp", sgs[k2][:, :, :N], sg_d.ap()[:, c0:c0 + N].rearrange("(c p) n -> p c n", p=128), writes=[("sgs", k2)])

            def do_chunk(tt):
                cs = tt * 128
                par = it[0] % 2
                it[0] += 1
                ix = [par * 4 + h for h in range(4)]
                pA = [banks[h][:, 0:128] for h in range(4)]
                pO = [banks[h][:, 128:384] for h in range(4)]
                pD = [banks[4 + h][:, 0:256] for h in range(4)]
                pT = [banks[4 + h][:, 256:320].bitcast(BF16) for h in range(4)]
                for h in range(4):
                    i3 = ix[h]
                    P.op("dve", lambda e, i3=i3, h=h: e.tensor_tensor_scan(out=bpos[i3][:], data0=onesf[:], data1=gs[k2][:, h, cs:cs + 128], initial=0.0,
                                                                          op0=ALU.mult, op1=ALU.add), reads=[("gs", k2), "onesf"], writes=[("bpos", i3)])
                for h in range(4):
                    i3 = ix[h]
                    P.op("act", lambda e, i3=i3: e.activation(out=Ep[i3][:], in_=bpos[i3][:], func=AF.Exp, scale=-1.0 / 16), reads=[("bpos", i3)],
                         writes=[("Ep", i3)])
                    P.op("act", lambda e, i3=i3: e.activation(out=Em[i3][:], in_=bpos[i3][:], func=AF.Exp, scale=1.0 / 16), reads=[("bpos", i3)],
                         writes=[("Em", i3)])
                for h in range(4):
                    i3 = ix[h]
                    P.op("dve", lambda e, i3=i3, h=h: e.tensor_tensor(out=qs[i3][:], in0=gq[k2][:, h, cs:cs + 128], in1=Ep[i3][:], op=ALU.mult),
                         reads=[("gq", k2), ("Ep", i3)], writes=[("qs", i3)])
                    P.op("pool", lambda e, i3=i3, h=h: e.tensor_tensor(out=ks[i3][:], in0=gk[k2][:, h, cs:cs + 128], in1=Em[i3][:], op=ALU.mult),
                         reads=[("gk", k2), ("Em", i3)], writes=[("ks", i3)])
                    P.op("dve", lambda e, i3=i3, h=h: e.scalar_tensor_tensor(out=ktl[i3][:], in0=gk[k2][:, h, cs:cs + 128], scalar=Ep[i3][:, 127:128],
                                                                            in1=Em[i3][:], op0=ALU.mult, op1=ALU.mult),
                         reads=[("gk", k2), ("Ep", i3), ("Em", i3)], writes=[("ktl", i3)])
                for h in range(4):
                    i3 = ix[h]
                    P.op("pe", lambda e, i3=i3, h=h: e.matmul(out=pA[h], lhsT=ks[i3][:], rhs=qs[i3][:], start=True, stop=True),
                         reads=[("ks", i3), ("qs", i3)], writes=[("ps", h)])
                    P.op("pe", lambda e, i3=i3, h=h: e.transpose(out=pT[h], in_=ktl[i3][:], identity=identb[:]), reads=[("ktl", i3), "identb"],
                         writes=[("ps", 4 + h)])
                for h in range(4):
                    i3 = ix[h]
                    P.op("act", lambda e, i3=i3, h=h: e.activation(out=ktk[i3][:], in_=pT[h], func=AF.Copy), reads=[("ps", 4 + h)], writes=[("ktk", i3)])
                    P.op("dve", lambda e, i3=i3, h=h: e.tensor_tensor(out=Am[i3][:], in0=pA[h], in1=trimask[:], op=ALU.mult),
                         reads=[("ps", h), "trimask"], writes=[("Am", i3)])
                for h in range(4):
                    i3 = ix[h]
                    for ec in range(2):
                        P.op("pe", lambda e, i3=i3, ec=ec, h=h: e.matmul(out=banks[h][:, 128 + ec * 128:256 + ec * 128],
                                                                        lhsT=vv[k2][:, tt, h * 256 + ec * 128:h * 256 + (ec + 1) * 128], rhs=Am[i3][:],
                                                                        start=True, stop=False), reads=[("vv", k2), ("Am", i3)], writes=[("ps", h)])
                        P.op("pe", lambda e, i3=i3, ec=ec, h=h: e.matmul(out=banks[h][:, 128 + ec * 128:256 + ec * 128],
                                                                        lhsT=Sb[:, h, ec * 128:(ec + 1) * 128], rhs=qs[i3][:], start=False, stop=True),
                             reads=[("Sb", h), ("qs", i3)], writes=[("ps", h)])
                    P.op("pe", lambda e, i3=i3, h=h: e.matmul(out=pD[h], lhsT=ktk[i3][:], rhs=vv[k2][:, tt, h * 256:(h + 1) * 256],
                                                              start=True, stop=True), reads=[("ktk", i3), ("vv", k2)], writes=[("ps", 4 + h)])
                for h in range(4):
                    i3 = ix[h]
                    P.op("dve", lambda e, i3=i3, h=h: e.scalar_tensor_tensor(out=S[:, h, :], in0=S[:, h, :], scalar=Ep[i3][:, 127:128],
                                                                            in1=pD[h], op0=ALU.mult, op1=ALU.add),
                         reads=[("S", h), ("Ep", i3), ("ps", 4 + h)], writes=[("S", h)])
                    P.op("pool", lambda e, h=h: e.tensor_copy(out=Sb[:, h, :], in_=S[:, h, :]), reads=[("S", h)], writes=[("Sb", h)])
                    src_ = pO[h].rearrange("p (c t) -> p c t", t=128)
                    P.op("act", lambda e, src_=src_, h=h: e.activation(out=ost[k2][:, 2 * h:2 * h + 2, cs:cs + 128], in_=src_, func=AF.Copy),
                         reads=[("ps", h)], writes=[("ost", k2, h)])

            for tt in range(nt):
                do_chunk(tt)
            for h in range(4):
                p2 = h % 2
                P.op("act", lambda e, h=h, p2=p2: e.activation(out=sqg[p2][:, :, :N], in_=ost[k2][:, 2 * h:2 * h + 2, :N], func=AF.Square),
                     reads=[("ost", k2, h)], writes=[("sqg", p2)])
                for jj in range(2):
                    P.op("pe", lambda e, h=h, jj=jj, p2=p2: e.matmul(out=banks[4 + h][:, :N], lhsT=onesb[:], rhs=sqg[p2][:, jj, :N],
                                                                    start=(jj == 0), stop=(jj == 1)),
                         reads=[("sqg", p2), "onesb"], writes=[("ps", 4 + h)])
                P.op("act", lambda e, h=h, p2=p2: e.activation(out=grt[p2][:, :N], in_=banks[4 + h][:, :N], func=AF.Ln, bias=epsc[:], scale=1.0 / 256),
                     reads=[("ps", 4 + h), "epsc"], writes=[("grt", p2)])
                P.op("act", lambda e, p2=p2: e.activation(out=gri[p2][:, :N], in_=grt[p2][:, :N], func=AF.Exp, scale=-0.5), reads=[("grt", p2)], writes=[("gri", p2)])
                for jj in range(2):
                    P.op("dve", lambda e, h=h, jj=jj, p2=p2: e.tensor_tensor(out=gtf[jj][:, :N], in0=ost[k2][:, 2 * h + jj, :N], in1=gri[p2][:, :N], op=ALU.mult),
                         reads=[("ost", k2, h), ("gri", p2)], writes=[("gtf", jj)])
                    P.op("pool", lambda e, h=h, jj=jj: e.tensor_tensor(out=onst[k2][:, 2 * h + jj, :N], in0=gtf[jj][:, :N], in1=sgs[k2][:, 2 * h + jj, :N], op=ALU.mult),
                         reads=[("gtf", jj), ("sgs", k2)], writes=[("onst", k2, h)])
            P.dma("sp", onT_d.ap()[:, c0:c0 + N].rearrange("(c p) n -> p c n", p=128), onst[k2][:, :, :N],
                  reads=[("onst", k2, h) for h in range(4)], writes=[("donT", bi)])

        for bi, (t0, nt) in enumerate(blocks):
            do_gblock(bi, t0, nt)
        P.barrier()

    for li in range(depth + 1):
        t_phase(li)
        if li < depth:
            if li % 2 == 0:
                attn_core(li)
            else:
                gla_core(li)
    P.emit()
    return nc


_CACHE = {}
_NAMES = ["meta_tokens", "mix_norm_w", "attn_w_in", "attn_lambda", "attn_subln_w", "attn_w_out", "gla_w_in", "gla_w_gate_up",
          "gla_gate_bias", "gla_norm_w", "gla_w_out", "mlp_norm_w", "mlp_w_up", "mlp_w_down", "final_norm_w"]


def run(inputs, depth=4, dbg=False):
    x = np.ascontiguousarray(np.asarray(inputs["x"], dtype=np.float32))
    B, SEQ, _ = x.shape
    key = (SEQ, depth, dbg)
    if key not in _CACHE:
        _CACHE[key] = build(SEQ, depth, dbg)
    nc = _CACHE[key]
    shared = {n: np.ascontiguousarray(np.asarray(inputs[n], dtype=np.float32)) for n in _NAMES}
    in_maps = []
    for b in range(B):
        m = dict(shared)
        m["x"] = x[b]
        in_maps.append(m)
    res = run_bass_kernel_spmd(nc, in_maps, core_ids=list(range(B)))
    return res


def kernel(**inputs):
    res = run(inputs, depth=4, dbg=False)
    B = np.asarray(inputs["x"]).shape[0]
    return np.stack([np.asarray(res.results[b]["out"], dtype=np.float32) for b in range(B)], axis=0)
```

```python
import math
import numpy as np
import concourse.bass as bass
import concourse.mybir as mybir
from concourse.bass_utils import run_bass_kernel_spmd

F32 = mybir.dt.float32
BF16 = mybir.dt.bfloat16
I32 = mybir.dt.int32
ALU = mybir.AluOpType
AF = mybir.ActivationFunctionType
AX = mybir.AxisListType

ENGS = ("pe", "act", "dve", "pool", "sp")
EPOCH = 16000
DMA_K = 8


class _Op:
    __slots__ = ("eng", "fn", "deps", "signal", "sig_seq", "dma", "dma_slot", "dma_val", "pre")

    def __init__(self, eng, fn):
        self.eng = eng
        self.fn = fn
        self.deps = []
        self.signal = False
        self.sig_seq = None
        self.dma = False
        self.dma_slot = None
        self.dma_val = None
        self.pre = None


class Prog:
    def __init__(self, nc):
        self.nc = nc
        self.ops = {e: [] for e in ENGS}
        self.last_w = {}
        self.readers = {}
        self.dma_hist = {e: [] for e in ENGS}
        self.bar_deps = []
        self.bar_pending = set()

    def _add(self, eng, fn, reads, writes, dma=False):
        op = _Op(eng, fn)
        op.dma = dma
        deps = []
        if eng in self.bar_pending:
            deps.extend(self.bar_deps)
            self.bar_pending.discard(eng)
        for r in reads:
            w = self.last_w.get(r)
            if w is not None:
                deps.append(w)
        for r in writes:
            w = self.last_w.get(r)
            if w is not None:
                deps.append(w)
            deps.extend(self.readers.get(r, ()))
        for r in reads:
            self.readers.setdefault(r, []).append(op)
        for r in writes:
            self.last_w[r] = op
            self.readers[r] = []
        op.deps = deps
        self.ops[eng].append(op)
        if dma:
            hist = self.dma_hist[eng]
            n = len(hist)
            op.dma_slot = n % DMA_K
            op.dma_val = 16 * (n // DMA_K + 1)
            if n >= DMA_K:
                op.pre = hist[n - DMA_K]
            hist.append(op)
        return op

    def op(self, eng, fn, reads=(), writes=()):
        return self._add(eng, fn, reads, writes)

    def dma(self, eng, out, in_, reads=(), writes=(), **kw):
        return self._add(eng, lambda e: e.dma_start(out=out, in_=in_, **kw), reads, writes, dma=True)

    def barrier(self):
        lasts = []
        for e in ENGS:
            for op in reversed(self.ops[e]):
                if not op.dma:
                    lasts.append(op)
                    break
            lasts.extend(self.dma_hist[e][-DMA_K:])
        self.bar_deps = lasts
        self.bar_pending = set(ENGS)
        self.last_w = {}
        self.readers = {}

    def emit(self):
        nc = self.nc
        for e in ENGS:
            for op in self.ops[e]:
                for d in op.deps:
                    if d.dma:
                        continue
                    if d.eng == "pe" and op.eng == "pe" and not op.dma:
                        continue
                    d.signal = True
        nsig = {}
        for e in ENGS:
            s = 0
            for op in self.ops[e]:
                if op.signal and not op.dma:
                    s += 1
                    op.sig_seq = s
            nsig[e] = s
        import contextlib
        stack = contextlib.ExitStack()
        sems = {}
        dsems = {}
        for e in ENGS:
            n_ep = max(1, (nsig[e] + EPOCH - 1) // EPOCH)
            sems[e] = [stack.enter_context(nc.semaphore(f"s_{e}_{k}")) for k in range(n_ep)]
            if self.dma_hist[e]:
                dsems[e] = [stack.enter_context(nc.semaphore(f"d_{e}_{k}")) for k in range(DMA_K)]

        def target(d):
            if d.dma:
                return ("d", d.eng, d.dma_slot), dsems[d.eng][d.dma_slot], d.dma_val
            ep = (d.sig_seq - 1) // EPOCH
            return ("s", d.eng, ep), sems[d.eng][ep], d.sig_seq - ep * EPOCH

        with stack:
            block = stack.enter_context(nc.Block())

            def run(e, h):
                seen = {}
                for op in self.ops[e]:
                    need = {}
                    dl = op.deps if op.pre is None else op.deps + [op.pre]
                    for d in dl:
                        if (not d.dma) and d.eng == "pe" and e == "pe" and not op.dma:
                            continue
                        key, sem, val = target(d)
                        if seen.get(key, 0) >= val:
                            continue
                        if key not in need or need[key][1] < val:
                            need[key] = (sem, val)
                    for key, (sem, val) in need.items():
                        h.wait_ge(sem, val)
                        seen[key] = val
                    ins = op.fn(h)
                    if op.dma:
                        ins.then_inc(dsems[e][op.dma_slot], 16)
                    elif op.signal:
                        ep = (op.sig_seq - 1) // EPOCH
                        ins.then_inc(sems[e][ep], 1)
                hist = self.dma_hist[e]
                for d in hist[-DMA_K:]:
                    h.wait_ge(dsems[e][d.dma_slot], d.dma_val)

            @block.tensor
            def _(h):
                run("pe", h)

            @block.scalar
            def _(h):
                run("act", h)

            @block.vector
            def _(h):
                run("dve", h)

            @block.gpsimd
            def _(h):
                run("pool", h)

            @block.sync
            def _(h):
                run("sp", h)


D = 1024
NMETA = 16
DFF = 4096
EPS = 1e-6
ARENA0 = 20480
ARENA_END = 229376 - 1024
MASKNEG = -240000.0
ATT_SKIP = 60.0


def lambda_init_for(i):
    return 0.8 - 0.6 * math.exp(-0.3 * i)


def build(SEQ, depth=4, dbg=False):
    nc = bass.Bass("TRN2", target_bir_lowering=False)
    NREAL = SEQ + NMETA
    NT = (NREAL + 127) // 128
    LP = NT * 128
    n_attn = (depth + 1) // 2
    n_gla = depth // 2
    blocks = [(t0, min(4, NT - t0)) for t0 in range(0, NT, 4)]
    skind = "ExternalOutput" if dbg else "Internal"

    def din(name, shape):
        return nc.dram_tensor(name, list(shape), F32, kind="ExternalInput")

    x_d = din("x", [SEQ, D])
    meta_d = din("meta_tokens", [NMETA, D])
    mixw_d = din("mix_norm_w", [depth, D])
    awin_d = din("attn_w_in", [n_attn, D, 3 * D])
    alam_d = din("attn_lambda", [n_attn, 4, 64])
    asub_d = din("attn_subln_w", [n_attn, 128])
    awout_d = din("attn_w_out", [n_attn, D, D])
    gwin_d = din("gla_w_in", [max(n_gla, 1), D, 3088])
    ggu_d = din("gla_w_gate_up", [max(n_gla, 1), 16, 512])
    ggb_d = din("gla_gate_bias", [max(n_gla, 1), 512])
    gnw_d = din("gla_norm_w", [max(n_gla, 1), 256])
    gwout_d = din("gla_w_out", [max(n_gla, 1), D, D])
    mlpw_d = din("mlp_norm_w", [depth, D])
    wup_d = din("mlp_w_up", [depth, D, DFF])
    wdn_d = din("mlp_w_down", [depth, DFF, D])
    fnw_d = din("final_norm_w", [D])
    out_d = nc.dram_tensor("out", [SEQ, D], F32, kind="ExternalOutput")

    def scr(name, shape, dt):
        return nc.dram_tensor(name, list(shape), dt, kind=skind)

    hT_d = scr("s_hT", [D, LP], F32)
    qT_d = scr("s_qT", [D, LP], BF16)
    kT_d = scr("s_kT", [D, LP], BF16)
    v_d = scr("s_v", [LP, D], BF16)
    gq_d = scr("s_gq", [512, LP], F32)
    gk_d = scr("s_gk", [512, LP], F32)
    gsp_d = scr("s_gsp", [512, LP], F32)
    sg_d = scr("s_sg", [D, LP], BF16)
    onT_d = scr("s_onT", [D, LP], BF16)
    oT_d = scr("s_oT", [D, LP], F32)
    Win_b = [nc.dram_tensor(f"w_in{i}", [6, 128, 4096], BF16) for i in range(depth)]
    Wout_b = [nc.dram_tensor(f"w_out{i}", [2, 128, 4096], BF16) for i in range(depth)]
    Wup_b = [nc.dram_tensor(f"w_up{i}", [8, 128, 4096], BF16) for i in range(depth)]
    Wdn_b = [nc.dram_tensor(f"w_dn{i}", [8, 128, 4096], BF16) for i in range(depth)]
    Wgz_b = [nc.dram_tensor(f"w_gz{i}", [128, 128], BF16) for i in range(depth)]
    Wgu_b = [nc.dram_tensor(f"w_gu{i}", [16, 512], BF16) for i in range(depth)]

    P = Prog(nc)
    banks = [nc.alloc_psum_tensor(f"bank{i}", [128, 512], F32) for i in range(8)]

    class Arena:
        def __init__(self, base):
            self.off = base
            self.n = 0

        def __call__(self, shape, dt, name=None):
            esz = 4 if dt in (F32, I32) else 2
            nbytes = int(np.prod(shape[1:])) * esz
            nbytes = (nbytes + 63) // 64 * 64
            uid[0] += 1
            t = nc.alloc_sbuf_tensor_at(f"{name or 't'}_{uid[0]}", list(shape), dt, offset=self.off)
            self.off += nbytes
            assert self.off <= ARENA_END, f"SBUF overflow {self.off}"
            return t

    uid = [0]
    CA = Arena(ARENA0)
    identf = CA([128, 128], F32, "identf")
    identb = CA([128, 128], BF16, "identb")
    onesb = CA([128, 128], BF16, "onesb")
    onesf = CA([128, 128], F32, "onesf")
    trimask = CA([128, 128], F32, "trimask")
    negmask = CA([128, 128], BF16, "negmask")
    epsc = CA([128, 1], F32, "epsc")
    kki = CA([128, 1], I32, "kki")
    kkf = CA([128, 1], F32, "kkf")
    NMB = 40
    btab = CA([128, 8 * NMB], F32, "btab")
    neglam = CA([128, max(n_attn, 1)], F32, "neglam")
    negb = CA([128, 4 * max(n_gla, 1)], F32, "negb")
    fnw = CA([128, 8], F32, "fnw")
    scl = CA([128, 40], F32, "scl")
    lamb = CA([128, 256], F32, "lamb")
    lamt = CA([128, 8], F32, "lamt")
    vstg = CA([8, 128], F32, "vstg")
    CONST_END = CA.off

    def load_vec_T(vec_d, off, nrows, dst_ap, dst_key):
        P.dma("sp", vstg[0:nrows, :], bass.AP(vec_d, off, [[128, nrows], [1, 128]]), writes=["vstg"])
        P.op("pe", lambda e: e.transpose(out=banks[7][:, 0:nrows], in_=vstg[0:nrows, :], identity=identf[0:nrows, 0:nrows]),
             reads=["vstg", "identf"], writes=[("ps", 7)])
        P.op("dve", lambda e: e.tensor_copy(out=dst_ap, in_=banks[7][:, 0:nrows]), reads=[("ps", 7)], writes=[dst_key])

    for idt, nm, val in ((identf, "identf", 1.0), (identb, "identb", 1.0)):
        P.op("pool", lambda e, idt=idt: e.memset(idt[:], 0.0), writes=[nm])
        P.op("pool", lambda e, idt=idt: e.affine_select(out=idt[:], in_=idt[:], compare_op=ALU.not_equal, fill=1.0,
                                                         base=0, pattern=[[-1, 128]], channel_multiplier=1),
             reads=[nm], writes=[nm])
    P.op("pool", lambda e: e.memset(onesb[:], 1.0), writes=["onesb"])
    P.op("pool", lambda e: e.memset(onesf[:], 1.0), writes=["onesf"])
    P.op("pool", lambda e: e.memset(epsc[:], EPS), writes=["epsc"])
    P.op("pool", lambda e: e.memset(trimask[:], 1.0), writes=["trimask"])
    P.op("pool", lambda e: e.affine_select(out=trimask[:], in_=trimask[:], compare_op=ALU.is_ge, fill=0.0, base=0,
                                           pattern=[[1, 128]], channel_multiplier=-1), reads=["trimask"], writes=["trimask"])
    P.op("pool", lambda e: e.memset(negmask[:], 0.0), writes=["negmask"])
    P.op("pool", lambda e: e.affine_select(out=negmask[:], in_=negmask[:], compare_op=ALU.is_ge, fill=MASKNEG, base=0,
                                           pattern=[[1, 128]], channel_multiplier=-1), reads=["negmask"], writes=["negmask"])
    P.op("pool", lambda e: e.iota(out=kki[:], pattern=[[0, 1]], base=0, channel_multiplier=1), writes=["kki"])
    P.op("pool", lambda e: e.tensor_copy(out=kkf[:], in_=kki[:]), reads=["kki"], writes=["kkf"])
    slopes = [2.0 ** (-(h + 1)) for h in range(8)]
    NQ = [128 if h == 0 else (256 if h == 1 else 512) for h in range(8)]
    for h in range(8):
        for mi in range(NMB):
            m = mi - 4
            P.op("dve", lambda e, h=h, mi=mi, m=m: e.tensor_scalar(
                out=btab[:, h * NMB + mi:h * NMB + mi + 1], in0=kkf[:], scalar1=slopes[h],
                scalar2=-slopes[h] * (128.0 * m + NQ[h] / 2.0), op0=ALU.mult, op1=ALU.add),
                reads=["kkf"], writes=[("btab", h, mi)])
    load_vec_T(fnw_d, 0, 8, fnw[:, 0:8], "fnw")
    for j in range(n_attn):
        li = lambda_init_for(2 * j)
        P.dma("sp", lamb[:], bass.AP(alam_d, j * 256, [[0, 128], [1, 256]]), writes=["lamb"])
        P.op("dve", lambda e: e.tensor_tensor(out=lamb[:, 0:64], in0=lamb[:, 0:64], in1=lamb[:, 64:128], op=ALU.mult),
             reads=["lamb"], writes=["lamb"])
        P.op("dve", lambda e: e.tensor_tensor(out=lamb[:, 128:192], in0=lamb[:, 128:192], in1=lamb[:, 192:256], op=ALU.mult),
             reads=["lamb"], writes=["lamb"])
        P.op("dve", lambda e: e.reduce_sum(out=lamt[:, 0:1], in_=lamb[:, 0:64], axis=AX.X), reads=["lamb"], writes=["lamt"])
        P.op("dve", lambda e: e.reduce_sum(out=lamt[:, 1:2], in_=lamb[:, 128:192], axis=AX.X), reads=["lamb"], writes=["lamt"])
        P.op("act", lambda e: e.activation(out=lamt[:, 2:4], in_=lamt[:, 0:2], func=AF.Exp), reads=["lamt"], writes=["lamt"])
        P.op("dve", lambda e: e.tensor_tensor(out=lamt[:, 4:5], in0=lamt[:, 3:4], in1=lamt[:, 2:3], op=ALU.subtract),
             reads=["lamt"], writes=["lamt"])
        P.op("dve", lambda e, j=j, li=li: e.tensor_scalar(out=neglam[:, j:j + 1], in0=lamt[:, 4:5], scalar1=-li, scalar2=None,
                                                          op0=ALU.add), reads=["lamt"], writes=[("neglam", j)])
    for j in range(n_gla):
        load_vec_T(ggb_d, j * 512, 4, negb[:, 4 * j:4 * j + 4], ("negb", j))
        P.op("dve", lambda e, j=j: e.tensor_scalar(out=negb[:, 4 * j:4 * j + 4], in0=negb[:, 4 * j:4 * j + 4], scalar1=-1.0,
                                                   scalar2=None, op0=ALU.mult), reads=[("negb", j)], writes=[("negb", j)])

    rsc = CA([128, 24 * depth], F32, "rsc")
    CONST_END = CA.off
    for i in range(depth):
        j = i // 2
        load_vec_T(mixw_d, i * D, 8, rsc[:, 24 * i:24 * i + 8], ("rsc", i, 0))
        load_vec_T(mlpw_d, i * D, 8, rsc[:, 24 * i + 16:24 * i + 24], ("rsc", i, 2))
        if i % 2 == 0:
            load_vec_T(asub_d, j * 128, 1, scl[:, 16:17], "scl")
            P.op("dve", lambda e, i=i: e.tensor_scalar(out=rsc[:, 24 * i + 8:24 * i + 16], in0=onesf[:, 0:8], scalar1=scl[:, 16:17],
                                                       scalar2=1.0 - lambda_init_for(i), op0=ALU.mult, op1=ALU.mult),
                 reads=["scl", "onesf"], writes=[("rsc", i, 1)])
        else:
            load_vec_T(gnw_d, j * 256, 2, scl[:, 16:18], "scl")
            for c in range(8):
                P.op("dve", lambda e, c=c, i=i: e.tensor_copy(out=rsc[:, 24 * i + 8 + c:24 * i + 9 + c], in_=scl[:, 16 + (c % 2):17 + (c % 2)]),
                     reads=["scl"], writes=[("rsc", i, 1)])

    PREP_BASE = ARENA_END - 32768 - 2048
    pst = [nc.alloc_sbuf_tensor_at(f"pst{k}", [128, 2048], F32, offset=PREP_BASE + k * 8192) for k in range(3)]
    pob = [nc.alloc_sbuf_tensor_at(f"pob{k}", [128, 2048], BF16, offset=PREP_BASE + 24576 + k * 4096) for k in range(2)]

    def matrix_tasks(W_ap, K, Nc, Wb, CW, sc0, tag):
        ts = []
        for kc in range(K // 128):
            for p0 in range(0, Nc, 2048):
                ts.append(("mat", W_ap, kc, p0, min(2048, Nc - p0), Wb, CW, sc0, tag))
        return ts

    def group_tasks(k):
        ts = []
        if k >= 1:
            i = k - 1
            j = i // 2
            wo = awout_d.ap()[j] if i % 2 == 0 else gwout_d.ap()[j]
            ts += matrix_tasks(wo, D, D, Wout_b[i], 512, 24 * i + 8, ("out", i))
            ts += matrix_tasks(wup_d.ap()[i], D, DFF, Wup_b[i], 512, 24 * i + 16, ("up", i))
            ts += matrix_tasks(wdn_d.ap()[i], DFF, D, Wdn_b[i], 128, None, ("dn", i))
        if k < depth:
            i = k
            j = i // 2
            if i % 2 == 0:
                ts += matrix_tasks(awin_d.ap()[j], D, 3 * D, Win_b[i], 512, 24 * i, ("in", i))
            else:
                ts += matrix_tasks(gwin_d.ap()[j][:, 0:3072], D, 3072, Win_b[i], 512, 24 * i, ("in", i))
                ts.append(("gz", i, j))
                ts.append(("gu", i, j))
        return ts

    def t_load(t, n):
        b = n % 3
        if t[0] == "mat":
            _, W_ap, kc, p0, pn, Wb, CW, sc0, tag = t
            P.dma("sp", pst[b][:, :pn], W_ap[kc * 128:(kc + 1) * 128, p0:p0 + pn], writes=[("pst", b)])
        elif t[0] == "gz":
            _, i, j = t
            for kc in range(8):
                P.dma("sp", pst[b][:, kc * 16:(kc + 1) * 16], gwin_d.ap()[j][kc * 128:(kc + 1) * 128, 3072:3088], writes=[("pst", b, kc), ("pst", b)])
        else:
            _, i, j = t
            P.dma("sp", pst[b][0:16, 0:512], ggu_d.ap()[j], writes=[("pst", b)])

    def t_conv(t, n, eng):
        b = n % 3
        o = n % 2
        if t[0] == "mat":
            _, W_ap, kc, p0, pn, Wb, CW, sc0, tag = t
            rs = [("pst", b)]
            if sc0 is None:
                if eng == "act":
                    P.op("act", lambda e: e.activation(out=pob[o][:, :pn], in_=pst[b][:, :pn], func=AF.Copy), reads=rs, writes=[("pob", o)])
                else:
                    P.op(eng, lambda e: e.tensor_copy(out=pob[o][:, :pn], in_=pst[b][:, :pn]), reads=rs, writes=[("pob", o)])
            else:
                sc = rsc[:, sc0 + kc:sc0 + kc + 1]
                if eng == "act":
                    P.op("act", lambda e: e.activation(out=pob[o][:, :pn], in_=pst[b][:, :pn], func=AF.Copy, scale=sc), reads=rs, writes=[("pob", o)])
                else:
                    P.op(eng, lambda e: e.tensor_scalar(out=pob[o][:, :pn], in0=pst[b][:, :pn], scalar1=sc, scalar2=0.0, op0=ALU.mult, op1=ALU.add),
                         reads=rs, writes=[("pob", o)])
        elif t[0] == "gz":
            _, i, j = t
            for kc in range(8):
                P.op("dve", lambda e, kc=kc: e.tensor_scalar(out=pob[o][:, kc * 16:(kc + 1) * 16], in0=pst[b][:, kc * 16:(kc + 1) * 16],
                                                             scalar1=rsc[:, 24 * i + kc:24 * i + kc + 1], scalar2=None, op0=ALU.mult),
                     reads=[("pst", b, kc), ("pst", b)], writes=[("pob", o)])
        else:
            P.op("dve", lambda e: e.tensor_copy(out=pob[o][0:16, 0:512], in_=pst[b][0:16, 0:512]), reads=[("pst", b)], writes=[("pob", o)])

    def t_store(t, n):
        o = n % 2
        if t[0] == "mat":
            _, W_ap, kc, p0, pn, Wb, CW, sc0, tag = t
            for s in range(p0 // CW, (p0 + pn) // CW):
                P.dma("sp", Wb.ap()[s, :, kc * CW:(kc + 1) * CW], pob[o][:, s * CW - p0:(s + 1) * CW - p0], reads=[("pob", o)],
                      writes=[("wb", tag, s, kc)])
        elif t[0] == "gz":
            _, i, j = t
            P.dma("sp", Wgz_b[i].ap()[:, :], pob[o][:, 0:128], reads=[("pob", o)], writes=[("wgz", i)])
        else:
            _, i, j = t
            P.dma("sp", Wgu_b[i].ap()[:, :], pob[o][0:16, 0:512], reads=[("pob", o)], writes=[("wgu", i)])

    def prep_gen(tasks, engs):
        n = len(tasks)
        for s in range(n + 2):
            if s < n:
                t_load(tasks[s], s)
            if 0 <= s - 1 < n:
                t_conv(tasks[s - 1], s - 1, engs[(s - 1) % len(engs)])
            if 0 <= s - 2 < n:
                t_store(tasks[s - 2], s - 2)
            yield

    class BG:
        gen = None
        cnt = 0

        def step(self, every=1):
            if self.gen is None:
                return
            self.cnt += 1
            if self.cnt % every:
                return
            try:
                next(self.gen)
            except StopIteration:
                self.gen = None

        def drain(self):
            while self.gen is not None:
                self.step()

    bg = BG()
    bg.gen = prep_gen(group_tasks(0), ["dve", "act"])
    bg.drain()
    P.barrier()

    def t_phase(li):
        A = Arena(CONST_END)
        NB = 512
        hT = [A([128, 8, NB], F32, "hT") for _ in range(2)]
        xreg_off = A.off
        A([128, 4096], F32, "xreg")
        xst = [nc.alloc_sbuf_tensor_at(f"xst{li}_{k}", [128, 1024], F32, offset=xreg_off + k * 4096) for k in range(4)]
        onTs = [nc.alloc_sbuf_tensor_at(f"onTs{li}_{k}", [128, 8, NB], BF16, offset=xreg_off + k * 8192) for k in range(2)]
        r1_o = A([128, 8, NB], F32, "r1")
        off_r1 = A.off - 8 * NB * 4
        sqb8 = nc.alloc_sbuf_tensor_at(f"sqb8_{li}", [128, 8, NB], BF16, offset=off_r1)
        hnT = nc.alloc_sbuf_tensor_at(f"hnT_{li}", [128, 8, NB], BF16, offset=off_r1 + 8 * NB * 2)
        rt = A([128, NB], F32, "rt")
        rinv = A([128, NB], F32, "rinv")
        tmpf = [A([128, NB], F32, "tmpf") for _ in range(2)]
        hid = A([128, 32, NB], BF16, "hid")
        stg_off = A.off
        stg = [A([128, 4, NB], F32, "stg") for _ in range(2)]
        ost = [nc.alloc_sbuf_tensor_at(f"ost{li}_{k}", [128, 1024], F32, offset=stg_off + k * 8192) for k in range(2)]
        vst = A([128, 4, 1024], BF16, "vst")
        gsps = A([128, 4, NB], F32, "gsps")
        gzb = A([16, NB], BF16, "gzb")
        wgz = A([128, 128], BF16, "wgz")
        wgu = A([16, 512], BF16, "wgu")
        wsl = [A([128, 4096], BF16, "wsl") for _ in range(4)]
        stgB = [A([128, 8, NB], BF16, "stgB") for _ in range(2)]

        fin_layer = li - 1
        do_fin = li > 0
        do_in = li < depth
        is_attn_in = do_in and (li % 2 == 0)
        nb = len(blocks)

        seq_all = []
        if do_fin:
            seq_all += [("out", 0), ("out", 1)]
        for b_ in range(nb):
            if do_fin:
                if b_ + 1 < nb:
                    seq_all.append(("out", 0))
                seq_all += [("up", s) for s in range(8)] + [("dn", s) for s in range(8)]
                if b_ + 1 < nb:
                    seq_all.append(("out", 1))
            if do_in:
                seq_all += [("in", s) for s in range(6)]
        total = len(seq_all)
        issued = [0]
        gctr = [0]

        def w_issue(g):
            kind, s = seq_all[g]
            if kind == "out":
                src_ = Wout_b[fin_layer].ap()[s]
            elif kind == "up":
                src_ = Wup_b[fin_layer].ap()[s]
            elif kind == "dn":
                src_ = Wdn_b[fin_layer].ap()[s]
            else:
                src_ = Win_b[li].ap()[s]
            P.dma("sp", wsl[g % 4][:, :], src_, writes=[("wsl", g % 4)])

        def w_next(kind):
            g = gctr[0]
            gctr[0] += 1
            assert seq_all[g][0] == kind, (g, seq_all[g], kind)
            while issued[0] < min(total, g + 3):
                w_issue(issued[0])
                issued[0] += 1
            return wsl[g % 4], ("wsl", g % 4)

        acc = [0]

        def next_acc():
            b = acc[0] % 4
            acc[0] += 1
            return b

        if do_in and not is_attn_in:
            P.dma("sp", wgz[:], Wgz_b[li].ap()[:, :], writes=["wgz"])
            P.dma("sp", wgu[:], Wgu_b[li].ap()[:, :], writes=["wgu"])

        def norm(hb, hkey, N, want_hn=True):
            for hh in range(2):
                P.op("act", lambda e, hh=hh: e.activation(out=sqb8[:, 4 * hh:4 * hh + 4, :N], in_=hb[:, 4 * hh:4 * hh + 4, :N], func=AF.Square),
                     reads=[hkey], writes=[("sq8", hh)])
            for c in range(8):
                P.op("pe", lambda e, c=c: e.matmul(out=banks[4][:, :N], lhsT=onesb[:], rhs=sqb8[:, c, :N], start=(c == 0), stop=(c == 7)),
                     reads=[("sq8", c // 4), "onesb"], writes=[("ps", 4)])
            P.op("act", lambda e: e.activation(out=rt[:, :N], in_=banks[4][:, :N], func=AF.Ln, bias=epsc[:], scale=1.0 / D),
                 reads=[("ps", 4), "epsc"], writes=["rt"])
            P.op("act", lambda e: e.activation(out=rinv[:, :N], in_=rt[:, :N], func=AF.Exp, scale=-0.5), reads=["rt"], writes=["rinv"])
            if want_hn:
                for c in range(8):
                    eng = "dve"
                    P.op(eng, lambda e, c=c: e.tensor_tensor(out=hnT[:, c, :N], in0=hb[:, c, :N], in1=rinv[:, :N], op=ALU.mult),
                         reads=[hkey, "rinv"], writes=[("hnT", c)])

        HN = [("hnT", c) for c in range(8)]

        def lin_fm(kind, oc_list, nkc, cw, rhs_of, rhs_keys, N, epi):
            w, wkey = w_next(kind)
            for j, oc in enumerate(oc_list):
                b = next_acc()
                for kc in range(nkc):
                    P.op("pe", lambda e, b=b, kc=kc, j=j: e.matmul(out=banks[b][:, :N], lhsT=w[:, kc * cw + j * 128:kc * cw + (j + 1) * 128],
                                                                   rhs=rhs_of(kc), start=(kc == 0), stop=(kc == nkc - 1)),
                         reads=[wkey] + rhs_keys, writes=[("ps", b)])
                epi(oc, b)

        def blk(bi):
            t0, nt = blocks[bi]
            return t0, nt, nt * 128, t0 * 128, hT[bi % 2], ("hT", bi % 2)

        def load_block(bi):
            t0, nt, N, c0, hb, hkey = blk(bi)
            if li == 0:
                for tt in range(nt):
                    tile_i = t0 + tt
                    xs = xst[tt % 4]
                    xkey = ("R2", tt % 4)
                    g0 = tile_i * 128
                    if tile_i == 0:
                        P.dma("sp", xs[0:NMETA, :], meta_d.ap()[:, :], writes=[xkey])
                        P.dma("sp", xs[NMETA:128, :], x_d.ap()[0:128 - NMETA, :], writes=[xkey])
                    else:
                        nv = min(128, NREAL - g0)
                        if nv < 128:
                            P.op("dve", lambda e, xs=xs: e.memset(xs[:], 0.0), writes=[xkey])
                        P.dma("sp", xs[0:nv, :], x_d.ap()[g0 - NMETA:g0 - NMETA + nv, :], writes=[xkey])
                for tt in range(nt):
                    xs = xst[tt % 4]
                    xkey = ("R2", tt % 4)
                    for half in range(2):
                        for c4 in range(4):
                            c = half * 4 + c4
                            P.op("pe", lambda e, xs=xs, c=c, c4=c4, half=half: e.transpose(
                                out=banks[6 + half][:, c4 * 128:(c4 + 1) * 128], in_=xs[:, c * 128:(c + 1) * 128], identity=identf[:]),
                                reads=[xkey, "identf"], writes=[("ps", 6 + half)])
                        src_ = banks[6 + half][:, :].rearrange("p (c t) -> p c t", t=128)
                        dst = hb[:, half * 4:half * 4 + 4, tt * 128:(tt + 1) * 128]
                        if half == 0:
                            P.op("act", lambda e, src_=src_, dst=dst: e.activation(out=dst, in_=src_, func=AF.Copy),
                                 reads=[("ps", 6 + half)], writes=[hkey])
                        else:
                            P.op("dve", lambda e, src_=src_, dst=dst: e.tensor_copy(out=dst, in_=src_),
                                 reads=[("ps", 6 + half)], writes=[hkey])
            else:
                P.dma("sp", hb[:, :, :N], hT_d.ap()[:, c0:c0 + N].rearrange("(c p) n -> p c n", p=128), writes=[hkey])

        def load_on(bi):
            t0, nt, N, c0, hb, hkey = blk(bi)
            P.dma("sp", onTs[bi % 2][:, :, :N], onT_d.ap()[:, c0:c0 + N].rearrange("(c p) n -> p c n", p=128), writes=[("onTs", bi % 2)])

        def make_epi_res(hb, hkey, N):
            def epi_res(oc, b):
                P.op("dve", lambda e, oc=oc, b=b: e.tensor_tensor(out=hb[:, oc, :N], in0=hb[:, oc, :N], in1=banks[b][:, :N], op=ALU.add),
                     reads=[hkey, ("ps", b)], writes=[hkey])
            return epi_res

        def stageA_half(bi, s):
            t0, nt, N, c0, hb, hkey = blk(bi)
            on = onTs[bi % 2]
            lin_fm("out", [4 * s + q for q in range(4)], 8, 512, lambda kc: on[:, kc, :N], [("onTs", bi % 2)], N, make_epi_res(hb, hkey, N))

        def do_block(bi):
            t0, nt, N, c0, hb, hkey = blk(bi)
            if bi == 0:
                load_block(0)
                if do_fin:
                    load_on(0)
                    if nb > 1:
                        load_on(1)
                    stageA_half(0, 0)
                    stageA_half(0, 1)
            if bi + 1 < nb:
                load_block(bi + 1)
            if do_fin and bi + 2 < nb:
                load_on(bi + 2)
            if do_in and not do_fin:
                P.dma("sp", hT_d.ap()[:, c0:c0 + N].rearrange("(c p) n -> p c n", p=128), hb[:, :, :N], reads=[hkey], writes=[("dhT", bi)])
            epi_res = make_epi_res(hb, hkey, N)
            if do_fin:
                norm(hb, hkey, N)
                if bi + 1 < nb:
                    stageA_half(bi + 1, 0)
                rl_i = [0]

                def epi_up(oc, b):
                    k = rl_i[0] % 2
                    rl_i[0] += 1
                    tf = tmpf[k]
                    P.op("act", lambda e, b=b, tf=tf: e.activation(out=tf[:, :N], in_=banks[b][:, :N], func=AF.Relu), reads=[("ps", b)],
                         writes=[("tmpf", k)])
                    P.op("dve", lambda e, oc=oc, tf=tf: e.tensor_tensor(out=hid[:, oc, :N], in0=tf[:, :N], in1=tf[:, :N], op=ALU.mult),
                         reads=[("tmpf", k)], writes=["hid"])

                for s in range(8):
                    lin_fm("up", [4 * s + q for q in range(4)], 8, 512, lambda kc: hnT[:, kc, :N], HN, N, epi_up)
                for s in range(8):
                    lin_fm("dn", [s], 32, 128, lambda kc: hid[:, kc, :N], ["hid"], N, epi_res)
                if do_in:
                    P.dma("sp", hT_d.ap()[:, c0:c0 + N].rearrange("(c p) n -> p c n", p=128), hb[:, :, :N], reads=[hkey], writes=[("dhT", bi)])

            if do_in:
                norm(hb, hkey, N)
                if do_fin and bi + 1 < nb:
                    stageA_half(bi + 1, 1)
                if is_attn_in:
                    for which, dst_d in ((0, qT_d), (1, kT_d)):
                        sb = stgB[which]
                        skey = ("stgB", which)

                        def epi_cp(oc, b, sb=sb, skey=skey):
                            P.op("act", lambda e, oc=oc, b=b: e.activation(out=sb[:, oc, :N], in_=banks[b][:, :N], func=AF.Copy),
                                 reads=[("ps", b)], writes=[skey])

                        for s in range(2):
                            lin_fm("in", [4 * s + q for q in range(4)], 8, 512, lambda kc: hnT[:, kc, :N], HN, N, epi_cp)
                        P.dma("sp", dst_d.ap()[:, c0:c0 + N].rearrange("(c p) n -> p c n", p=128), sb[:, :, :N], reads=[skey],
                              writes=[("dqk", which, bi)])
                else:
                    sq_ = stg[0]
                    sk_ = stg[1]

                    def epi_q(oc, b):
                        if oc < 4:
                            P.op("act", lambda e, oc=oc, b=b: e.activation(out=sq_[:, oc, :N], in_=banks[b][:, :N], func=AF.Copy, scale=128.0 ** -0.5),
                                 reads=[("ps", b)], writes=[("stg", 0)])
                        else:
                            P.op("dve", lambda e, oc=oc, b=b: e.tensor_copy(out=sk_[:, oc - 4, :N], in_=banks[b][:, :N]),
                                 reads=[("ps", b)], writes=[("stg", 1)])

                    for s in range(2):
                        lin_fm("in", [4 * s + q for q in range(4)], 8, 512, lambda kc: hnT[:, kc, :N], HN, N, epi_q)
                    P.dma("sp", gq_d.ap()[:, c0:c0 + N].rearrange("(c p) n -> p c n", p=128), sq_[:, :, :N], reads=[("stg", 0)],
                          writes=[("dgq", bi)])
                    P.dma("sp", gk_d.ap()[:, c0:c0 + N].rearrange("(c p) n -> p c n", p=128), sk_[:, :, :N], reads=[("stg", 1)],
                          writes=[("dgk", bi)])
                for half in range(2):
                    w, wkey = w_next("in")
                    for tt in range(nt):
                        b = next_acc()
                        for kc in range(8):
                            P.op("pe", lambda e, b=b, kc=kc, tt=tt, w=w: e.matmul(out=banks[b][:, :], lhsT=hnT[:, kc, tt * 128:(tt + 1) * 128],
                                                                                  rhs=w[:, kc * 512:(kc + 1) * 512], start=(kc == 0), stop=(kc == 7)),
                                 reads=[wkey] + HN, writes=[("ps", b)])
                        if tt % 2 == 0:
                            P.op("act", lambda e, b=b, tt=tt, half=half: e.activation(out=vst[:, tt, half * 512:(half + 1) * 512], in_=banks[b][:, :], func=AF.Copy),
                                 reads=[("ps", b)], writes=["vst"])
                        else:
                            P.op("dve", lambda e, b=b, tt=tt, half=half: e.tensor_copy(out=vst[:, tt, half * 512:(half + 1) * 512], in_=banks[b][:, :]),
                                 reads=[("ps", b)], writes=["vst"])
                P.dma("sp", v_d.ap()[c0:c0 + N, :].rearrange("(t p) n -> p t n", p=128), vst[:, :nt, :], reads=["vst"], writes=[("dv", bi)])
                if not is_attn_in:
                    sb = stgB[0]
                    skey = ("stgB", 0)

                    def epi_g(oc, b):
                        P.op("act", lambda e, oc=oc, b=b: e.activation(out=sb[:, oc, :N], in_=banks[b][:, :N], func=AF.Silu), reads=[("ps", b)],
                             writes=[skey])

                    for s in range(2):
                        lin_fm("in", [4 * s + q for q in range(4)], 8, 512, lambda kc: hnT[:, kc, :N], HN, N, epi_g)
                    P.dma("sp", sg_d.ap()[:, c0:c0 + N].rearrange("(c p) n -> p c n", p=128), sb[:, :, :N], reads=[skey], writes=[("dsg", bi)])
                    b = next_acc()
                    for kc in range(8):
                        P.op("pe", lambda e, b=b, kc=kc: e.matmul(out=banks[b][0:16, :N], lhsT=wgz[:, kc * 16:(kc + 1) * 16], rhs=hnT[:, kc, :N],
                                                                  start=(kc == 0), stop=(kc == 7)), reads=["wgz"] + HN, writes=[("ps", b)])
                    P.op("act", lambda e, b=b: e.activation(out=gzb[:, :N], in_=banks[b][0:16, :N], func=AF.Copy), reads=[("ps", b)], writes=["gzb"])
                    jg = li // 2
                    for oc in range(4):
                        b = next_acc()
                        P.op("pe", lambda e, b=b, oc=oc: e.matmul(out=banks[b][:, :N], lhsT=wgu[0:16, oc * 128:(oc + 1) * 128], rhs=gzb[0:16, :N],
                                                                  start=True, stop=True), reads=["wgu", "gzb"], writes=[("ps", b)])
                        tf = tmpf[oc % 2]
                        P.op("act", lambda e, b=b, oc=oc, tf=tf: e.activation(out=tf[:, :N], in_=banks[b][:, :N], func=AF.Exp, scale=-1.0,
                                                                              bias=negb[:, 4 * jg + oc:4 * jg + oc + 1]),
                             reads=[("ps", b), ("negb", jg)], writes=[("tmpf", oc % 2)])
                        P.op("act", lambda e, oc=oc, tf=tf: e.activation(out=gsps[:, oc, :N], in_=tf[:, :N], func=AF.Ln, bias=1.0),
                             reads=[("tmpf", oc % 2)], writes=["gsps"])
                    P.dma("sp", gsp_d.ap()[:, c0:c0 + N].rearrange("(c p) n -> p c n", p=128), gsps[:, :, :N], reads=["gsps"], writes=[("dgsp", bi)])
            else:
                norm(hb, hkey, N, want_hn=False)
                if bi + 1 < nb:
                    stageA_half(bi + 1, 1)
                yT = r1_o
                for c in range(8):
                    P.op("dve", lambda e, c=c: e.scalar_tensor_tensor(out=yT[:, c, :N], in0=hb[:, c, :N], scalar=fnw[:, c:c + 1], in1=rinv[:, :N],
                                                                      op0=ALU.mult, op1=ALU.mult),
                         reads=[hkey, "rinv", "fnw"], writes=[("sq8", 0), ("sq8", 1)] + HN)
                for tt in range(nt):
                    tile_i = t0 + tt
                    g0 = tile_i * 128
                    lo = max(g0, NMETA)
                    hi = min(g0 + 128, NREAL)
                    if hi <= lo:
                        continue
                    os_ = ost[tt % 2]
                    okey = ("stg", tt % 2)
                    for half in range(2):
                        for c4 in range(4):
                            c = half * 4 + c4
                            P.op("pe", lambda e, c=c, c4=c4, half=half, tt=tt: e.transpose(
                                out=banks[6 + half][:, c4 * 128:(c4 + 1) * 128], in_=yT[:, c, tt * 128:(tt + 1) * 128], identity=identf[:]),
                                reads=HN + [("sq8", 0), ("sq8", 1), "identf"], writes=[("ps", 6 + half)])
                        if half == 0:
                            P.op("act", lambda e, os_=os_, half=half: e.activation(out=os_[:, half * 512:(half + 1) * 512], in_=banks[6 + half][:, :], func=AF.Copy),
                                 reads=[("ps", 6 + half)], writes=[okey])
                        else:
                            P.op("dve", lambda e, os_=os_, half=half: e.tensor_copy(out=os_[:, half * 512:(half + 1) * 512], in_=banks[6 + half][:, :]),
                                 reads=[("ps", 6 + half)], writes=[okey])
                    P.dma("sp", out_d.ap()[lo - NMETA:hi - NMETA, :], os_[lo - g0:hi - g0, :], reads=[okey], writes=[("dout", tile_i)])

        for bi in range(nb):
            do_block(bi)
        assert gctr[0] == total, (gctr[0], total)
        P.barrier()

    def attn_core(li):
        ja = li // 2
        ts = []
        for k in (li + 1, li + 2):
            if k <= depth:
                ts += group_tasks(k)
        bg.gen = prep_gen(ts, ["pool"])
        bg.cnt = 0
        A = Arena(CONST_END)
        Vall = A([128, NT, 1024], BF16, "Vall")
        KT = [A([128, LP], BF16, "KT") for _ in range(2)]
        QT = [A([128, LP], BF16, "QT") for _ in range(2)]
        Et = [[A([128, 512], BF16, "Et") for _ in range(3)] for _ in range(2)]
        fbs = [[A([128, 512], F32, "fb") for _ in range(7)] for _ in range(3)]
        pend2 = []
        sqos = [A([128, 512], BF16, "sqo") for _ in range(3)]
        ons = [A([128, 512], BF16, "ons") for _ in range(3)]
        P.dma("sp", Vall[:, :, :], v_d.ap()[:, :].rearrange("(t p) n -> p t n", p=128), writes=["Vall"])
        sb_i = [0]
        e_i = [0]
        qb_i = [0]
        def do_head(h):
            kt = KT[h % 2]
            qt = QT[h % 2]
            kkey = ("KT", h % 2)
            qkey = ("QT", h % 2)
            P.dma("sp", kt[:, :], kT_d.ap()[h * 128:(h + 1) * 128, :], writes=[kkey])
            P.dma("sp", qt[:, :], qT_d.ap()[h * 128:(h + 1) * 128, :], writes=[qkey])
            nq = NQ[h]
            def do_qblock(q0):
                N = min(nq, LP - q0)
                kbs = []
                for kb in range((q0 + N) // 128):
                    k0 = kb * 128
                    gap = q0 - (k0 + 127)
                    if gap > 0 and slopes[h] * gap > ATT_SKIP:
                        continue
                    kbs.append(kb)
                nk = len(kbs)

                def issue_qk(idx):
                    kb = kbs[idx]
                    k0 = kb * 128
                    m = (q0 - k0) // 128
                    res = []
                    for mp in range(2):
                        b = (sb_i[0] % 2) * 2 + mp
                        lo, hi = mp * 64, (mp + 1) * 64
                        if k0 < q0:
                            cs = 0
                            P.op("pe", lambda e, b=b, lo=lo, hi=hi, k0=k0: e.matmul(out=banks[b][:, 0:N], lhsT=kt[lo:hi, k0:k0 + 128], rhs=qt[lo:hi, q0:q0 + N],
                                                                                   start=True, stop=True), reads=[kkey, qkey], writes=[("ps", b)])
                        else:
                            cs = k0 - q0
                            P.op("pe", lambda e, b=b, lo=lo, hi=hi, k0=k0, cs=cs: e.matmul(out=banks[b][:, cs:cs + 128], lhsT=kt[lo:hi, k0:k0 + 128],
                                                                                          rhs=qt[lo:hi, k0:k0 + 128], start=True, stop=False),
                                 reads=[kkey, qkey], writes=[("ps", b)])
                            P.op("pe", lambda e, b=b, cs=cs: e.matmul(out=banks[b][:, cs:cs + 128], lhsT=identb[:], rhs=negmask[:], start=False, stop=True),
                                 reads=["identb", "negmask"], writes=[("ps", b)])
                            if cs + 128 < N:
                                P.op("pe", lambda e, b=b, lo=lo, hi=hi, k0=k0, cs=cs: e.matmul(out=banks[b][:, cs + 128:N], lhsT=kt[lo:hi, k0:k0 + 128],
                                                                                              rhs=qt[lo:hi, q0 + cs + 128:q0 + N], start=True, stop=True),
                                     reads=[kkey, qkey], writes=[("ps", b)])
                        res.append((b, cs))
                    sb_i[0] += 1
                    return res, m

                def issue_exp_pv(idx, res, m):
                    kb = kbs[idx]
                    first = idx == 0
                    last = idx == nk - 1
                    ei = e_i[0] % 3
                    e_i[0] += 1
                    for mp in range(2):
                        b, cs = res[mp]
                        et = Et[mp][ei]
                        ekey = ("Et", mp, ei)
                        col = h * NMB + (m + 4)
                        P.op("act", lambda e, b=b, cs=cs, et=et, col=col: e.activation(out=et[:, cs:N], in_=banks[b][:, cs:N], func=AF.Exp,
                                                                                      bias=btab[:, col:col + 1], scale=0.125),
                             reads=[("ps", b), ("btab", h, m + 4)], writes=[ekey])
                    for mp in range(2):
                        b, cs = res[mp]
                        et = Et[mp][ei]
                        ekey = ("Et", mp, ei)
                        P.op("pe", lambda e, mp=mp, cs=cs, et=et, kb=kb: e.matmul(out=banks[4 + mp][:, cs:N], lhsT=Vall[:, kb, h * 128:(h + 1) * 128],
                                                                                 rhs=et[:, cs:N], start=first, stop=last),
                             reads=["Vall", ekey], writes=[("ps", 4 + mp)])
                        P.op("pe", lambda e, mp=mp, cs=cs, et=et: e.matmul(out=banks[6 + mp][:, cs:N], lhsT=onesb[:], rhs=et[:, cs:N], start=first, stop=last),
                             reads=["onesb", ekey], writes=[("ps", 6 + mp)])

                pend = issue_qk(0)
                for idx in range(nk):
                    nxt = issue_qk(idx + 1) if idx + 1 < nk else None
                    issue_exp_pv(idx, pend[0], pend[1])
                    pend = nxt
                    bg.step(every=3)
                fsel = qb_i[0] % 3
                qb_i[0] += 1
                fb = fbs[fsel]
                P.op("dve", lambda e: e.tensor_copy(out=fb[2][:, :N], in_=banks[4][:, :N]), reads=[("ps", 4)], writes=[("fb", fsel, 2)])
                P.op("dve", lambda e: e.tensor_copy(out=fb[0][:, :N], in_=banks[6][:, :N]), reads=[("ps", 6)], writes=[("fb", fsel, 0)])
                P.op("dve", lambda e: e.tensor_copy(out=fb[3][:, :N], in_=banks[5][:, :N]), reads=[("ps", 5)], writes=[("fb", fsel, 3)])
                P.op("dve", lambda e: e.tensor_copy(out=fb[1][:, :N], in_=banks[7][:, :N]), reads=[("ps", 7)], writes=[("fb", fsel, 1)])
                P.op("dve", lambda e: e.reciprocal(out=fb[0][:, :N], in_=fb[0][:, :N]), reads=[("fb", fsel, 0)], writes=[("fb", fsel, 0)])
                P.op("dve", lambda e: e.reciprocal(out=fb[1][:, :N], in_=fb[1][:, :N]), reads=[("fb", fsel, 1)], writes=[("fb", fsel, 1)])
                P.op("pool", lambda e: e.tensor_tensor(out=fb[2][:, :N], in0=fb[2][:, :N], in1=fb[0][:, :N], op=ALU.mult),
                     reads=[("fb", fsel, 2), ("fb", fsel, 0)], writes=[("fb", fsel, 2)])
                P.op("pool", lambda e: e.tensor_tensor(out=fb[3][:, :N], in0=fb[3][:, :N], in1=fb[1][:, :N], op=ALU.mult),
                     reads=[("fb", fsel, 3), ("fb", fsel, 1)], writes=[("fb", fsel, 3)])
                P.op("dve", lambda e: e.scalar_tensor_tensor(out=fb[4][:, :N], in0=fb[3][:, :N], scalar=neglam[:, ja:ja + 1], in1=fb[2][:, :N],
                                                             op0=ALU.mult, op1=ALU.add), reads=[("fb", fsel, 2), ("fb", fsel, 3), ("neglam", ja)], writes=[("fb", fsel, 4)])

                def part2():
                    P.op("act", lambda e: e.activation(out=sqos[fsel][:, :N], in_=fb[4][:, :N], func=AF.Square), reads=[("fb", fsel, 4)], writes=[("sqo", fsel)])
                    bss = (sb_i[0] % 2) * 2
                    sb_i[0] += 1
                    P.op("pe", lambda e: e.matmul(out=banks[bss][:, :N], lhsT=onesb[:], rhs=sqos[fsel][:, :N], start=True, stop=True),
                         reads=["onesb", ("sqo", fsel)], writes=[("ps", bss)])
                    P.op("act", lambda e: e.activation(out=fb[5][:, :N], in_=banks[bss][:, :N], func=AF.Ln, bias=epsc[:], scale=1.0 / 128),
                         reads=[("ps", bss), "epsc"], writes=[("fb", fsel, 5)])
                    P.op("act", lambda e: e.activation(out=fb[6][:, :N], in_=fb[5][:, :N], func=AF.Exp, scale=-0.5), reads=[("fb", fsel, 5)], writes=[("fb", fsel, 6)])
                    on = ons[fsel]
                    P.op("pool", lambda e: e.tensor_tensor(out=on[:, :N], in0=fb[4][:, :N], in1=fb[6][:, :N], op=ALU.mult),
                         reads=[("fb", fsel, 4), ("fb", fsel, 6)], writes=[("ons", fsel)])
                    P.dma("sp", onT_d.ap()[h * 128:(h + 1) * 128, q0:q0 + N], on[:, :N], reads=[("ons", fsel)], writes=[("donT", h, q0)])

                pend2.append(part2)
                if len(pend2) > 2:
                    pend2.pop(0)()

            for q0 in range(0, LP, nq):
                do_qblock(q0)

        for h in range(8):
            do_head(h)
        while pend2:
            pend2.pop(0)()
        bg.drain()
        P.barrier()

    def gla_core(li):
        A = Arena(CONST_END)
        NB = 512
        gq = [A([128, 4, NB], F32, "gq") for _ in range(2)]
        gk = [A([128, 4, NB], F32, "gk") for _ in range(2)]
        gs = [A([128, 4, NB], F32, "gs") for _ in range(2)]
        vv = [A([128, 4, 1024], BF16, "vv") for _ in range(2)]
        ost = [A([128, 8, NB], F32, "ost") for _ in range(2)]
        S = A([128, 4, 256], F32, "S")
        Sb = A([128, 4, 256], BF16, "Sb")
        NS = 8
        bpos = [A([128, 128], F32, "bpos") for _ in range(NS)]
        Ep = [A([128, 128], F32, "Ep") for _ in range(NS)]
        Em = [A([128, 128], F32, "Em") for _ in range(NS)]
        qs = [A([128, 128], BF16, "qs") for _ in range(NS)]
        ks = [A([128, 128], BF16, "ks") for _ in range(NS)]
        ktl = [A([128, 128], BF16, "ktl") for _ in range(NS)]
        ktk = [A([128, 128], BF16, "ktk") for _ in range(NS)]
        Am = [A([128, 128], BF16, "Am") for _ in range(NS)]
        sgs = [A([128, 8, NB], BF16, "sgs") for _ in range(2)]
        onst = [A([128, 8, NB], BF16, "onst") for _ in range(2)]
        sqg = [A([128, 2, NB], BF16, "sqg") for _ in range(2)]
        grt = [A([128, NB], F32, "grt") for _ in range(2)]
        gri = [A([128, NB], F32, "gri") for _ in range(2)]
        gtf = [A([128, NB], F32, "gtf") for _ in range(2)]
        P.op("dve", lambda e: e.memset(S[:], 0.0), writes=[("S", 0), ("S", 1), ("S", 2), ("S", 3)])
        P.op("dve", lambda e: e.memset(Sb[:], 0.0), writes=[("Sb", 0), ("Sb", 1), ("Sb", 2), ("Sb", 3)])
        it = [0]

        def do_gblock(bi, t0, nt):
            N = nt * 128
            c0 = t0 * 128
            k2 = bi % 2
            P.dma("sp", gq[k2][:, :, :N], gq_d.ap()[:, c0:c0 + N].rearrange("(c p) n -> p c n", p=128), writes=[("gq", k2)])
            P.dma("sp", gk[k2][:, :, :N], gk_d.ap()[:, c0:c0 + N].rearrange("(c p) n -> p c n", p=128), writes=[("gk", k2)])
            P.dma("sp", gs[k2][:, :, :N], gsp_d.ap()[:, c0:c0 + N].rearrange("(c p) n -> p c n", p=128), writes=[("gs", k2)])
            P.dma("sp", vv[k2][:, :nt, :], v_d.ap()[c0:c0 + N, :].rearrange("(t p) n -> p t n", p=128), writes=[("vv", k2)])
            P.dma("sp", sgs[k2][:, :, :N], sg_d.ap()[:, c0:c0 + N].rearrange("(c p) n -> p c n", p=128), writes=[("sgs", k2)])

            def do_chunk(tt):
                cs = tt * 128
                par = it[0] % 2
                it[0] += 1
                ix = [par * 4 + h for h in range(4)]
                pA = [banks[h][:, 0:128] for h in range(4)]
                pO = [banks[h][:, 128:384] for h in range(4)]
                pD = [banks[4 + h][:, 0:256] for h in range(4)]
                pT = [banks[4 + h][:, 256:320].bitcast(BF16) for h in range(4)]
                for h in range(4):
                    i3 = ix[h]
                    P.op("dve", lambda e, i3=i3, h=h: e.tensor_tensor_scan(out=bpos[i3][:], data0=onesf[:], data1=gs[k2][:, h, cs:cs + 128], initial=0.0,
                                                                          op0=ALU.mult, op1=ALU.add), reads=[("gs", k2), "onesf"], writes=[("bpos", i3)])
                for h in range(4):
                    i3 = ix[h]
                    P.op("act", lambda e, i3=i3: e.activation(out=Ep[i3][:], in_=bpos[i3][:], func=AF.Exp, scale=-1.0 / 16), reads=[("bpos", i3)],
                         writes=[("Ep", i3)])
                    P.op("act", lambda e, i3=i3: e.activation(out=Em[i3][:], in_=bpos[i3][:], func=AF.Exp, scale=1.0 / 16), reads=[("bpos", i3)],
                         writes=[("Em", i3)])
                for h in range(4):
                    i3 = ix[h]
                    P.op("dve", lambda e, i3=i3, h=h: e.tensor_tensor(out=qs[i3][:], in0=gq[k2][:, h, cs:cs + 128], in1=Ep[i3][:], op=ALU.mult),
                         reads=[("gq", k2), ("Ep", i3)], writes=[("qs", i3)])
                    P.op("pool", lambda e, i3=i3, h=h: e.tensor_tensor(out=ks[i3][:], in0=gk[k2][:, h, cs:cs + 128], in1=Em[i3][:], op=ALU.mult),
                         reads=[("gk", k2), ("Em", i3)], writes=[("ks", i3)])
                    P.op("dve", lambda e, i3=i3, h=h: e.scalar_tensor_tensor(out=ktl[i3][:], in0=gk[k2][:, h, cs:cs + 128], scalar=Ep[i3][:, 127:128],
                                                                            in1=Em[i3][:], op0=ALU.mult, op1=ALU.mult),
                         reads=[("gk", k2), ("Ep", i3), ("Em", i3)], writes=[("ktl", i3)])
                for h in range(4):
                    i3 = ix[h]
                    P.op("pe", lambda e, i3=i3, h=h: e.matmul(out=pA[h], lhsT=ks[i3][:], rhs=qs[i3][:], start=True, stop=True),
                         reads=[("ks", i3), ("qs", i3)], writes=[("ps", h)])
                    P.op("pe", lambda e, i3=i3, h=h: e.transpose(out=pT[h], in_=ktl[i3][:], identity=identb[:]), reads=[("ktl", i3), "identb"],
                         writes=[("ps", 4 + h)])
                for h in range(4):
                    i3 = ix[h]
                    P.op("act", lambda e, i3=i3, h=h: e.activation(out=ktk[i3][:], in_=pT[h], func=AF.Copy), reads=[("ps", 4 + h)], writes=[("ktk", i3)])
                    P.op("dve", lambda e, i3=i3, h=h: e.tensor_tensor(out=Am[i3][:], in0=pA[h], in1=trimask[:], op=ALU.mult),
                         reads=[("ps", h), "trimask"], writes=[("Am", i3)])
                for h in range(4):
                    i3 = ix[h]
                    for ec in range(2):
                        P.op("pe", lambda e, i3=i3, ec=ec, h=h: e.matmul(out=banks[h][:, 128 + ec * 128:256 + ec * 128],
                                                                        lhsT=vv[k2][:, tt, h * 256 + ec * 128:h * 256 + (ec + 1) * 128], rhs=Am[i3][:],
                                                                        start=True, stop=False), reads=[("vv", k2), ("Am", i3)], writes=[("ps", h)])
                        P.op("pe", lambda e, i3=i3, ec=ec, h=h: e.matmul(out=banks[h][:, 128 + ec * 128:256 + ec * 128],
                                                                        lhsT=Sb[:, h, ec * 128:(ec + 1) * 128], rhs=qs[i3][:], start=False, stop=True),
                             reads=[("Sb", h), ("qs", i3)], writes=[("ps", h)])
                    P.op("pe", lambda e, i3=i3, h=h: e.matmul(out=pD[h], lhsT=ktk[i3][:], rhs=vv[k2][:, tt, h * 256:(h + 1) * 256],
                                                              start=True, stop=True), reads=[("ktk", i3), ("vv", k2)], writes=[("ps", 4 + h)])
                for h in range(4):
                    i3 = ix[h]
                    P.op("dve", lambda e, i3=i3, h=h: e.scalar_tensor_tensor(out=S[:, h, :], in0=S[:, h, :], scalar=Ep[i3][:, 127:128],
                                                                            in1=pD[h], op0=ALU.mult, op1=ALU.add),
                         reads=[("S", h), ("Ep", i3), ("ps", 4 + h)], writes=[("S", h)])
                    P.op("pool", lambda e, h=h: e.tensor_copy(out=Sb[:, h, :], in_=S[:, h, :]), reads=[("S", h)], writes=[("Sb", h)])
                    src_ = pO[h].rearrange("p (c t) -> p c t", t=128)
                    P.op("act", lambda e, src_=src_, h=h: e.activation(out=ost[k2][:, 2 * h:2 * h + 2, cs:cs + 128], in_=src_, func=AF.Copy),
                         reads=[("ps", h)], writes=[("ost", k2, h)])

            for tt in range(nt):
                do_chunk(tt)
            for h in range(4):
                p2 = h % 2
                P.op("act", lambda e, h=h, p2=p2: e.activation(out=sqg[p2][:, :, :N], in_=ost[k2][:, 2 * h:2 * h + 2, :N], func=AF.Square),
                     reads=[("ost", k2, h)], writes=[("sqg", p2)])
                for jj in range(2):
                    P.op("pe", lambda e, h=h, jj=jj, p2=p2: e.matmul(out=banks[4 + h][:, :N], lhsT=onesb[:], rhs=sqg[p2][:, jj, :N],
                                                                    start=(jj == 0), stop=(jj == 1)),
                         reads=[("sqg", p2), "onesb"], writes=[("ps", 4 + h)])
                P.op("act", lambda e, h=h, p2=p2: e.activation(out=grt[p2][:, :N], in_=banks[4 + h][:, :N], func=AF.Ln, bias=epsc[:], scale=1.0 / 256),
                     reads=[("ps", 4 + h), "epsc"], writes=[("grt", p2)])
                P.op("act", lambda e, p2=p2: e.activation(out=gri[p2][:, :N], in_=grt[p2][:, :N], func=AF.Exp, scale=-0.5), reads=[("grt", p2)], writes=[("gri", p2)])
                for jj in range(2):
                    P.op("dve", lambda e, h=h, jj=jj, p2=p2: e.tensor_tensor(out=gtf[jj][:, :N], in0=ost[k2][:, 2 * h + jj, :N], in1=gri[p2][:, :N], op=ALU.mult),
                         reads=[("ost", k2, h), ("gri", p2)], writes=[("gtf", jj)])
                    P.op("pool", lambda e, h=h, jj=jj: e.tensor_tensor(out=onst[k2][:, 2 * h + jj, :N], in0=gtf[jj][:, :N], in1=sgs[k2][:, 2 * h + jj, :N], op=ALU.mult),
                         reads=[("gtf", jj), ("sgs", k2)], writes=[("onst", k2, h)])
            P.dma("sp", onT_d.ap()[:, c0:c0 + N].rearrange("(c p) n -> p c n", p=128), onst[k2][:, :, :N],
                  reads=[("onst", k2, h) for h in range(4)], writes=[("donT", bi)])

        for bi, (t0, nt) in enumerate(blocks):
            do_gblock(bi, t0, nt)
        P.barrier()

    for li in range(depth + 1):
        t_phase(li)
        if li < depth:
            if li % 2 == 0:
                attn_core(li)
            else:
                gla_core(li)
    P.emit()
    return nc


_CACHE = {}
_NAMES = ["meta_tokens", "mix_norm_w", "attn_w_in", "attn_lambda", "attn_subln_w", "attn_w_out", "gla_w_in", "gla_w_gate_up",
          "gla_gate_bias", "gla_norm_w", "gla_w_out", "mlp_norm_w", "mlp_w_up", "mlp_w_down", "final_norm_w"]


def run(inputs, depth=4, dbg=False):
    x = np.ascontiguousarray(np.asarray(inputs["x"], dtype=np.float32))
    B, SEQ, _ = x.shape
    key = (SEQ, depth, dbg)
    if key not in _CACHE:
        _CACHE[key] = build(SEQ, depth, dbg)
    nc = _CACHE[key]
    shared = {n: np.ascontiguousarray(np.asarray(inputs[n], dtype=np.float32)) for n in _NAMES}
    in_maps = []
    for b in range(B):
        m = dict(shared)
        m["x"] = x[b]
        in_maps.append(m)
    res = run_bass_kernel_spmd(nc, in_maps, core_ids=list(range(B)))
    return res


def kernel(**inputs):
    res = run(inputs, depth=4, dbg=False)
    B = np.asarray(inputs["x"]).shape[0]
    return np.stack([np.asarray(res.results[b]["out"], dtype=np.float32) for b in range(B)], axis=0)
```

```python
import math
import numpy as np
import concourse.bass as bass
import concourse.mybir as mybir
from concourse.bass_utils import run_bass_kernel_spmd

F32 = mybir.dt.float32
BF16 = mybir.dt.bfloat16
I32 = mybir.dt.int32
ALU = mybir.AluOpType
AF = mybir.ActivationFunctionType
AX = mybir.AxisListType

ENGS = ("pe", "act", "dve", "pool", "sp")
EPOCH = 16000
DMA_K = 8


class _Op:
    __slots__ = ("eng", "fn", "deps", "signal", "sig_seq", "dma", "dma_slot", "dma_val", "pre")

    def __init__(self, eng, fn):
        self.eng = eng
        self.fn = fn
        self.deps = []
        self.signal = False
        self.sig_seq = None
        self.dma = False
        self.dma_slot = None
        self.dma_val = None
        self.pre = None


class Prog:
    def __init__(self, nc):
        self.nc = nc
        self.ops = {e: [] for e in ENGS}
        self.last_w = {}
        self.readers = {}
        self.dma_hist = {e: [] for e in ENGS}
        self.bar_deps = []
        self.bar_pending = set()

    def _add(self, eng, fn, reads, writes, dma=False):
        op = _Op(eng, fn)
        op.dma = dma
        deps = []
        if eng in self.bar_pending:
            deps.extend(self.bar_deps)
            self.bar_pending.discard(eng)
        for r in reads:
            w = self.last_w.get(r)
            if w is not None:
                deps.append(w)
        for r in writes:
            w = self.last_w.get(r)
            if w is not None:
                deps.append(w)
            deps.extend(self.readers.get(r, ()))
        for r in reads:
            self.readers.setdefault(r, []).append(op)
        for r in writes:
            self.last_w[r] = op
            self.readers[r] = []
        op.deps = deps
        self.ops[eng].append(op)
        if dma:
            hist = self.dma_hist[eng]
            n = len(hist)
            op.dma_slot = n % DMA_K
            op.dma_val = 16 * (n // DMA_K + 1)
            if n >= DMA_K:
                op.pre = hist[n - DMA_K]
            hist.append(op)
        return op

    def op(self, eng, fn, reads=(), writes=()):
        return self._add(eng, fn, reads, writes)

    def dma(self, eng, out, in_, reads=(), writes=(), **kw):
        return self._add(eng, lambda e: e.dma_start(out=out, in_=in_, **kw), reads, writes, dma=True)

    def barrier(self):
        lasts = []
        for e in ENGS:
            for op in reversed(self.ops[e]):
                if not op.dma:
                    lasts.append(op)
                    break
            lasts.extend(self.dma_hist[e][-DMA_K:])
        self.bar_deps = lasts
        self.bar_pending = set(ENGS)
        self.last_w = {}
        self.readers = {}

    def emit(self):
        nc = self.nc
        for e in ENGS:
            for op in self.ops[e]:
                for d in op.deps:
                    if d.dma:
                        continue
                    if d.eng == "pe" and op.eng == "pe" and not op.dma:
                        continue
                    d.signal = True
        nsig = {}
        for e in ENGS:
            s = 0
            for op in self.ops[e]:
                if op.signal and not op.dma:
                    s += 1
                    op.sig_seq = s
            nsig[e] = s
        import contextlib
        stack = contextlib.ExitStack()
        sems = {}
        dsems = {}
        for e in ENGS:
            n_ep = max(1, (nsig[e] + EPOCH - 1) // EPOCH)
            sems[e] = [stack.enter_context(nc.semaphore(f"s_{e}_{k}")) for k in range(n_ep)]
            if self.dma_hist[e]:
                dsems[e] = [stack.enter_context(nc.semaphore(f"d_{e}_{k}")) for k in range(DMA_K)]

        def target(d):
            if d.dma:
                return ("d", d.eng, d.dma_slot), dsems[d.eng][d.dma_slot], d.dma_val
            ep = (d.sig_seq - 1) // EPOCH
            return ("s", d.eng, ep), sems[d.eng][ep], d.sig_seq - ep * EPOCH

        with stack:
            block = stack.enter_context(nc.Block())

            def run(e, h):
                seen = {}
                for op in self.ops[e]:
                    need = {}
                    dl = op.deps if op.pre is None else op.deps + [op.pre]
                    for d in dl:
                        if (not d.dma) and d.eng == "pe" and e == "pe" and not op.dma:
                            continue
                        key, sem, val = target(d)
                        if seen.get(key, 0) >= val:
                            continue
                        if key not in need or need[key][1] < val:
                            need[key] = (sem, val)
                    for key, (sem, val) in need.items():
                        h.wait_ge(sem, val)
                        seen[key] = val
                    ins = op.fn(h)
                    if op.dma:
                        ins.then_inc(dsems[e][op.dma_slot], 16)
                    elif op.signal:
                        ep = (op.sig_seq - 1) // EPOCH
                        ins.then_inc(sems[e][ep], 1)
                hist = self.dma_hist[e]
                for d in hist[-DMA_K:]:
                    h.wait_ge(dsems[e][d.dma_slot], d.dma_val)

            @block.tensor
            def _(h):
                run("pe", h)

            @block.scalar
            def _(h):
                run("act", h)

            @block.vector
            def _(h):
                run("dve", h)

            @block.gpsimd
            def _(h):
                run("pool", h)

            @block.sync
            def _(h):
                run("sp", h)


D = 1024
NMETA = 16
DFF = 4096
EPS = 1e-6
ARENA0 = 20480
ARENA_END = 229376 - 1024
MASKNEG = -240000.0
ATT_SKIP = 60.0


def lambda_init_for(i):
    return 0.8 - 0.6 * math.exp(-0.3 * i)


def build(SEQ, depth=4, dbg=False):
    nc = bass.Bass("TRN2", target_bir_lowering=False)
    NREAL = SEQ + NMETA
    NT = (NREAL + 127) // 128
    LP = NT * 128
    n_attn = (depth + 1) // 2
    n_gla = depth // 2
    blocks = [(t0, min(4, NT - t0)) for t0 in range(0, NT, 4)]
    _nb = (NREAL + 511) // 512
    _bs = ((NREAL + _nb - 1) // _nb + 1) // 2 * 2
    cblocks = []
    _c = 0
    while _c < NREAL:
        cblocks.append((_c, min(_bs, NREAL - _c)))
        _c += _bs
    skind = "ExternalOutput" if dbg else "Internal"

    def din(name, shape):
        return nc.dram_tensor(name, list(shape), F32, kind="ExternalInput")

    x_d = din("x", [SEQ, D])
    meta_d = din("meta_tokens", [NMETA, D])
    mixw_d = din("mix_norm_w", [depth, D])
    awin_d = din("attn_w_in", [n_attn, D, 3 * D])
    alam_d = din("attn_lambda", [n_attn, 4, 64])
    asub_d = din("attn_subln_w", [n_attn, 128])
    awout_d = din("attn_w_out", [n_attn, D, D])
    gwin_d = din("gla_w_in", [max(n_gla, 1), D, 3088])
    ggu_d = din("gla_w_gate_up", [max(n_gla, 1), 16, 512])
    ggb_d = din("gla_gate_bias", [max(n_gla, 1), 512])
    gnw_d = din("gla_norm_w", [max(n_gla, 1), 256])
    gwout_d = din("gla_w_out", [max(n_gla, 1), D, D])
    mlpw_d = din("mlp_norm_w", [depth, D])
    wup_d = din("mlp_w_up", [depth, D, DFF])
    wdn_d = din("mlp_w_down", [depth, DFF, D])
    fnw_d = din("final_norm_w", [D])
    out_d = nc.dram_tensor("out", [SEQ, D], F32, kind="ExternalOutput")

    def scr(name, shape, dt):
        return nc.dram_tensor(name, list(shape), dt, kind=skind)

    hT_d = scr("s_hT", [D, LP], F32)
    qT_d = scr("s_qT", [D, LP], BF16)
    kT_d = scr("s_kT", [D, LP], BF16)
    v_d = scr("s_v", [LP, D], BF16)
    gq_d = scr("s_gq", [512, LP], F32)
    gk_d = scr("s_gk", [512, LP], F32)
    gsp_d = scr("s_gsp", [512, LP], F32)
    sg_d = scr("s_sg", [D, LP], BF16)
    onT_d = scr("s_onT", [D, LP], BF16)
    oT_d = scr("s_oT", [D, LP], F32)
    Win_b = [nc.dram_tensor(f"w_in{i}", [6, 128, 4096], BF16) for i in range(depth)]
    Wout_b = [nc.dram_tensor(f"w_out{i}", [2, 128, 4096], BF16) for i in range(depth)]
    Wup_b = [nc.dram_tensor(f"w_up{i}", [8, 128, 4096], BF16) for i in range(depth)]
    Wdn_b = [nc.dram_tensor(f"w_dn{i}", [8, 128, 4096], BF16) for i in range(depth)]
    Wgz_b = [nc.dram_tensor(f"w_gz{i}", [128, 128], BF16) for i in range(depth)]
    Wgu_b = [nc.dram_tensor(f"w_gu{i}", [16, 512], BF16) for i in range(depth)]

    P = Prog(nc)
    banks = [nc.alloc_psum_tensor(f"bank{i}", [128, 512], F32) for i in range(8)]

    class Arena:
        def __init__(self, base):
            self.off = base
            self.n = 0

        def __call__(self, shape, dt, name=None):
            esz = 4 if dt in (F32, I32) else 2
            nbytes = int(np.prod(shape[1:])) * esz
            nbytes = (nbytes + 63) // 64 * 64
            uid[0] += 1
            t = nc.alloc_sbuf_tensor_at(f"{name or 't'}_{uid[0]}", list(shape), dt, offset=self.off)
            self.off += nbytes
            assert self.off <= ARENA_END, f"SBUF overflow {self.off}"
            return t

    uid = [0]
    CA = Arena(ARENA0)
    identf = CA([128, 128], F32, "identf")
    identb = CA([128, 128], BF16, "identb")
    onesb = CA([128, 128], BF16, "onesb")
    onesf = CA([128, 128], F32, "onesf")
    trimask = CA([128, 128], F32, "trimask")
    negmask = CA([128, 128], BF16, "negmask")
    epsc = CA([128, 1], F32, "epsc")
    kki = CA([128, 1], I32, "kki")
    kkf = CA([128, 1], F32, "kkf")
    NMB = 40
    btab = CA([128, 8 * NMB], F32, "btab")
    neglam = CA([128, max(n_attn, 1)], F32, "neglam")
    negb = CA([128, 4 * max(n_gla, 1)], F32, "negb")
    fnw = CA([128, 8], F32, "fnw")
    scl = CA([128, 40], F32, "scl")
    lamb = CA([128, 256], F32, "lamb")
    lamt = CA([128, 8], F32, "lamt")
    vstg = CA([8, 128], F32, "vstg")
    CONST_END = CA.off

    def load_vec_T(vec_d, off, nrows, dst_ap, dst_key):
        P.dma("sp", vstg[0:nrows, :], bass.AP(vec_d, off, [[128, nrows], [1, 128]]), writes=["vstg"])
        P.op("pe", lambda e: e.transpose(out=banks[7][:, 0:nrows], in_=vstg[0:nrows, :], identity=identf[0:nrows, 0:nrows]),
             reads=["vstg", "identf"], writes=[("ps", 7)])
        P.op("dve", lambda e: e.tensor_copy(out=dst_ap, in_=banks[7][:, 0:nrows]), reads=[("ps", 7)], writes=[dst_key])

    for idt, nm, val in ((identf, "identf", 1.0), (identb, "identb", 1.0)):
        P.op("pool", lambda e, idt=idt: e.memset(idt[:], 0.0), writes=[nm])
        P.op("pool", lambda e, idt=idt: e.affine_select(out=idt[:], in_=idt[:], compare_op=ALU.not_equal, fill=1.0,
                                                         base=0, pattern=[[-1, 128]], channel_multiplier=1),
             reads=[nm], writes=[nm])
    P.op("pool", lambda e: e.memset(onesb[:], 1.0), writes=["onesb"])
    P.op("pool", lambda e: e.memset(onesf[:], 1.0), writes=["onesf"])
    P.op("pool", lambda e: e.memset(epsc[:], EPS), writes=["epsc"])
    P.op("pool", lambda e: e.memset(trimask[:], 1.0), writes=["trimask"])
    P.op("pool", lambda e: e.affine_select(out=trimask[:], in_=trimask[:], compare_op=ALU.is_ge, fill=0.0, base=0,
                                           pattern=[[1, 128]], channel_multiplier=-1), reads=["trimask"], writes=["trimask"])
    P.op("pool", lambda e: e.memset(negmask[:], 0.0), writes=["negmask"])
    P.op("pool", lambda e: e.affine_select(out=negmask[:], in_=negmask[:], compare_op=ALU.is_ge, fill=MASKNEG, base=0,
                                           pattern=[[1, 128]], channel_multiplier=-1), reads=["negmask"], writes=["negmask"])
    P.op("pool", lambda e: e.iota(out=kki[:], pattern=[[0, 1]], base=0, channel_multiplier=1), writes=["kki"])
    P.op("pool", lambda e: e.tensor_copy(out=kkf[:], in_=kki[:]), reads=["kki"], writes=["kkf"])
    slopes = [2.0 ** (-(h + 1)) for h in range(8)]
    NQ = [128 if h == 0 else (256 if h == 1 else 512) for h in range(8)]
    for h in range(8):
        for mi in range(NMB):
            m = mi - 4
            P.op("dve", lambda e, h=h, mi=mi, m=m: e.tensor_scalar(
                out=btab[:, h * NMB + mi:h * NMB + mi + 1], in0=kkf[:], scalar1=slopes[h],
                scalar2=-slopes[h] * (128.0 * m + NQ[h] / 2.0), op0=ALU.mult, op1=ALU.add),
                reads=["kkf"], writes=[("btab", h, mi)])
    load_vec_T(fnw_d, 0, 8, fnw[:, 0:8], "fnw")
    for j in range(n_attn):
        li = lambda_init_for(2 * j)
        P.dma("sp", lamb[:], bass.AP(alam_d, j * 256, [[0, 128], [1, 256]]), writes=["lamb"])
        P.op("dve", lambda e: e.tensor_tensor(out=lamb[:, 0:64], in0=lamb[:, 0:64], in1=lamb[:, 64:128], op=ALU.mult),
             reads=["lamb"], writes=["lamb"])
        P.op("dve", lambda e: e.tensor_tensor(out=lamb[:, 128:192], in0=lamb[:, 128:192], in1=lamb[:, 192:256], op=ALU.mult),
             reads=["lamb"], writes=["lamb"])
        P.op("dve", lambda e: e.reduce_sum(out=lamt[:, 0:1], in_=lamb[:, 0:64], axis=AX.X), reads=["lamb"], writes=["lamt"])
        P.op("dve", lambda e: e.reduce_sum(out=lamt[:, 1:2], in_=lamb[:, 128:192], axis=AX.X), reads=["lamb"], writes=["lamt"])
        P.op("act", lambda e: e.activation(out=lamt[:, 2:4], in_=lamt[:, 0:2], func=AF.Exp), reads=["lamt"], writes=["lamt"])
        P.op("dve", lambda e: e.tensor_tensor(out=lamt[:, 4:5], in0=lamt[:, 3:4], in1=lamt[:, 2:3], op=ALU.subtract),
             reads=["lamt"], writes=["lamt"])
        P.op("dve", lambda e, j=j, li=li: e.tensor_scalar(out=neglam[:, j:j + 1], in0=lamt[:, 4:5], scalar1=-li, scalar2=None,
                                                          op0=ALU.add), reads=["lamt"], writes=[("neglam", j)])
    for j in range(n_gla):
        load_vec_T(ggb_d, j * 512, 4, negb[:, 4 * j:4 * j + 4], ("negb", j))
        P.op("dve", lambda e, j=j: e.tensor_scalar(out=negb[:, 4 * j:4 * j + 4], in0=negb[:, 4 * j:4 * j + 4], scalar1=-1.0,
                                                   scalar2=None, op0=ALU.mult), reads=[("negb", j)], writes=[("negb", j)])

    rsc = CA([128, 24 * depth], F32, "rsc")
    CONST_END = CA.off
    for i in range(depth):
        j = i // 2
        load_vec_T(mixw_d, i * D, 8, rsc[:, 24 * i:24 * i + 8], ("rsc", i, 0))
        load_vec_T(mlpw_d, i * D, 8, rsc[:, 24 * i + 16:24 * i + 24], ("rsc", i, 2))
        if i % 2 == 0:
            load_vec_T(asub_d, j * 128, 1, scl[:, 16:17], "scl")
            P.op("dve", lambda e, i=i: e.tensor_scalar(out=rsc[:, 24 * i + 8:24 * i + 16], in0=onesf[:, 0:8], scalar1=scl[:, 16:17],
                                                       scalar2=1.0 - lambda_init_for(i), op0=ALU.mult, op1=ALU.mult),
                 reads=["scl", "onesf"], writes=[("rsc", i, 1)])
        else:
            load_vec_T(gnw_d, j * 256, 2, scl[:, 16:18], "scl")
            for c in range(8):
                P.op("dve", lambda e, c=c, i=i: e.tensor_copy(out=rsc[:, 24 * i + 8 + c:24 * i + 9 + c], in_=scl[:, 16 + (c % 2):17 + (c % 2)]),
                     reads=["scl"], writes=[("rsc", i, 1)])

    PREP_BASE = ARENA_END - 32768 - 2048
    pst = [nc.alloc_sbuf_tensor_at(f"pst{k}", [128, 2048], F32, offset=PREP_BASE + k * 8192) for k in range(3)]
    pob = [nc.alloc_sbuf_tensor_at(f"pob{k}", [128, 2048], BF16, offset=PREP_BASE + 24576 + k * 4096) for k in range(2)]

    def matrix_tasks(W_ap, K, Nc, Wb, CW, sc0, tag):
        ts = []
        for kc in range(K // 128):
            for p0 in range(0, Nc, 2048):
                ts.append(("mat", W_ap, kc, p0, min(2048, Nc - p0), Wb, CW, sc0, tag))
        return ts

    def group_tasks(k):
        ts = []
        if k >= 1:
            i = k - 1
            j = i // 2
            wo = awout_d.ap()[j] if i % 2 == 0 else gwout_d.ap()[j]
            ts += matrix_tasks(wo, D, D, Wout_b[i], 512, 24 * i + 8, ("out", i))
            ts += matrix_tasks(wup_d.ap()[i], D, DFF, Wup_b[i], 512, 24 * i + 16, ("up", i))
            ts += matrix_tasks(wdn_d.ap()[i], DFF, D, Wdn_b[i], 128, None, ("dn", i))
        if k < depth:
            i = k
            j = i // 2
            if i % 2 == 0:
                ts += matrix_tasks(awin_d.ap()[j], D, 3 * D, Win_b[i], 512, 24 * i, ("in", i))
            else:
                ts += matrix_tasks(gwin_d.ap()[j][:, 0:3072], D, 3072, Win_b[i], 512, 24 * i, ("in", i))
                ts.append(("gz", i, j))
                ts.append(("gu", i, j))
        return ts

    def t_load(t, n):
        b = n % 3
        if t[0] == "mat":
            _, W_ap, kc, p0, pn, Wb, CW, sc0, tag = t
            P.dma("sp", pst[b][:, :pn], W_ap[kc * 128:(kc + 1) * 128, p0:p0 + pn], writes=[("pst", b)])
        elif t[0] == "gz":
            _, i, j = t
            for kc in range(8):
                P.dma("sp", pst[b][:, kc * 16:(kc + 1) * 16], gwin_d.ap()[j][kc * 128:(kc + 1) * 128, 3072:3088], writes=[("pst", b, kc), ("pst", b)])
        else:
            _, i, j = t
            P.dma("sp", pst[b][0:16, 0:512], ggu_d.ap()[j], writes=[("pst", b)])

    def t_conv(t, n, eng):
        b = n % 3
        o = n % 2
        if t[0] == "mat":
            _, W_ap, kc, p0, pn, Wb, CW, sc0, tag = t
            rs = [("pst", b)]
            if sc0 is None:
                if eng == "act":
                    P.op("act", lambda e: e.activation(out=pob[o][:, :pn], in_=pst[b][:, :pn], func=AF.Copy), reads=rs, writes=[("pob", o)])
                else:
                    P.op(eng, lambda e: e.tensor_copy(out=pob[o][:, :pn], in_=pst[b][:, :pn]), reads=rs, writes=[("pob", o)])
            else:
                sc = rsc[:, sc0 + kc:sc0 + kc + 1]
                if eng == "act":
                    P.op("act", lambda e: e.activation(out=pob[o][:, :pn], in_=pst[b][:, :pn], func=AF.Copy, scale=sc), reads=rs, writes=[("pob", o)])
                else:
                    P.op(eng, lambda e: e.tensor_scalar(out=pob[o][:, :pn], in0=pst[b][:, :pn], scalar1=sc, scalar2=0.0, op0=ALU.mult, op1=ALU.add),
                         reads=rs, writes=[("pob", o)])
        elif t[0] == "gz":
            _, i, j = t
            for kc in range(8):
                P.op("dve", lambda e, kc=kc: e.tensor_scalar(out=pob[o][:, kc * 16:(kc + 1) * 16], in0=pst[b][:, kc * 16:(kc + 1) * 16],
                                                             scalar1=rsc[:, 24 * i + kc:24 * i + kc + 1], scalar2=None, op0=ALU.mult),
                     reads=[("pst", b, kc), ("pst", b)], writes=[("pob", o)])
        else:
            P.op("dve", lambda e: e.tensor_copy(out=pob[o][0:16, 0:512], in_=pst[b][0:16, 0:512]), reads=[("pst", b)], writes=[("pob", o)])

    def t_store(t, n):
        o = n % 2
        if t[0] == "mat":
            _, W_ap, kc, p0, pn, Wb, CW, sc0, tag = t
            for s in range(p0 // CW, (p0 + pn) // CW):
                P.dma("sp", Wb.ap()[s, :, kc * CW:(kc + 1) * CW], pob[o][:, s * CW - p0:(s + 1) * CW - p0], reads=[("pob", o)],
                      writes=[("wb", tag, s, kc)])
        elif t[0] == "gz":
            _, i, j = t
            P.dma("sp", Wgz_b[i].ap()[:, :], pob[o][:, 0:128], reads=[("pob", o)], writes=[("wgz", i)])
        else:
            _, i, j = t
            P.dma("sp", Wgu_b[i].ap()[:, :], pob[o][0:16, 0:512], reads=[("pob", o)], writes=[("wgu", i)])

    def prep_gen(tasks, engs):
        n = len(tasks)
        for s in range(n + 2):
            if s < n:
                t_load(tasks[s], s)
            if 0 <= s - 1 < n:
                t_conv(tasks[s - 1], s - 1, engs[(s - 1) % len(engs)])
            if 0 <= s - 2 < n:
                t_store(tasks[s - 2], s - 2)
            yield

    class BG:
        gen = None
        cnt = 0

        def step(self, every=1):
            if self.gen is None:
                return
            self.cnt += 1
            if self.cnt % every:
                return
            try:
                next(self.gen)
            except StopIteration:
                self.gen = None

        def drain(self):
            while self.gen is not None:
                self.step()

    if LP > NREAL:
        npad = LP - NREAL
        P.op("dve", lambda e: e.memset(pst[0][:, :], 0.0), writes=[("pst", 0)])
        P.op("dve", lambda e: e.memset(pob[0][:, :], 0.0), writes=[("pob", 0)])
        for nm, dd in (("qT", qT_d), ("kT", kT_d), ("sg", sg_d)):
            P.dma("sp", dd.ap()[:, NREAL:LP].rearrange("(c p) n -> p c n", p=128), pob[0][:, 0:8 * npad].rearrange("p (c n) -> p c n", n=npad),
                  reads=[("pob", 0)], writes=[("zpad", nm)])
        for nm, dd in (("gq", gq_d), ("gk", gk_d), ("gsp", gsp_d)):
            P.dma("sp", dd.ap()[:, NREAL:LP].rearrange("(c p) n -> p c n", p=128), pst[0][:, 0:4 * npad].rearrange("p (c n) -> p c n", n=npad),
                  reads=[("pst", 0)], writes=[("zpad", nm)])
        P.dma("sp", v_d.ap()[NREAL:LP, :], pob[0][0:npad, 0:1024], reads=[("pob", 0)], writes=[("zpad", "v")])
    bg = BG()
    bg.gen = prep_gen(group_tasks(0), ["dve", "act"])
    bg.drain()
    P.barrier()

    def t_phase(li):
        A = Arena(CONST_END)
        NB = 512
        hT = [A([128, 8, NB], F32, "hT") for _ in range(2)]
        xreg_off = A.off
        A([128, 4096], F32, "xreg")
        xst = [nc.alloc_sbuf_tensor_at(f"xst{li}_{k}", [128, 1024], F32, offset=xreg_off + k * 4096) for k in range(4)]
        onTs = [nc.alloc_sbuf_tensor_at(f"onTs{li}_{k}", [128, 8, NB], BF16, offset=xreg_off + k * 8192) for k in range(2)]
        r1_o = A([128, 8, NB], F32, "r1")
        off_r1 = A.off - 8 * NB * 4
        sqb8 = nc.alloc_sbuf_tensor_at(f"sqb8_{li}", [128, 8, NB], BF16, offset=off_r1)
        hnT = nc.alloc_sbuf_tensor_at(f"hnT_{li}", [128, 8, NB], BF16, offset=off_r1 + 8 * NB * 2)
        rt = A([128, NB], F32, "rt")
        rinv = A([128, NB], F32, "rinv")
        tmpf = [A([128, NB], F32, "tmpf") for _ in range(2)]
        hid = A([128, 32, NB], BF16, "hid")
        stg_off = A.off
        stg = [A([128, 4, NB], F32, "stg") for _ in range(2)]
        ost = [nc.alloc_sbuf_tensor_at(f"ost{li}_{k}", [128, 1024], F32, offset=stg_off + k * 8192) for k in range(2)]
        vst = A([128, 4, 1024], BF16, "vst")
        gsps = A([128, 4, NB], F32, "gsps")
        gzb = A([16, NB], BF16, "gzb")
        wgz = A([128, 128], BF16, "wgz")
        wgu = A([16, 512], BF16, "wgu")
        wsl = [A([128, 4096], BF16, "wsl") for _ in range(4)]
        stgB = [A([128, 8, NB], BF16, "stgB") for _ in range(2)]

        fin_layer = li - 1
        do_fin = li > 0
        do_in = li < depth
        is_attn_in = do_in and (li % 2 == 0)
        nb = len(cblocks)

        seq_all = []
        if do_fin:
            seq_all += [("out", 0), ("out", 1)]
        for b_ in range(nb):
            if do_fin:
                if b_ + 1 < nb:
                    seq_all.append(("out", 0))
                seq_all += [("up", s) for s in range(8)] + [("dn", s) for s in range(8)]
                if b_ + 1 < nb:
                    seq_all.append(("out", 1))
            if do_in:
                seq_all += [("in", s) for s in range(6)]
        total = len(seq_all)
        issued = [0]
        gctr = [0]

        def w_issue(g):
            kind, s = seq_all[g]
            if kind == "out":
                src_ = Wout_b[fin_layer].ap()[s]
            elif kind == "up":
                src_ = Wup_b[fin_layer].ap()[s]
            elif kind == "dn":
                src_ = Wdn_b[fin_layer].ap()[s]
            else:
                src_ = Win_b[li].ap()[s]
            P.dma("sp", wsl[g % 4][:, :], src_, writes=[("wsl", g % 4)])

        def w_next(kind):
            g = gctr[0]
            gctr[0] += 1
            assert seq_all[g][0] == kind, (g, seq_all[g], kind)
            while issued[0] < min(total, g + 3):
                w_issue(issued[0])
                issued[0] += 1
            return wsl[g % 4], ("wsl", g % 4)

        acc = [0]

        def next_acc():
            b = acc[0] % 4
            acc[0] += 1
            return b

        if do_in and not is_attn_in:
            P.dma("sp", wgz[:], Wgz_b[li].ap()[:, :], writes=["wgz"])
            P.dma("sp", wgu[:], Wgu_b[li].ap()[:, :], writes=["wgu"])

        def norm(hb, hkey, N, want_hn=True):
            for hh in range(2):
                P.op("act", lambda e, hh=hh: e.activation(out=sqb8[:, 4 * hh:4 * hh + 4, :N], in_=hb[:, 4 * hh:4 * hh + 4, :N], func=AF.Square),
                     reads=[hkey], writes=[("sq8", hh)])
            for c in range(8):
                P.op("pe", lambda e, c=c: e.matmul(out=banks[4][:, :N], lhsT=onesb[:], rhs=sqb8[:, c, :N], start=(c == 0), stop=(c == 7)),
                     reads=[("sq8", c // 4), "onesb"], writes=[("ps", 4)])
            P.op("act", lambda e: e.activation(out=rt[:, :N], in_=banks[4][:, :N], func=AF.Ln, bias=epsc[:], scale=1.0 / D),
                 reads=[("ps", 4), "epsc"], writes=["rt"])
            P.op("act", lambda e: e.activation(out=rinv[:, :N], in_=rt[:, :N], func=AF.Exp, scale=-0.5), reads=["rt"], writes=["rinv"])
            if want_hn:
                for c in range(8):
                    eng = "dve"
                    P.op(eng, lambda e, c=c: e.tensor_tensor(out=hnT[:, c, :N], in0=hb[:, c, :N], in1=rinv[:, :N], op=ALU.mult),
                         reads=[hkey, "rinv"], writes=[("hnT", c)])

        HN = [("hnT", c) for c in range(8)]

        def lin_fm(kind, oc_list, nkc, cw, rhs_of, rhs_keys, N, epi):
            w, wkey = w_next(kind)
            for j, oc in enumerate(oc_list):
                b = next_acc()
                for kc in range(nkc):
                    P.op("pe", lambda e, b=b, kc=kc, j=j: e.matmul(out=banks[b][:, :N], lhsT=w[:, kc * cw + j * 128:kc * cw + (j + 1) * 128],
                                                                   rhs=rhs_of(kc), start=(kc == 0), stop=(kc == nkc - 1)),
                         reads=[wkey] + rhs_keys, writes=[("ps", b)])
                epi(oc, b)

        def blk(bi):
            c0, N = cblocks[bi]
            tiles = [(off, min(128, N - off)) for off in range(0, N, 128)]
            return tiles, len(tiles), N, c0, hT[bi % 2], ("hT", bi % 2)

        def load_block(bi):
            tiles, nt, N, c0, hb, hkey = blk(bi)
            if li == 0:
                for tt, (off, m) in enumerate(tiles):
                    xs = xst[tt % 4]
                    xkey = ("R2", tt % 4)
                    g0 = c0 + off
                    hm = min(g0 + m, NMETA)
                    if hm > g0:
                        P.dma("sp", xs[0:hm - g0, :], meta_d.ap()[g0:hm, :], writes=[xkey])
                    lx = max(g0, NMETA)
                    hx = min(g0 + m, NREAL)
                    P.dma("sp", xs[lx - g0:hx - g0, :], x_d.ap()[lx - NMETA:hx - NMETA, :], writes=[xkey])
                for tt, (off, m) in enumerate(tiles):
                    xs = xst[tt % 4]
                    xkey = ("R2", tt % 4)
                    for half in range(2):
                        for c4 in range(4):
                            c = half * 4 + c4
                            P.op("pe", lambda e, xs=xs, c=c, c4=c4, half=half, m=m: e.transpose(
                                out=banks[6 + half][:, c4 * 128:c4 * 128 + m], in_=xs[0:m, c * 128:(c + 1) * 128], identity=identf[0:m, 0:m]),
                                reads=[xkey, "identf"], writes=[("ps", 6 + half)])
                        src_ = banks[6 + half][:, :].rearrange("p (c t) -> p c t", t=128)[:, :, 0:m]
                        dst = hb[:, half * 4:half * 4 + 4, off:off + m]
                        if half == 0:
                            P.op("act", lambda e, src_=src_, dst=dst: e.activation(out=dst, in_=src_, func=AF.Copy),
                                 reads=[("ps", 6 + half)], writes=[hkey])
                        else:
                            P.op("dve", lambda e, src_=src_, dst=dst: e.tensor_copy(out=dst, in_=src_),
                                 reads=[("ps", 6 + half)], writes=[hkey])
            else:
                P.dma("sp", hb[:, :, :N], hT_d.ap()[:, c0:c0 + N].rearrange("(c p) n -> p c n", p=128), writes=[hkey])

        def load_on(bi):
            tiles, nt, N, c0, hb, hkey = blk(bi)
            P.dma("sp", onTs[bi % 2][:, :, :N], onT_d.ap()[:, c0:c0 + N].rearrange("(c p) n -> p c n", p=128), writes=[("onTs", bi % 2)])

        def make_epi_res(hb, hkey, N):
            def epi_res(oc, b):
                P.op("dve", lambda e, oc=oc, b=b: e.tensor_tensor(out=hb[:, oc, :N], in0=hb[:, oc, :N], in1=banks[b][:, :N], op=ALU.add),
                     reads=[hkey, ("ps", b)], writes=[hkey])
            return epi_res

        def stageA_half(bi, s):
            tiles, nt, N, c0, hb, hkey = blk(bi)
            on = onTs[bi % 2]
            lin_fm("out", [4 * s + q for q in range(4)], 8, 512, lambda kc: on[:, kc, :N], [("onTs", bi % 2)], N, make_epi_res(hb, hkey, N))

        def do_block(bi):
            tiles, nt, N, c0, hb, hkey = blk(bi)
            if bi == 0:
                load_block(0)
                if do_fin:
                    load_on(0)
                    if nb > 1:
                        load_on(1)
                    stageA_half(0, 0)
                    stageA_half(0, 1)
            if bi + 1 < nb:
                load_block(bi + 1)
            if do_fin and bi + 2 < nb:
                load_on(bi + 2)
            if do_in and not do_fin:
                P.dma("sp", hT_d.ap()[:, c0:c0 + N].rearrange("(c p) n -> p c n", p=128), hb[:, :, :N], reads=[hkey], writes=[("dhT", bi)])
            epi_res = make_epi_res(hb, hkey, N)
            if do_fin:
                norm(hb, hkey, N)
                if bi + 1 < nb:
                    stageA_half(bi + 1, 0)
                rl_i = [0]

                def epi_up(oc, b):
                    k = rl_i[0] % 2
                    rl_i[0] += 1
                    tf = tmpf[k]
                    P.op("act", lambda e, b=b, tf=tf: e.activation(out=tf[:, :N], in_=banks[b][:, :N], func=AF.Relu), reads=[("ps", b)],
                         writes=[("tmpf", k)])
                    P.op("dve", lambda e, oc=oc, tf=tf: e.tensor_tensor(out=hid[:, oc, :N], in0=tf[:, :N], in1=tf[:, :N], op=ALU.mult),
                         reads=[("tmpf", k)], writes=["hid"])

                for s in range(8):
                    lin_fm("up", [4 * s + q for q in range(4)], 8, 512, lambda kc: hnT[:, kc, :N], HN, N, epi_up)
                for s in range(8):
                    lin_fm("dn", [s], 32, 128, lambda kc: hid[:, kc, :N], ["hid"], N, epi_res)
                if do_in:
                    P.dma("sp", hT_d.ap()[:, c0:c0 + N].rearrange("(c p) n -> p c n", p=128), hb[:, :, :N], reads=[hkey], writes=[("dhT", bi)])

            if do_in:
                norm(hb, hkey, N)
                if do_fin and bi + 1 < nb:
                    stageA_half(bi + 1, 1)
                if is_attn_in:
                    for which, dst_d in ((0, qT_d), (1, kT_d)):
                        sb = stgB[which]
                        skey = ("stgB", which)

                        def epi_cp(oc, b, sb=sb, skey=skey):
                            P.op("act", lambda e, oc=oc, b=b: e.activation(out=sb[:, oc, :N], in_=banks[b][:, :N], func=AF.Copy),
                                 reads=[("ps", b)], writes=[skey])

                        for s in range(2):
                            lin_fm("in", [4 * s + q for q in range(4)], 8, 512, lambda kc: hnT[:, kc, :N], HN, N, epi_cp)
                        P.dma("sp", dst_d.ap()[:, c0:c0 + N].rearrange("(c p) n -> p c n", p=128), sb[:, :, :N], reads=[skey],
                              writes=[("dqk", which, bi)])
                else:
                    sq_ = stg[0]
                    sk_ = stg[1]

                    def epi_q(oc, b):
                        if oc < 4:
                            P.op("act", lambda e, oc=oc, b=b: e.activation(out=sq_[:, oc, :N], in_=banks[b][:, :N], func=AF.Copy, scale=128.0 ** -0.5),
                                 reads=[("ps", b)], writes=[("stg", 0)])
                        else:
                            P.op("dve", lambda e, oc=oc, b=b: e.tensor_copy(out=sk_[:, oc - 4, :N], in_=banks[b][:, :N]),
                                 reads=[("ps", b)], writes=[("stg", 1)])

                    for s in range(2):
                        lin_fm("in", [4 * s + q for q in range(4)], 8, 512, lambda kc: hnT[:, kc, :N], HN, N, epi_q)
                    P.dma("sp", gq_d.ap()[:, c0:c0 + N].rearrange("(c p) n -> p c n", p=128), sq_[:, :, :N], reads=[("stg", 0)],
                          writes=[("dgq", bi)])
                    P.dma("sp", gk_d.ap()[:, c0:c0 + N].rearrange("(c p) n -> p c n", p=128), sk_[:, :, :N], reads=[("stg", 1)],
                          writes=[("dgk", bi)])
                for half in range(2):
                    w, wkey = w_next("in")
                    for tt, (off, m) in enumerate(tiles):
                        b = next_acc()
                        for kc in range(8):
                            P.op("pe", lambda e, b=b, kc=kc, off=off, m=m, w=w: e.matmul(out=banks[b][0:m, :], lhsT=hnT[:, kc, off:off + m],
                                                                                         rhs=w[:, kc * 512:(kc + 1) * 512], start=(kc == 0), stop=(kc == 7)),
                                 reads=[wkey] + HN, writes=[("ps", b)])
                        if tt % 2 == 0:
                            P.op("act", lambda e, b=b, tt=tt, half=half, m=m: e.activation(out=vst[0:m, tt, half * 512:(half + 1) * 512], in_=banks[b][0:m, :], func=AF.Copy),
                                 reads=[("ps", b)], writes=[("vst", tt)])
                        else:
                            P.op("dve", lambda e, b=b, tt=tt, half=half, m=m: e.tensor_copy(out=vst[0:m, tt, half * 512:(half + 1) * 512], in_=banks[b][0:m, :]),
                                 reads=[("ps", b)], writes=[("vst", tt)])
                for tt, (off, m) in enumerate(tiles):
                    P.dma("sp", v_d.ap()[c0 + off:c0 + off + m, :], vst[0:m, tt, :], reads=[("vst", tt)], writes=[("dv", bi, tt)])
                if not is_attn_in:
                    sb = stgB[0]
                    skey = ("stgB", 0)

                    def epi_g(oc, b):
                        P.op("act", lambda e, oc=oc, b=b: e.activation(out=sb[:, oc, :N], in_=banks[b][:, :N], func=AF.Silu), reads=[("ps", b)],
                             writes=[skey])

                    for s in range(2):
                        lin_fm("in", [4 * s + q for q in range(4)], 8, 512, lambda kc: hnT[:, kc, :N], HN, N, epi_g)
                    P.dma("sp", sg_d.ap()[:, c0:c0 + N].rearrange("(c p) n -> p c n", p=128), sb[:, :, :N], reads=[skey], writes=[("dsg", bi)])
                    b = next_acc()
                    for kc in range(8):
                        P.op("pe", lambda e, b=b, kc=kc: e.matmul(out=banks[b][0:16, :N], lhsT=wgz[:, kc * 16:(kc + 1) * 16], rhs=hnT[:, kc, :N],
                                                                  start=(kc == 0), stop=(kc == 7)), reads=["wgz"] + HN, writes=[("ps", b)])
                    P.op("act", lambda e, b=b: e.activation(out=gzb[:, :N], in_=banks[b][0:16, :N], func=AF.Copy), reads=[("ps", b)], writes=["gzb"])
                    jg = li // 2
                    for oc in range(4):
                        b = next_acc()
                        P.op("pe", lambda e, b=b, oc=oc: e.matmul(out=banks[b][:, :N], lhsT=wgu[0:16, oc * 128:(oc + 1) * 128], rhs=gzb[0:16, :N],
                                                                  start=True, stop=True), reads=["wgu", "gzb"], writes=[("ps", b)])
                        tf = tmpf[oc % 2]
                        P.op("act", lambda e, b=b, oc=oc, tf=tf: e.activation(out=tf[:, :N], in_=banks[b][:, :N], func=AF.Exp, scale=-1.0,
                                                                              bias=negb[:, 4 * jg + oc:4 * jg + oc + 1]),
                             reads=[("ps", b), ("negb", jg)], writes=[("tmpf", oc % 2)])
                        P.op("act", lambda e, oc=oc, tf=tf: e.activation(out=gsps[:, oc, :N], in_=tf[:, :N], func=AF.Ln, bias=1.0),
                             reads=[("tmpf", oc % 2)], writes=["gsps"])
                    P.dma("sp", gsp_d.ap()[:, c0:c0 + N].rearrange("(c p) n -> p c n", p=128), gsps[:, :, :N], reads=["gsps"], writes=[("dgsp", bi)])
            else:
                norm(hb, hkey, N, want_hn=False)
                if bi + 1 < nb:
                    stageA_half(bi + 1, 1)
                yT = r1_o
                for c in range(8):
                    P.op("dve", lambda e, c=c: e.scalar_tensor_tensor(out=yT[:, c, :N], in0=hb[:, c, :N], scalar=fnw[:, c:c + 1], in1=rinv[:, :N],
                                                                      op0=ALU.mult, op1=ALU.mult),
                         reads=[hkey, "rinv", "fnw"], writes=[("sq8", 0), ("sq8", 1)] + HN)
                for tt, (off, m) in enumerate(tiles):
                    g0 = c0 + off
                    lo = max(g0, NMETA)
                    hi = min(g0 + m, NREAL)
                    if hi <= lo:
                        continue
                    os_ = ost[tt % 2]
                    okey = ("stg", tt % 2)
                    for half in range(2):
                        for c4 in range(4):
                            c = half * 4 + c4
                            P.op("pe", lambda e, c=c, c4=c4, half=half, off=off, m=m: e.transpose(
                                out=banks[6 + half][0:m, c4 * 128:(c4 + 1) * 128], in_=yT[:, c, off:off + m], identity=identf[:]),
                                reads=HN + [("sq8", 0), ("sq8", 1), "identf"], writes=[("ps", 6 + half)])
                        if half == 0:
                            P.op("act", lambda e, os_=os_, half=half, m=m: e.activation(out=os_[0:m, half * 512:(half + 1) * 512], in_=banks[6 + half][0:m, :], func=AF.Copy),
                                 reads=[("ps", 6 + half)], writes=[okey])
                        else:
                            P.op("dve", lambda e, os_=os_, half=half, m=m: e.tensor_copy(out=os_[0:m, half * 512:(half + 1) * 512], in_=banks[6 + half][0:m, :]),
                                 reads=[("ps", 6 + half)], writes=[okey])
                    P.dma("sp", out_d.ap()[lo - NMETA:hi - NMETA, :], os_[lo - g0:hi - g0, :], reads=[okey], writes=[("dout", bi, tt)])

        for bi in range(nb):
            do_block(bi)
        assert gctr[0] == total, (gctr[0], total)
        P.barrier()

    def attn_core(li):
        ja = li // 2
        ts = []
        for k in (li + 1, li + 2):
            if k <= depth:
                ts += group_tasks(k)
        bg.gen = prep_gen(ts, ["pool"])
        bg.cnt = 0
        A = Arena(CONST_END)
        Vall = A([128, NT, 1024], BF16, "Vall")
        KT = [A([128, LP], BF16, "KT") for _ in range(2)]
        QT = [A([128, LP], BF16, "QT") for _ in range(2)]
        Et = [[A([128, 512], BF16, "Et") for _ in range(3)] for _ in range(2)]
        fbs = [[A([128, 512], F32, "fb") for _ in range(7)] for _ in range(3)]
        pend2 = []
        sqos = [A([128, 512], BF16, "sqo") for _ in range(3)]
        ons = [A([128, 512], BF16, "ons") for _ in range(3)]
        P.dma("sp", Vall[:, :, :], v_d.ap()[:, :].rearrange("(t p) n -> p t n", p=128), writes=["Vall"])
        sb_i = [0]
        e_i = [0]
        qb_i = [0]
        def do_head(h):
            kt = KT[h % 2]
            qt = QT[h % 2]
            kkey = ("KT", h % 2)
            qkey = ("QT", h % 2)
            P.dma("sp", kt[:, :], kT_d.ap()[h * 128:(h + 1) * 128, :], writes=[kkey])
            P.dma("sp", qt[:, :], qT_d.ap()[h * 128:(h + 1) * 128, :], writes=[qkey])
            nq = NQ[h]
            def do_qblock(q0):
                N = min(nq, LP - q0)
                kbs = []
                for kb in range((q0 + N) // 128):
                    k0 = kb * 128
                    gap = q0 - (k0 + 127)
                    if gap > 0 and slopes[h] * gap > ATT_SKIP:
                        continue
                    kbs.append(kb)
                nk = len(kbs)

                def issue_qk(idx):
                    kb = kbs[idx]
                    k0 = kb * 128
                    m = (q0 - k0) // 128
                    res = []
                    for mp in range(2):
                        b = (sb_i[0] % 2) * 2 + mp
                        lo, hi = mp * 64, (mp + 1) * 64
                        if k0 < q0:
                            cs = 0
                            P.op("pe", lambda e, b=b, lo=lo, hi=hi, k0=k0: e.matmul(out=banks[b][:, 0:N], lhsT=kt[lo:hi, k0:k0 + 128], rhs=qt[lo:hi, q0:q0 + N],
                                                                                   start=True, stop=True), reads=[kkey, qkey], writes=[("ps", b)])
                        else:
                            cs = k0 - q0
                            P.op("pe", lambda e, b=b, lo=lo, hi=hi, k0=k0, cs=cs: e.matmul(out=banks[b][:, cs:cs + 128], lhsT=kt[lo:hi, k0:k0 + 128],
                                                                                          rhs=qt[lo:hi, k0:k0 + 128], start=True, stop=False),
                                 reads=[kkey, qkey], writes=[("ps", b)])
                            P.op("pe", lambda e, b=b, cs=cs: e.matmul(out=banks[b][:, cs:cs + 128], lhsT=identb[:], rhs=negmask[:], start=False, stop=True),
                                 reads=["identb", "negmask"], writes=[("ps", b)])
                            if cs + 128 < N:
                                P.op("pe", lambda e, b=b, lo=lo, hi=hi, k0=k0, cs=cs: e.matmul(out=banks[b][:, cs + 128:N], lhsT=kt[lo:hi, k0:k0 + 128],
                                                                                              rhs=qt[lo:hi, q0 + cs + 128:q0 + N], start=True, stop=True),
                                     reads=[kkey, qkey], writes=[("ps", b)])
                        res.append((b, cs))
                    sb_i[0] += 1
                    return res, m

                def issue_exp_pv(idx, res, m):
                    kb = kbs[idx]
                    first = idx == 0
                    last = idx == nk - 1
                    ei = e_i[0] % 3
                    e_i[0] += 1
                    for mp in range(2):
                        b, cs = res[mp]
                        et = Et[mp][ei]
                        ekey = ("Et", mp, ei)
                        col = h * NMB + (m + 4)
                        P.op("act", lambda e, b=b, cs=cs, et=et, col=col: e.activation(out=et[:, cs:N], in_=banks[b][:, cs:N], func=AF.Exp,
                                                                                      bias=btab[:, col:col + 1], scale=0.125),
                             reads=[("ps", b), ("btab", h, m + 4)], writes=[ekey])
                    for mp in range(2):
                        b, cs = res[mp]
                        et = Et[mp][ei]
                        ekey = ("Et", mp, ei)
                        P.op("pe", lambda e, mp=mp, cs=cs, et=et, kb=kb: e.matmul(out=banks[4 + mp][:, cs:N], lhsT=Vall[:, kb, h * 128:(h + 1) * 128],
                                                                                 rhs=et[:, cs:N], start=first, stop=last),
                             reads=["Vall", ekey], writes=[("ps", 4 + mp)])
                        P.op("pe", lambda e, mp=mp, cs=cs, et=et: e.matmul(out=banks[6 + mp][:, cs:N], lhsT=onesb[:], rhs=et[:, cs:N], start=first, stop=last),
                             reads=["onesb", ekey], writes=[("ps", 6 + mp)])

                pend = issue_qk(0)
                for idx in range(nk):
                    nxt = issue_qk(idx + 1) if idx + 1 < nk else None
                    issue_exp_pv(idx, pend[0], pend[1])
                    pend = nxt
                    bg.step(every=3)
                fsel = qb_i[0] % 3
                qb_i[0] += 1
                fb = fbs[fsel]
                P.op("dve", lambda e: e.tensor_copy(out=fb[2][:, :N], in_=banks[4][:, :N]), reads=[("ps", 4)], writes=[("fb", fsel, 2)])
                P.op("dve", lambda e: e.tensor_copy(out=fb[0][:, :N], in_=banks[6][:, :N]), reads=[("ps", 6)], writes=[("fb", fsel, 0)])
                P.op("dve", lambda e: e.tensor_copy(out=fb[3][:, :N], in_=banks[5][:, :N]), reads=[("ps", 5)], writes=[("fb", fsel, 3)])
                P.op("dve", lambda e: e.tensor_copy(out=fb[1][:, :N], in_=banks[7][:, :N]), reads=[("ps", 7)], writes=[("fb", fsel, 1)])
                P.op("dve", lambda e: e.reciprocal(out=fb[0][:, :N], in_=fb[0][:, :N]), reads=[("fb", fsel, 0)], writes=[("fb", fsel, 0)])
                P.op("dve", lambda e: e.reciprocal(out=fb[1][:, :N], in_=fb[1][:, :N]), reads=[("fb", fsel, 1)], writes=[("fb", fsel, 1)])
                P.op("pool", lambda e: e.tensor_tensor(out=fb[2][:, :N], in0=fb[2][:, :N], in1=fb[0][:, :N], op=ALU.mult),
                     reads=[("fb", fsel, 2), ("fb", fsel, 0)], writes=[("fb", fsel, 2)])
                P.op("pool", lambda e: e.tensor_tensor(out=fb[3][:, :N], in0=fb[3][:, :N], in1=fb[1][:, :N], op=ALU.mult),
                     reads=[("fb", fsel, 3), ("fb", fsel, 1)], writes=[("fb", fsel, 3)])
                P.op("dve", lambda e: e.scalar_tensor_tensor(out=fb[4][:, :N], in0=fb[3][:, :N], scalar=neglam[:, ja:ja + 1], in1=fb[2][:, :N],
                                                             op0=ALU.mult, op1=ALU.add), reads=[("fb", fsel, 2), ("fb", fsel, 3), ("neglam", ja)], writes=[("fb", fsel, 4)])

                def part2():
                    P.op("act", lambda e: e.activation(out=sqos[fsel][:, :N], in_=fb[4][:, :N], func=AF.Square), reads=[("fb", fsel, 4)], writes=[("sqo", fsel)])
                    bss = (sb_i[0] % 2) * 2
                    sb_i[0] += 1
                    P.op("pe", lambda e: e.matmul(out=banks[bss][:, :N], lhsT=onesb[:], rhs=sqos[fsel][:, :N], start=True, stop=True),
                         reads=["onesb", ("sqo", fsel)], writes=[("ps", bss)])
                    P.op("act", lambda e: e.activation(out=fb[5][:, :N], in_=banks[bss][:, :N], func=AF.Ln, bias=epsc[:], scale=1.0 / 128),
                         reads=[("ps", bss), "epsc"], writes=[("fb", fsel, 5)])
                    P.op("act", lambda e: e.activation(out=fb[6][:, :N], in_=fb[5][:, :N], func=AF.Exp, scale=-0.5), reads=[("fb", fsel, 5)], writes=[("fb", fsel, 6)])
                    on = ons[fsel]
                    P.op("pool", lambda e: e.tensor_tensor(out=on[:, :N], in0=fb[4][:, :N], in1=fb[6][:, :N], op=ALU.mult),
                         reads=[("fb", fsel, 4), ("fb", fsel, 6)], writes=[("ons", fsel)])
                    P.dma("sp", onT_d.ap()[h * 128:(h + 1) * 128, q0:q0 + N], on[:, :N], reads=[("ons", fsel)], writes=[("donT", h, q0)])

                pend2.append(part2)
                if len(pend2) > 2:
                    pend2.pop(0)()

            for q0 in range(0, LP, nq):
                do_qblock(q0)

        for h in range(8):
            do_head(h)
        while pend2:
            pend2.pop(0)()
        bg.drain()
        P.barrier()

    def gla_core(li):
        A = Arena(CONST_END)
        NB = 512
        gq = [A([128, 4, NB], F32, "gq") for _ in range(2)]
        gk = [A([128, 4, NB], F32, "gk") for _ in range(2)]
        gs = [A([128, 4, NB], F32, "gs") for _ in range(2)]
        vv = [A([128, 4, 1024], BF16, "vv") for _ in range(2)]
        ost = [A([128, 8, NB], F32, "ost") for _ in range(2)]
        S = A([128, 4, 256], F32, "S")
        Sb = A([128, 4, 256], BF16, "Sb")
        NS = 8
        bpos = [A([128, 128], F32, "bpos") for _ in range(NS)]
        Ep = [A([128, 128], F32, "Ep") for _ in range(NS)]
        Em = [A([128, 128], F32, "Em") for _ in range(NS)]
        qs = [A([128, 128], BF16, "qs") for _ in range(NS)]
        ks = [A([128, 128], BF16, "ks") for _ in range(NS)]
        ktl = [A([128, 128], BF16, "ktl") for _ in range(NS)]
        ktk = [A([128, 128], BF16, "ktk") for _ in range(NS)]
        Am = [A([128, 128], BF16, "Am") for _ in range(NS)]
        sgs = [A([128, 8, NB], BF16, "sgs") for _ in range(2)]
        onst = [A([128, 8, NB], BF16, "onst") for _ in range(2)]
        sqg = [A([128, 2, NB], BF16, "sqg") for _ in range(2)]
        grt = [A([128, NB], F32, "grt") for _ in range(2)]
        gri = [A([128, NB], F32, "gri") for _ in range(2)]
        gtf = [A([128, NB], F32, "gtf") for _ in range(2)]
        P.op("dve", lambda e: e.memset(S[:], 0.0), writes=[("S", 0), ("S", 1), ("S", 2), ("S", 3)])
        P.op("dve", lambda e: e.memset(Sb[:], 0.0), writes=[("Sb", 0), ("Sb", 1), ("Sb", 2), ("Sb", 3)])
        it = [0]

        def do_gblock(bi, t0, nt):
            N = nt * 128
            c0 = t0 * 128
            k2 = bi % 2
            P.dma("sp", gq[k2][:, :, :N], gq_d.ap()[:, c0:c0 + N].rearrange("(c p) n -> p c n", p=128), writes=[("gq", k2)])
            P.dma("sp", gk[k2][:, :, :N], gk_d.ap()[:, c0:c0 + N].rearrange("(c p) n -> p c n", p=128), writes=[("gk", k2)])
            P.dma("sp", gs[k2][:, :, :N], gsp_d.ap()[:, c0:c0 + N].rearrange("(c p) n -> p c n", p=128), writes=[("gs", k2)])
            P.dma("sp", vv[k2][:, :nt, :], v_d.ap()[c0:c0 + N, :].rearrange("(t p) n -> p t n", p=128), writes=[("vv", k2)])
            P.dma("sp", sgs[k2][:, :, :N], sg_d.ap()[:, c0:c0 + N].rearrange("(c p) n -> p c n", p=128), writes=[("sgs", k2)])

            def do_chunk(tt):
                cs = tt * 128
                par = it[0] % 2
                it[0] += 1
                ix = [par * 4 + h for h in range(4)]
                pA = [banks[h][:, 0:128] for h in range(4)]
                pO = [banks[h][:, 128:384] for h in range(4)]
                pD = [banks[4 + h][:, 0:256] for h in range(4)]
                pT = [banks[4 + h][:, 256:320].bitcast(BF16) for h in range(4)]
                for h in range(4):
                    i3 = ix[h]
                    P.op("dve", lambda e, i3=i3, h=h: e.tensor_tensor_scan(out=bpos[i3][:], data0=onesf[:], data1=gs[k2][:, h, cs:cs + 128], initial=0.0,
                                                                          op0=ALU.mult, op1=ALU.add), reads=[("gs", k2), "onesf"], writes=[("bpos", i3)])
                for h in range(4):
                    i3 = ix[h]
                    P.op("act", lambda e, i3=i3: e.activation(out=Ep[i3][:], in_=bpos[i3][:], func=AF.Exp, scale=-1.0 / 16), reads=[("bpos", i3)],
                         writes=[("Ep", i3)])
                    P.op("act", lambda e, i3=i3: e.activation(out=Em[i3][:], in_=bpos[i3][:], func=AF.Exp, scale=1.0 / 16), reads=[("bpos", i3)],
                         writes=[("Em", i3)])
                for h in range(4):
                    i3 = ix[h]
                    P.op("dve", lambda e, i3=i3, h=h: e.tensor_tensor(out=qs[i3][:], in0=gq[k2][:, h, cs:cs + 128], in1=Ep[i3][:], op=ALU.mult),
                         reads=[("gq", k2), ("Ep", i3)], writes=[("qs", i3)])
                    P.op("pool", lambda e, i3=i3, h=h: e.tensor_tensor(out=ks[i3][:], in0=gk[k2][:, h, cs:cs + 128], in1=Em[i3][:], op=ALU.mult),
                         reads=[("gk", k2), ("Em", i3)], writes=[("ks", i3)])
                    P.op("dve", lambda e, i3=i3, h=h: e.scalar_tensor_tensor(out=ktl[i3][:], in0=gk[k2][:, h, cs:cs + 128], scalar=Ep[i3][:, 127:128],
                                                                            in1=Em[i3][:], op0=ALU.mult, op1=ALU.mult),
                         reads=[("gk", k2), ("Ep", i3), ("Em", i3)], writes=[("ktl", i3)])
                for h in range(4):
                    i3 = ix[h]
                    P.op("pe", lambda e, i3=i3, h=h: e.matmul(out=pA[h], lhsT=ks[i3][:], rhs=qs[i3][:], start=True, stop=True),
                         reads=[("ks", i3), ("qs", i3)], writes=[("ps", h)])
                    P.op("pe", lambda e, i3=i3, h=h: e.transpose(out=pT[h], in_=ktl[i3][:], identity=identb[:]), reads=[("ktl", i3), "identb"],
                         writes=[("ps", 4 + h)])
                for h in range(4):
                    i3 = ix[h]
                    P.op("act", lambda e, i3=i3, h=h: e.activation(out=ktk[i3][:], in_=pT[h], func=AF.Copy), reads=[("ps", 4 + h)], writes=[("ktk", i3)])
                    P.op("dve", lambda e, i3=i3, h=h: e.tensor_tensor(out=Am[i3][:], in0=pA[h], in1=trimask[:], op=ALU.mult),
                         reads=[("ps", h), "trimask"], writes=[("Am", i3)])
                for h in range(4):
                    i3 = ix[h]
                    for ec in range(2):
                        P.op("pe", lambda e, i3=i3, ec=ec, h=h: e.matmul(out=banks[h][:, 128 + ec * 128:256 + ec * 128],
                                                                        lhsT=vv[k2][:, tt, h * 256 + ec * 128:h * 256 + (ec + 1) * 128], rhs=Am[i3][:],
                                                                        start=True, stop=False), reads=[("vv", k2), ("Am", i3)], writes=[("ps", h)])
                        P.op("pe", lambda e, i3=i3, ec=ec, h=h: e.matmul(out=banks[h][:, 128 + ec * 128:256 + ec * 128],
                                                                        lhsT=Sb[:, h, ec * 128:(ec + 1) * 128], rhs=qs[i3][:], start=False, stop=True),
                             reads=[("Sb", h), ("qs", i3)], writes=[("ps", h)])
                    P.op("pe", lambda e, i3=i3, h=h: e.matmul(out=pD[h], lhsT=ktk[i3][:], rhs=vv[k2][:, tt, h * 256:(h + 1) * 256],
                                                              start=True, stop=True), reads=[("ktk", i3), ("vv", k2)], writes=[("ps", 4 + h)])
                for h in range(4):
                    i3 = ix[h]
                    P.op("dve", lambda e, i3=i3, h=h: e.scalar_tensor_tensor(out=S[:, h, :], in0=S[:, h, :], scalar=Ep[i3][:, 127:128],
                                                                            in1=pD[h], op0=ALU.mult, op1=ALU.add),
                         reads=[("S", h), ("Ep", i3), ("ps", 4 + h)], writes=[("S", h)])
                    P.op("pool", lambda e, h=h: e.tensor_copy(out=Sb[:, h, :], in_=S[:, h, :]), reads=[("S", h)], writes=[("Sb", h)])
                    src_ = pO[h].rearrange("p (c t) -> p c t", t=128)
                    P.op("act", lambda e, src_=src_, h=h: e.activation(out=ost[k2][:, 2 * h:2 * h + 2, cs:cs + 128], in_=src_, func=AF.Copy),
                         reads=[("ps", h)], writes=[("ost", k2, h)])

            for tt in range(nt):
                do_chunk(tt)
            for h in range(4):
                p2 = h % 2
                P.op("act", lambda e, h=h, p2=p2: e.activation(out=sqg[p2][:, :, :N], in_=ost[k2][:, 2 * h:2 * h + 2, :N], func=AF.Square),
                     reads=[("ost", k2, h)], writes=[("sqg", p2)])
                for jj in range(2):
                    P.op("pe", lambda e, h=h, jj=jj, p2=p2: e.matmul(out=banks[4 + h][:, :N], lhsT=onesb[:], rhs=sqg[p2][:, jj, :N],
                                                                    start=(jj == 0), stop=(jj == 1)),
                         reads=[("sqg", p2), "onesb"], writes=[("ps", 4 + h)])
                P.op("act", lambda e, h=h, p2=p2: e.activation(out=grt[p2][:, :N], in_=banks[4 + h][:, :N], func=AF.Ln, bias=epsc[:], scale=1.0 / 256),
                     reads=[("ps", 4 + h), "epsc"], writes=[("grt", p2)])
                P.op("act", lambda e, p2=p2: e.activation(out=gri[p2][:, :N], in_=grt[p2][:, :N], func=AF.Exp, scale=-0.5), reads=[("grt", p2)], writes=[("gri", p2)])
                for jj in range(2):
                    P.op("dve", lambda e, h=h, jj=jj, p2=p2: e.tensor_tensor(out=gtf[jj][:, :N], in0=ost[k2][:, 2 * h + jj, :N], in1=gri[p2][:, :N], op=ALU.mult),
                         reads=[("ost", k2, h), ("gri", p2)], writes=[("gtf", jj)])
                    P.op("pool", lambda e, h=h, jj=jj: e.tensor_tensor(out=onst[k2][:, 2 * h + jj, :N], in0=gtf[jj][:, :N], in1=sgs[k2][:, 2 * h + jj, :N], op=ALU.mult),
                         reads=[("gtf", jj), ("sgs", k2)], writes=[("onst", k2, h)])
            P.dma("sp", onT_d.ap()[:, c0:c0 + N].rearrange("(c p) n -> p c n", p=128), onst[k2][:, :, :N],
                  reads=[("onst", k2, h) for h in range(4)], writes=[("donT", bi)])

        for bi, (t0, nt) in enumerate(blocks):
            do_gblock(bi, t0, nt)
        P.barrier()

    for li in range(depth + 1):
        t_phase(li)
        if li < depth:
            if li % 2 == 0:
                attn_core(li)
            else:
                gla_core(li)
    P.emit()
    return nc


_CACHE = {}
_NAMES = ["meta_tokens", "mix_norm_w", "attn_w_in", "attn_lambda", "attn_subln_w", "attn_w_out", "gla_w_in", "gla_w_gate_up",
          "gla_gate_bias", "gla_norm_w", "gla_w_out", "mlp_norm_w", "mlp_w_up", "mlp_w_down", "final_norm_w"]


def run(inputs, depth=4, dbg=False):
    x = np.ascontiguousarray(np.asarray(inputs["x"], dtype=np.float32))
    B, SEQ, _ = x.shape
    key = (SEQ, depth, dbg)
    if key not in _CACHE:
        _CACHE[key] = build(SEQ, depth, dbg)
    nc = _CACHE[key]
    shared = {n: np.ascontiguousarray(np.asarray(inputs[n], dtype=np.float32)) for n in _NAMES}
    in_maps = []
    for b in range(B):
        m = dict(shared)
        m["x"] = x[b]
        in_maps.append(m)
    res = run_bass_kernel_spmd(nc, in_maps, core_ids=list(range(B)))
    return res


def kernel(**inputs):
    res = run(inputs, depth=4, dbg=False)
    B = np.asarray(inputs["x"]).shape[0]
    return np.stack([np.asarray(res.results[b]["out"], dtype=np.float32) for b in range(B)], axis=0)
```
